# Optimizing a Trainium2 kernel written in Bass

```python
import jax, jax.numpy as jnp
from jax import lax
import numpy as np

D_MODEL = 4096
BATCH = 2
SEQ = 4096
DEPTH = 2
DEC_BATCH = 8
DEC_SEQ = 2048
PAST_LEN = 128

MIX_WIDTH = D_MODEL
RET_WIDTH = MIX_WIDTH // 2
CONV_WIDTH = MIX_WIDTH - RET_WIDTH
RET_HEAD_DIM = 256
N_RET_HEADS = RET_WIDTH // RET_HEAD_DIM
CONV_K = 3
CHUNK = 128
D_FF = 4 * D_MODEL
PLE_DIM = 256
ROPE_BASE = 10000.0
NORM_EPS = 1e-6
IN_PROJ_WIDTH = 4 * RET_WIDTH + 3 * CONV_WIDTH

kernel_name = "hybrid_retention_shortconv_encoder"


def rmsnorm(x, gain):
    xf = x.astype(jnp.float32)
    y = xf * lax.rsqrt(jnp.mean(xf * xf, axis=-1, keepdims=True) + NORM_EPS)
    return (y * gain.astype(jnp.float32)).astype(x.dtype)


def rotary(t, pos):
    half = t.shape[-1] // 2
    inv_freq = ROPE_BASE ** (-jnp.arange(half, dtype=jnp.float32) / half)
    ang = pos[:, None] * inv_freq[None, :]
    cos = jnp.cos(ang)[None, :, None, :]
    sin = jnp.sin(ang)[None, :, None, :]
    t1, t2 = t[..., :half], t[..., half:]
    return jnp.concatenate([t1 * cos - t2 * sin, t2 * cos + t1 * sin], axis=-1)


def retention_one_direction(q, k, v, log_g, include_diag):
    b_, L, H, dk = q.shape
    dv = v.shape[-1]
    n = L // CHUNK

    def to_chunks(t):
        return t.reshape(b_, n, CHUNK, H, t.shape[-1]).transpose(1, 0, 3, 2, 4)

    qc, kc, vc = to_chunks(q), to_chunks(k), to_chunks(v)
    idx = jnp.arange(CHUNK, dtype=jnp.float32)
    diff = idx[:, None] - idx[None, :]
    mask = (diff >= 0) if include_diag else (diff > 0)
    intra = jnp.where(mask[None], jnp.exp(jnp.where(mask, diff, 0.0)[None] * log_g[:, None, None]), 0.0)
    q_decay = jnp.exp((idx + 1.0)[None, :] * log_g[:, None])
    k_decay = jnp.exp((CHUNK - 1.0 - idx)[None, :] * log_g[:, None])
    chunk_decay = jnp.exp(CHUNK * log_g)

    def step(state, xs):
        qi, ki, vi = xs
        scores = jnp.einsum('bhid,bhjd->bhij', qi, ki) * intra[None]
        o = (jnp.einsum('bhij,bhje->bhie', scores, vi)
             + jnp.einsum('bhid,bhde->bhie', qi, state) * q_decay[None, :, :, None])
        state = (state * chunk_decay[None, :, None, None]
                 + jnp.einsum('bhjd,bhje->bhde', ki * k_decay[None, :, :, None], vi))
        return state, o

    state0 = jnp.zeros((b_, H, dk, dv), jnp.float32)
    _, o = lax.scan(step, state0, (qc, kc, vc))
    return o.transpose(1, 0, 3, 2, 4).reshape(b_, L, H, dv)


def retention_branch(q, k, v, g, decay_fwd_raw, decay_bwd_raw):
    b_, L, _ = q.shape
    pos = jnp.arange(L, dtype=jnp.float32)

    def heads(t):
        return t.astype(jnp.float32).reshape(b_, L, N_RET_HEADS, RET_HEAD_DIM)

    qh = rotary(heads(q), pos)
    kh = rotary(heads(k), pos) * (RET_HEAD_DIM ** -0.5)
    vh = heads(v)
    lg_f = -jnp.exp(decay_fwd_raw.astype(jnp.float32))
    lg_b = -jnp.exp(decay_bwd_raw.astype(jnp.float32))
    o_f = retention_one_direction(qh, kh, vh, lg_f, True)
    o_b = jnp.flip(retention_one_direction(jnp.flip(qh, 1), jnp.flip(kh, 1), jnp.flip(vh, 1), lg_b, False), 1)
    o = o_f + o_b
    mean = jnp.mean(o, axis=-1, keepdims=True)
    var = jnp.mean(jnp.square(o - mean), axis=-1, keepdims=True)
    o = ((o - mean) * lax.rsqrt(var + NORM_EPS)).reshape(b_, L, RET_WIDTH)
    return (jax.nn.silu(g.astype(jnp.float32)) * o).astype(q.dtype)


def short_conv_branch(b, c, u, w):
    z = c * u
    L = z.shape[1]
    pad = CONV_K // 2
    zp = jnp.pad(z, ((0, 0), (pad, pad), (0, 0)))
    y = w[0] * zp[:, 0:L]
    for t in range(1, CONV_K):
        y = y + w[t] * zp[:, t:t + L]
    return b * y


def trunk(x, p, norm_mix, w_in, ret_decay_fwd, ret_decay_bwd, conv_w, w_out,
          norm_mlp, w_ff1, w_ff2, norm_ple, w_ple_gate, w_ple_proj, norm_final):
    R = RET_WIDTH
    Cw = CONV_WIDTH
    for i in range(DEPTH):
        h = rmsnorm(x, norm_mix[i])
        proj = h @ w_in[i]
        q = proj[..., 0:R]
        k = proj[..., R:2 * R]
        v = proj[..., 2 * R:3 * R]
        g = proj[..., 3 * R:4 * R]
        o0 = 4 * R
        cb = proj[..., o0:o0 + Cw]
        cc = proj[..., o0 + Cw:o0 + 2 * Cw]
        cu = proj[..., o0 + 2 * Cw:o0 + 3 * Cw]
        ret = retention_branch(q, k, v, g, ret_decay_fwd[i], ret_decay_bwd[i])
        conv = short_conv_branch(cb, cc, cu, conv_w[i])
        x = x + jnp.concatenate([ret, conv], axis=-1) @ w_out[i]
        h2 = rmsnorm(x, norm_mlp[i])
        x = x + jnp.square(jax.nn.relu(h2 @ w_ff1[i])) @ w_ff2[i]
        gate = jax.nn.sigmoid(rmsnorm(x, norm_ple[i]) @ w_ple_gate[i])
        x = x + gate * (p[i] @ w_ple_proj[i])
    return rmsnorm(x, norm_final)


def setup_inputs(seed: int = 0) -> dict:
    key = jax.random.key(seed)
    ks = jax.random.split(key, 20)
    f32 = jnp.float32

    def nrm(k, shape, scale):
        return jax.random.normal(k, shape, f32) * scale

    gamma = 1.0 - 2.0 ** (-5.0 - np.arange(N_RET_HEADS, dtype=np.float32))
    raw0 = jnp.asarray(np.log(-np.log(gamma)), f32)
    return {
        "x_prompt": nrm(ks[0], (BATCH, SEQ, D_MODEL), 1.0),
        "x_sample": nrm(ks[1], (DEC_BATCH, DEC_SEQ, D_MODEL), 1.0),
        "p_prompt": nrm(ks[2], (DEPTH, BATCH, SEQ, PLE_DIM), 1.0),
        "p_sample": nrm(ks[3], (DEPTH, DEC_BATCH, DEC_SEQ, PLE_DIM), 1.0),
        "norm_mix": 1.0 + nrm(ks[4], (DEPTH, D_MODEL), 0.02),
        "w_in": nrm(ks[5], (DEPTH, D_MODEL, IN_PROJ_WIDTH), D_MODEL ** -0.5),
        "ret_decay_fwd": raw0[None, :] + nrm(ks[6], (DEPTH, N_RET_HEADS), 0.05),
        "ret_decay_bwd": raw0[None, :] + nrm(ks[7], (DEPTH, N_RET_HEADS), 0.05),
        "conv_w": nrm(ks[8], (DEPTH, CONV_K, CONV_WIDTH), CONV_K ** -0.5),
        "w_out": nrm(ks[9], (DEPTH, MIX_WIDTH, D_MODEL), MIX_WIDTH ** -0.5),
        "norm_mlp": 1.0 + nrm(ks[10], (DEPTH, D_MODEL), 0.02),
        "w_ff1": nrm(ks[11], (DEPTH, D_MODEL, D_FF), D_MODEL ** -0.5),
        "w_ff2": nrm(ks[12], (DEPTH, D_FF, D_MODEL), D_FF ** -0.5),
        "norm_ple": 1.0 + nrm(ks[13], (DEPTH, D_MODEL), 0.02),
        "w_ple_gate": nrm(ks[14], (DEPTH, D_MODEL, D_MODEL), D_MODEL ** -0.5),
        "w_ple_proj": nrm(ks[15], (DEPTH, PLE_DIM, D_MODEL), PLE_DIM ** -0.5),
        "norm_final": 1.0 + nrm(ks[16], (D_MODEL,), 0.02),
    }


def reference(x_prompt, x_sample, p_prompt, p_sample, norm_mix, w_in, ret_decay_fwd, ret_decay_bwd,
              conv_w, w_out, norm_mlp, w_ff1, w_ff2, norm_ple, w_ple_gate, w_ple_proj, norm_final):
    y_prompt = trunk(x_prompt, p_prompt, norm_mix, w_in, ret_decay_fwd, ret_decay_bwd, conv_w, w_out,
                     norm_mlp, w_ff1, w_ff2, norm_ple, w_ple_gate, w_ple_proj, norm_final)
    y_sample = trunk(x_sample, p_sample, norm_mix, w_in, ret_decay_fwd, ret_decay_bwd, conv_w, w_out,
                     norm_mlp, w_ff1, w_ff2, norm_ple, w_ple_gate, w_ple_proj, norm_final)
    return (y_prompt, y_sample)
```

```python
from contextlib import ExitStack

import numpy as np

import concourse.bass as bass
import concourse.mybir as mybir
from concourse.bass_utils import run_bass_kernel_spmd

F32 = mybir.dt.float32
BF16 = mybir.dt.bfloat16
AF = mybir.ActivationFunctionType
ALU = mybir.AluOpType

NORM_EPS = 1e-6
ROPE_BASE = 10000.0


class Cfg:
    def __init__(self, D=4096, NT=4096, DEPTH=2, PL=256, n_cores=8, stop_after=None):
        self.stop_after = stop_after
        self.pace_a, self.pace_c1, self.pace_c2 = 6, 4, 3
        self.D = D
        self.NT = NT
        self.DEPTH = DEPTH
        self.PL = PL
        self.n_cores = n_cores
        self.FC = D // 128
        self.RW = D // 2
        self.CW = D - self.RW
        self.HD = 256
        self.H = self.RW // self.HD
        self.RFC = self.RW // 128
        self.CFC = self.CW // 128
        self.DFF = 4 * D
        self.INW = 4 * self.RW + 3 * self.CW
        self.TS = 512
        self.NTILE = NT // self.TS
        self.NCH = NT // 128
        self.HALF_CH = self.NCH // 2
        self.HALF_TILE = self.NTILE // 2
        self.FG = 1024
        self.NFG = self.DFF // self.FG
        self.PKC = PL // 128


class Buf:
    __slots__ = ("name", "w", "r")

    def __init__(self, name):
        self.name = name
        self.w = None
        self.r = []


class Op:
    __slots__ = ("eng", "fn", "deps", "need", "sem", "val", "is_dma")


class SemSlot:
    __slots__ = ("sem", "count", "last")

    def __init__(self, sem):
        self.sem = sem
        self.count = 0
        self.last = None


class SemPool:
    def __init__(self, slots):
        self.slots = slots
        self.i = 0

    def next(self):
        s = self.slots[self.i % len(self.slots)]
        self.i += 1
        return s


ENGS = ("pe", "act", "dve", "pool", "sp")


class Sched:
    def __init__(self, nc, es):
        self.nc = nc
        self.es = es
        self.ops = []
        self.eng_sem = {e: es.enter_context(nc.semaphore("sem_" + e)) for e in ("pe", "act", "dve", "pool")}
        self.last_on_eng = {e: None for e in ENGS}
        self.all_slots = []
        self.bg_queue = []
        self.bg_next = 0
        self.bg_done = set()
        self.bg_every = 0
        self.pool_ticks = 0
        self.in_bg = False

    def sem_pool(self, name, n):
        slots = [SemSlot(self.es.enter_context(self.nc.semaphore(f"{name}{i}"))) for i in range(n)]
        self.all_slots.extend(slots)
        return SemPool(slots)

    def op(self, eng, fn, reads=(), writes=(), pool=None, extra_deps=()):
        o = Op()
        o.eng = eng
        o.fn = fn
        o.need = False
        o.sem = None
        o.val = None
        o.is_dma = pool is not None
        deps = set(extra_deps)
        for b in reads:
            if b.w is not None:
                deps.add(b.w)
        for b in writes:
            if b.w is not None:
                deps.add(b.w)
            deps.update(b.r)
        for b in reads:
            b.r.append(o)
        for b in writes:
            b.w = o
            b.r = []
        if pool is not None:
            slot = pool.next()
            if slot.last is not None:
                deps.add(slot.last)
            slot.last = o
            slot.count += 16
            o.sem = slot.sem
            o.val = slot.count
        deps.discard(o)
        o.deps = deps
        for d in deps:
            d.need = True
        self.ops.append(o)
        self.last_on_eng[eng] = o
        if eng == "pool" and not self.in_bg and self.bg_every > 0:
            self.pool_ticks += 1
            if self.pool_ticks % self.bg_every == 0:
                self.bg_release(1)
        return o

    def bg_release(self, n):
        self.in_bg = True
        while n > 0 and self.bg_next < len(self.bg_queue):
            key, fn = self.bg_queue[self.bg_next]
            self.bg_next += 1
            fn()
            self.bg_done.add(key)
            n -= 1
        self.in_bg = False

    def bg_flush_to(self, key):
        while key not in self.bg_done:
            assert self.bg_next < len(self.bg_queue), key
            self.bg_release(1)

    def barrier(self):
        lasts = [o for o in self.last_on_eng.values() if o is not None]
        lasts += [s.last for s in self.all_slots if s.last is not None]
        for e in ENGS:
            self.op(e, None, extra_deps=[o for o in lasts])

    def emit(self, final_wait_ops):
        nc = self.nc
        cnt = {e: 0 for e in self.eng_sem}
        for o in self.ops:
            if o.is_dma:
                continue
            if o.need and o.fn is not None:
                cnt[o.eng] += 1
                o.sem = self.eng_sem[o.eng]
                o.val = cnt[o.eng]
        def resolve(d, out, seen):
            if d in seen:
                return
            seen.add(d)
            if d.fn is None:
                for dd in d.deps:
                    resolve(dd, out, seen)
            else:
                out.append(d)

        per_eng = {e: [] for e in ENGS}
        for o in self.ops:
            per_eng[o.eng].append(o)

        def emit_engine(ename, eng):
            waited = {}
            for o in per_eng[ename]:
                flat = []
                seen = set()
                for d in o.deps:
                    resolve(d, flat, seen)
                for d in flat:
                    if ename == "pe" and d.eng == "pe" and not d.is_dma:
                        continue
                    key = id(d.sem)
                    if waited.get(key, 0) >= d.val:
                        continue
                    eng.wait_ge(d.sem, d.val)
                    waited[key] = d.val
                if o.fn is None:
                    continue
                inst = o.fn(eng)
                if o.is_dma:
                    inst.then_inc(o.sem, 16)
                elif o.need:
                    inst.then_inc(o.sem, 1)
            if ename == "sp":
                for d in final_wait_ops:
                    key = id(d.sem)
                    if waited.get(key, 0) >= d.val:
                        continue
                    eng.wait_ge(d.sem, d.val)
                    waited[key] = d.val

        with nc.Block() as block:
            @block.tensor
            def _(e):
                emit_engine("pe", e)

            @block.scalar
            def _(e):
                emit_engine("act", e)

            @block.vector
            def _(e):
                emit_engine("dve", e)

            @block.gpsimd
            def _(e):
                emit_engine("pool", e)

            @block.sync
            def _(e):
                emit_engine("sp", e)


CONST_COLS = {}


def make_consts():
    i = np.arange(128, dtype=np.float32)
    parts = []
    off = 0

    def add(name, arr):
        nonlocal off
        arr = np.asarray(arr, np.float32)
        CONST_COLS[name] = (off, arr.shape[1])
        off += arr.shape[1]
        parts.append(arr)

    diff = i[None, :] - i[:, None]
    add("ident", np.eye(128))
    add("ones", np.ones((128, 128)))
    add("row_f", np.broadcast_to(i[None, :] + 1.0, (128, 128)))
    add("row_b", np.broadcast_to(128.0 - i[None, :], (128, 128)))
    add("dpos", np.maximum(diff, 0.0))
    add("dneg", np.maximum(-diff, 0.0))
    add("mf", (diff >= 0).astype(np.float32))
    add("mb", (diff < 0).astype(np.float32))
    add("col_f", (127.0 - i)[:, None])
    add("col_b", i[:, None])
    return np.ascontiguousarray(np.concatenate(parts, axis=1))


def rope_tables(pos):
    half = 128
    inv_freq = (ROPE_BASE ** (-np.arange(half, dtype=np.float32) / half)).astype(np.float32)
    ang = (pos.astype(np.float32)[None, :] * inv_freq[:, None]).astype(np.float32)
    return np.ascontiguousarray(np.cos(ang).astype(np.float32)), np.ascontiguousarray(np.sin(ang).astype(np.float32))


def build_program(cfg, debug=False):
    c = cfg
    nc = bass.Bass("TRN2", target_bir_lowering=False)
    D, NT, FC, TS, NTILE, NCH, H = c.D, c.NT, c.FC, c.TS, c.NTILE, c.NCH, c.H
    RFC, CFC, L = c.RFC, c.CFC, c.DEPTH
    NCONST = sum(v[1] for v in CONST_COLS.values())

    def din(name, shape, dt=F32):
        return nc.dram_tensor(name, list(shape), dt, kind="ExternalInput").ap()

    def dscr(name, shape, dt):
        kind = "ExternalOutput" if debug else "Internal"
        return nc.dram_tensor(name, list(shape), dt, kind=kind).ap()

    x_in = din("x", [NT, D])
    p_in = din("p", [L, NT, c.PL])
    w_in_d = din("w_in", [L, D, c.INW])
    w_out_d = din("w_out", [L, D, D])
    w_ff1_d = din("w_ff1", [L, D, c.DFF])
    w_ff2_d = din("w_ff2", [L, c.DFF, D])
    w_gate_d = din("w_gate", [L, D, D])
    w_ple_d = din("w_ple", [L, c.PL, D])
    norms_d = din("norms", [128, (3 * L + 1) * FC])
    convw_d = din("convw", [128, L * 3 * CFC])
    dec_d = din("dec", [128, L * 2 * H])
    consts_d = din("consts", [128, NCONST])
    link_d = din("link", [128, 1])
    cos_d = din("cosT", [128, NT])
    sin_d = din("sinT", [128, NT])
    y_out = nc.dram_tensor("y", [NT, D], F32, kind="ExternalOutput").ap()

    wt_in = dscr("wt_in", [L, c.INW // 128, 128, FC, 128], BF16)
    wt_out = dscr("wt_out", [L, FC, 128, FC, 128], BF16)
    wt_ff1 = dscr("wt_ff1", [L, c.DFF // 128, 128, FC, 128], BF16)
    wt_gate = dscr("wt_gate", [L, FC, 128, FC, 128], BF16)
    wt_ff2 = dscr("wt_ff2", [L, c.NFG, D // 512, 128, c.FG // 128, 512], BF16)
    wt_ple = dscr("wt_ple", [L, 128, c.PKC, D], BF16)
    xT_d = dscr("xT_s", [NTILE, 128, FC, TS], F32)
    pT_d = dscr("pT_s", [L, NTILE, 128, c.PKC, TS], BF16)
    qc_d = dscr("qc_s", [NCH, 128, H, 2, 128], BF16)
    kc_d = dscr("kc_s", [NCH, 128, H, 2, 128], BF16)
    kkf_d = dscr("kkf_s", [NCH, H, 128, 256], BF16)
    v_d = dscr("v_s", [NCH, H, 128, 256], BF16)
    sb_d = dscr("sb_s", [NCH, H, 128, 2, 256], BF16)
    sg_d = dscr("sg_s", [NCH, 128, RFC, 128], BF16)
    z_d = dscr("z_s", [NTILE, 128, CFC, TS], F32)
    cb_d = dscr("cb_s", [NTILE, 128, CFC, TS], F32)
    mix_d = dscr("mix_s", [NTILE, 128, FC, TS], BF16)

    es = ExitStack()
    S = Sched(nc, es)

    uid = [0]

    def sb(name, shape, dt, stack=None):
        uid[0] += 1
        return (stack or es).enter_context(nc.sbuf_tensor(f"{name}_u{uid[0]}", list(shape), dt))

    def ps(name, shape, dt, stack=None):
        return (stack or es).enter_context(nc.psum_tensor(name, list(shape), dt))

    consts = sb("consts_sb", [128, NCONST], F32)
    B_consts = Buf("consts")
    ident_bf = sb("ident_bf", [128, 128], BF16)
    ones_bf = sb("ones_bf", [128, 128], BF16)
    B_cbf = Buf("cbf")
    norms_sb = sb("norms_sb", [128, (3 * L + 1) * FC], F32)
    convw_sb = sb("convw_sb", [128, L * 3 * CFC], F32)
    dec_sb = sb("dec_sb", [128, L * 2 * H], F32)
    link_sb = sb("link_sb", [128, 1], F32)
    B_par = Buf("params")
    NW = 4
    wslots = [sb(f"wslot{i}", [128, 4096], BF16) for i in range(NW)]
    B_w = [Buf(f"wslot{i}") for i in range(NW)]
    wsem = S.sem_pool("wsem", NW)

    def cst(name, cols=None):
        off, n = CONST_COLS[name]
        return consts[:, off:off + n]

    psb = [ps(f"psb{i}", [128, 512], F32) for i in range(8)]
    B_ps = [Buf(f"ps{i}") for i in range(8)]
    ps_rr = [0]

    def next_bank():
        i = ps_rr[0] % 8
        ps_rr[0] += 1
        return i

    ld_pool = S.sem_pool("ld", 6)
    st_pool = S.sem_pool("st", 8)
    misc_pool = S.sem_pool("misc", 2)

    out_store_ops = []

    S.op("sp", lambda e: e.dma_start(out=consts[:], in_=consts_d[:, :]), writes=[B_consts], pool=misc_pool)
    S.op("sp", lambda e: e.dma_start(out=norms_sb[:], in_=norms_d[:, :]), writes=[B_par], pool=misc_pool)
    S.op("sp", lambda e: e.dma_start(out=convw_sb[:], in_=convw_d[:, :]), writes=[B_par], pool=misc_pool)
    S.op("sp", lambda e: e.dma_start(out=dec_sb[:], in_=dec_d[:, :]), writes=[B_par], pool=misc_pool)
    S.op("sp", lambda e: e.dma_start(out=link_sb[:], in_=link_d[:, :]), writes=[B_par], pool=misc_pool)
    S.op("dve", lambda e: e.tensor_copy(out=ident_bf[:], in_=cst("ident")), reads=[B_consts], writes=[B_cbf])
    S.op("dve", lambda e: e.tensor_copy(out=ones_bf[:], in_=cst("ones")), reads=[B_consts], writes=[B_cbf])

    wpieces = []
    wissued = [0]

    def wpiece_ap(slot, shape):
        n = int(np.prod(shape))
        ap = wslots[slot][:, 0:n]
        if len(shape) == 2:
            return ap.rearrange("p (a b) -> p a b", a=shape[0])
        return ap

    def wget(i, B_src=None):
        while wissued[0] < min(len(wpieces), i + NW):
            j = wissued[0]
            src, shape, key = wpieces[j]
            S.bg_flush_to(key)
            slot = j % NW
            dst = wpiece_ap(slot, shape)
            S.op("sp", (lambda e, dst=dst, src=src: e.dma_start(out=dst, in_=src)),
                 reads=[B_wtp[key]], writes=[B_w[slot]], pool=SemPool([wsem.slots[slot]]))
            wissued[0] += 1
        slot = i % NW
        return wpiece_ap(slot, wpieces[i][1]), B_w[slot]

    B_wtp = {}
    cv_pool = S.sem_pool("cv", 12)

    def phase0():
        def add(key, dst, srcap):
            B_wtp[key] = Buf("wt" + str(key))

            def fn(key=key, dst=dst, srcap=srcap):
                S.op("pool", (lambda e: e.dma_start(out=dst, in_=srcap)), writes=[B_wtp[key]], pool=cv_pool)
            S.bg_queue.append((key, fn))

        GK = c.FG // 128
        for l in range(L):
            v_in = w_in_d[l].rearrange("(kc p) n -> p kc n", p=128)
            v_out = w_out_d[l].rearrange("(kc p) n -> p kc n", p=128)
            v_ff1 = w_ff1_d[l].rearrange("(kc p) n -> p kc n", p=128)
            v_ff2 = w_ff2_d[l].rearrange("(kc p) n -> p kc n", p=128)
            v_gate = w_gate_d[l].rearrange("(kc p) n -> p kc n", p=128)
            v_ple = w_ple_d[l].rearrange("(kc p) n -> p kc n", p=128)
            for oc in in_order:
                add(("in", l, oc), wt_in[l, oc], v_in[:, :, oc * 128:(oc + 1) * 128])
            add(("ple", l), wt_ple[l], v_ple)
            for oc in range(FC):
                add(("out", l, oc), wt_out[l, oc], v_out[:, :, oc * 128:(oc + 1) * 128])
            for kind, g in ffn_order():
                if kind == "ff1":
                    for a in range(GK):
                        oc = g * GK + a
                        add(("ff1", l, oc), wt_ff1[l, oc], v_ff1[:, :, oc * 128:(oc + 1) * 128])
                else:
                    for q in range(D // 512):
                        add(("ff2", l, g, q), wt_ff2[l, g, q], v_ff2[:, g * GK:(g + 1) * GK, q * 512:(q + 1) * 512])
            for oc in range(FC):
                add(("gate", l, oc), wt_gate[l, oc], v_gate[:, :, oc * 128:(oc + 1) * 128])
        S.bg_flush_to(("in", 0, in_order[-1]))

    def ffn_order():
        seq = []
        for g in range(c.NFG):
            seq.append(("ff1", g))
            if g >= 1:
                seq.append(("ff2", g - 1))
        seq.append(("ff2", c.NFG - 1))
        return seq

    in_order = []
    for h in range(H):
        for o0 in (0, RFC, 2 * RFC):
            for half in range(2):
                in_order.append(o0 + 2 * h + half)
    for fc in range(RFC):
        in_order.append(3 * RFC + fc)
    for fc in range(CFC):
        in_order.append(4 * RFC + fc)
        in_order.append(4 * RFC + CFC + fc)
        in_order.append(4 * RFC + 2 * CFC + fc)

    B_xT = [Buf(f"xT{t}") for t in range(NTILE)]
    B_pT = [[Buf(f"pT{l}_{t}") for t in range(NTILE)] for l in range(L)]

    def phaseT():
        st = ExitStack()
        xin = [sb(f"tx{i}", [128, D], F32, st) for i in range(2)]
        B_xin = [Buf(f"tx{i}") for i in range(2)]
        xTs = sb("txT", [128, FC, TS], F32, st)
        B_xTs = Buf("txT")
        pin = [sb(f"tp{i}", [128, c.PL], F32, st) for i in range(2)]
        B_pin = [Buf(f"tp{i}") for i in range(2)]
        pTs = [sb(f"tpT{i}", [128, c.PKC, TS], BF16, st) for i in range(2)]
        B_pTs = [Buf(f"tpT{i}") for i in range(2)]
        k = 0
        ev = 0
        for t in range(NTILE):
            for tg in range(4):
                i = k % 2
                k += 1
                r0 = t * TS + tg * 128
                S.op("sp", (lambda e, i=i, r0=r0: e.dma_start(out=xin[i][:], in_=x_in[r0:r0 + 128, :])),
                     writes=[B_xin[i]], pool=ld_pool)
                for f0 in range(0, FC, 4):
                    nf = min(4, FC - f0)
                    bk = next_bank()

                    def tr(e, i=i, f0=f0, nf=nf, bk=bk):
                        for a in range(nf):
                            ins = e.transpose(out=psb[bk][:, a * 128:(a + 1) * 128], in_=xin[i][:, (f0 + a) * 128:(f0 + a + 1) * 128], identity=cst("ident"))
                        return ins
                    S.op("pe", tr, reads=[B_xin[i], B_consts], writes=[B_ps[bk]])
                    outv = xTs[:, f0:f0 + nf, tg * 128:(tg + 1) * 128]
                    inv = psb[bk][:, 0:nf * 128].rearrange("p (a b) -> p a b", a=nf)
                    if ev % 2 == 0:
                        S.op("dve", (lambda e, outv=outv, inv=inv: e.tensor_copy(out=outv, in_=inv)), reads=[B_ps[bk]], writes=[B_xTs])
                    else:
                        S.op("act", (lambda e, outv=outv, inv=inv: e.activation(out=outv, in_=inv, func=AF.Copy)), reads=[B_ps[bk]], writes=[B_xTs])
                    ev += 1
            S.op("pool", (lambda e, t=t: e.dma_start(out=xT_d[t], in_=xTs[:])), reads=[B_xTs], writes=[B_xT[t]], pool=st_pool)
        for l in range(L):
            for t in range(NTILE):
                j = (l * NTILE + t) % 2
                for tg in range(4):
                    i = k % 2
                    k += 1
                    r0 = t * TS + tg * 128
                    S.op("sp", (lambda e, i=i, r0=r0, l=l: e.dma_start(out=pin[i][:], in_=p_in[l, r0:r0 + 128, :])),
                         writes=[B_pin[i]], pool=ld_pool)
                    bk = next_bank()

                    def tr(e, i=i, bk=bk):
                        for a in range(c.PKC):
                            ins = e.transpose(out=psb[bk][:, a * 128:(a + 1) * 128], in_=pin[i][:, a * 128:(a + 1) * 128], identity=cst("ident"))
                        return ins
                    S.op("pe", tr, reads=[B_pin[i], B_consts], writes=[B_ps[bk]])
                    outv = pTs[j][:, :, tg * 128:(tg + 1) * 128]
                    inv = psb[bk][:, 0:c.PKC * 128].rearrange("p (a b) -> p a b", a=c.PKC)
                    S.op("dve", (lambda e, outv=outv, inv=inv: e.tensor_copy(out=outv, in_=inv)), reads=[B_ps[bk]], writes=[B_pTs[j]])
                S.op("pool", (lambda e, l=l, t=t, j=j: e.dma_start(out=pT_d[l, t], in_=pTs[j][:])), reads=[B_pTs[j]], writes=[B_pT[l][t]], pool=st_pool)
        S.barrier()
        st.close()

    def norm_col(l, kind, fc):
        idx = (l * 3 + kind) if l < L else 3 * L
        o = idx * FC + fc
        return norms_sb[:, o:o + 1]

    qdf = sb("qdf", [128, H, 128], F32)
    qdb = sb("qdb", [128, H, 128], F32)
    Mt = sb("Mt", [128, H, 128], F32)
    kdf = sb("kdf", [128, H], F32)
    kdb = sb("kdb", [128, H], F32)
    cdf = sb("cdf", [128, H], F32)
    cdb = sb("cdb", [128, H], F32)
    lgf = sb("lgf", [128, H], F32)
    lgb = sb("lgb", [128, H], F32)
    e1 = sb("e1t", [128, 128], F32)
    e2 = sb("e2t", [128, 128], F32)
    B_dec = Buf("dec")
    zedge = sb("zedge", [128, CFC, NTILE, 2], F32)
    B_zedge = Buf("zedge")

    def layer_setup(l):
        o = l * 2 * H
        A = lambda fn, **kw: S.op("act", fn, reads=[B_par, B_consts, B_dec], writes=[B_dec])
        Dv = lambda fn, **kw: S.op("dve", fn, reads=[B_par, B_consts, B_dec], writes=[B_dec])
        A(lambda e: e.activation(out=lgf[:], in_=dec_sb[:, o:o + H], func=AF.Exp))
        A(lambda e: e.activation(out=lgb[:], in_=dec_sb[:, o + H:o + 2 * H], func=AF.Exp))
        Dv(lambda e: e.tensor_scalar(out=lgf[:], in0=lgf[:], scalar1=-1.0, scalar2=None, op0=ALU.mult))
        Dv(lambda e: e.tensor_scalar(out=lgb[:], in0=lgb[:], scalar1=-1.0, scalar2=None, op0=ALU.mult))
        A(lambda e: e.activation(out=cdf[:], in_=lgf[:], func=AF.Exp, scale=128.0))
        A(lambda e: e.activation(out=cdb[:], in_=lgb[:], func=AF.Exp, scale=128.0))
        for h in range(H):
            A(lambda e, h=h: e.activation(out=qdf[:, h, :], in_=cst("row_f"), func=AF.Exp, scale=lgf[:, h:h + 1]))
            A(lambda e, h=h: e.activation(out=qdb[:, h, :], in_=cst("row_b"), func=AF.Exp, scale=lgb[:, h:h + 1]))
            A(lambda e, h=h: e.activation(out=kdf[:, h:h + 1], in_=cst("col_f"), func=AF.Exp, scale=lgf[:, h:h + 1]))
            A(lambda e, h=h: e.activation(out=kdb[:, h:h + 1], in_=cst("col_b"), func=AF.Exp, scale=lgb[:, h:h + 1]))
            A(lambda e, h=h: e.activation(out=e1[:], in_=cst("dpos"), func=AF.Exp, scale=lgf[:, h:h + 1]))
            A(lambda e, h=h: e.activation(out=e2[:], in_=cst("dneg"), func=AF.Exp, scale=lgb[:, h:h + 1]))
            Dv(lambda e: e.tensor_tensor(out=e1[:], in0=e1[:], in1=cst("mf"), op=ALU.mult))
            Dv(lambda e: e.tensor_tensor(out=e2[:], in0=e2[:], in1=cst("mb"), op=ALU.mult))
            Dv(lambda e, h=h: e.tensor_tensor(out=Mt[:, h, :], in0=e1[:], in1=e2[:], op=ALU.add))

    B_qc = [[Buf(f"qc_{g}_{h}") for h in range(H)] for g in range(NCH)]
    B_kc = [[Buf(f"kc_{g}_{h}") for h in range(H)] for g in range(NCH)]
    B_kkf = [[Buf(f"kkf_{g}_{h}") for h in range(H)] for g in range(NCH)]
    B_v = [[Buf(f"v_{g}_{h}") for h in range(H)] for g in range(NCH)]
    B_sbd = [[Buf(f"sb_{g}_{h}") for h in range(H)] for g in range(NCH)]
    B_sg = [Buf(f"sg_{g}") for g in range(NCH)]
    B_z = [Buf(f"z_{t}") for t in range(NTILE)]
    B_cb = [Buf(f"cb_{t}") for t in range(NTILE)]
    B_mix = [Buf(f"mix_{t}") for t in range(NTILE)]

    def phaseA(l):
        st = ExitStack()
        XP = 8 if FC >= 8 else FC
        NXP = FC // XP
        xp = [sb(f"a_xp{i}", [128, XP, TS], F32, st) for i in range(2)]
        B_xp = [Buf(f"a_xp{i}") for i in range(2)]
        hT = sb("a_hT", [128, FC, TS], BF16, st)
        B_hT = Buf("a_hT")
        sqb = [sb(f"a_sq{i}", [128, TS], BF16, st) for i in range(2)]
        B_sq = [Buf(f"a_sq{i}") for i in range(2)]
        rstd = sb("a_rstd", [128, TS], F32, st)
        B_rstd = Buf("a_rstd")
        cs = [sb(f"a_cs{i}", [128, 2, TS], F32, st) for i in range(2)]
        B_cs = [Buf(f"a_cs{i}") for i in range(2)]
        AB = [sb(f"a_AB{i}", [128, 2, TS], F32, st) for i in range(2)]
        B_AB = [Buf(f"a_AB{i}") for i in range(2)]
        t12 = sb("a_t12", [128, 2, TS], F32, st)
        B_t12 = Buf("a_t12")
        t34 = sb("a_t34", [128, 2, TS], F32, st)
        B_t34 = Buf("a_t34")
        qst = [sb(f"a_q{i}", [128, 2, TS], BF16, st) for i in range(2)]
        B_qst = [Buf(f"a_q{i}") for i in range(2)]
        kTh = [sb(f"a_kT{i}", [128, 2, TS], BF16, st) for i in range(2)]
        B_kTh = [Buf(f"a_kT{i}") for i in range(2)]
        vTh = [sb(f"a_vT{i}", [128, 2, TS], BF16, st) for i in range(2)]
        B_vTh = [Buf(f"a_vT{i}") for i in range(2)]
        tm = [sb(f"a_tm{i}", [128, 4, 3, 256], BF16, st) for i in range(2)]
        B_tm = [Buf(f"a_tm{i}") for i in range(2)]
        S32 = sb("a_S32", [128, H, 2, 256], F32, st)
        B_S32 = [Buf(f"a_S32_{h}") for h in range(H)]
        Sbf = [sb(f"a_Sbf{i}", [128, 2, 256], BF16, st) for i in range(2)]
        B_Sbf = [Buf(f"a_Sbf{i}") for i in range(2)]
        sgs = [sb(f"a_sg{i}", [128, TS], BF16, st) for i in range(2)]
        B_sgs = [Buf(f"a_sg{i}") for i in range(2)]
        zst = [sb(f"a_z{i}", [128, TS], F32, st) for i in range(2)]
        B_zst = [Buf(f"a_z{i}") for i in range(2)]
        cbst = [sb(f"a_cb{i}", [128, TS], F32, st) for i in range(2)]
        B_cbst = [Buf(f"a_cb{i}") for i in range(2)]
        rr = {"xp": 0, "sq": 0, "cs": 0, "AB": 0, "q": 0, "kT": 0, "vT": 0, "tm": 0, "Sbf": 0, "sg": 0, "z": 0, "cb": 0}

        for h in range(H):
            S.op("pool", (lambda e, h=h: e.memset(S32[:, h, :, :], 0.0)), writes=[B_S32[h]])

        base = len(wpieces)
        B_src = None
        for t in range(NTILE):
            for oc in in_order:
                wpieces.append((wt_in[l, oc], [FC, 128], ("in", l, oc)))
        wi = [base]

        def proj(bk):
            i = wi[0]
            wi[0] += 1
            wap, bw = wget(i, B_src)

            def f(e, wap=wap, bk=bk):
                for kc in range(FC):
                    ins = e.matmul(psb[bk][:], wap[:, kc, :], hT[:, kc, :], start=(kc == 0), stop=(kc == FC - 1))
                return ins
            S.op("pe", f, reads=[bw, B_hT], writes=[B_ps[bk]])

        def rotary(ab, ci, out1, out2, Bout):
            A_ = AB[ab][:, 0, :]
            B_ = AB[ab][:, 1, :]
            cos_ = cs[ci][:, 0, :]
            sin_ = cs[ci][:, 1, :]
            S.op("dve", lambda e: e.tensor_tensor(out=t12[:, 0, :], in0=A_, in1=cos_, op=ALU.mult), reads=[B_AB[ab], B_cs[ci]], writes=[B_t12])
            S.op("dve", lambda e: e.tensor_tensor(out=t12[:, 1, :], in0=B_, in1=sin_, op=ALU.mult), reads=[B_AB[ab], B_cs[ci]], writes=[B_t12])
            S.op("dve", lambda e: e.tensor_tensor(out=out1, in0=t12[:, 0, :], in1=t12[:, 1, :], op=ALU.subtract), reads=[B_t12], writes=[Bout])
            S.op("pool", lambda e: e.tensor_tensor(out=t34[:, 0, :], in0=B_, in1=cos_, op=ALU.mult), reads=[B_AB[ab], B_cs[ci]], writes=[B_t34])
            S.op("pool", lambda e: e.tensor_tensor(out=t34[:, 1, :], in0=A_, in1=sin_, op=ALU.mult), reads=[B_AB[ab], B_cs[ci]], writes=[B_t34])
            S.op("pool", lambda e: e.tensor_tensor(out=out2, in0=t34[:, 0, :], in1=t34[:, 1, :], op=ALU.add), reads=[B_t34], writes=[Bout])

        for t in reversed(range(NTILE)):
            tok0 = t * TS
            g0 = t * 4
            ci = rr["cs"] % 2
            rr["cs"] += 1
            S.op("sp", (lambda e, ci=ci, tok0=tok0: e.dma_start(out=cs[ci][:, 0, :], in_=cos_d[:, tok0:tok0 + TS])), writes=[B_cs[ci]], pool=ld_pool)
            S.op("sp", (lambda e, ci=ci, tok0=tok0: e.dma_start(out=cs[ci][:, 1, :], in_=sin_d[:, tok0:tok0 + TS])), writes=[B_cs[ci]], pool=ld_pool)
            bk_n = next_bank()
            cntm = 0
            for pi in range(NXP):
                i = rr["xp"] % 2
                rr["xp"] += 1
                S.op("sp", (lambda e, i=i, t=t, pi=pi: e.dma_start(out=xp[i][:], in_=xT_d[t, :, pi * XP:(pi + 1) * XP, :])),
                     reads=[B_xT[t]], writes=[B_xp[i]], pool=ld_pool)
                for a in range(XP):
                    j = rr["sq"] % 2
                    rr["sq"] += 1
                    S.op("act", (lambda e, i=i, a=a, j=j: e.activation(out=sqb[j][:], in_=xp[i][:, a, :], func=AF.Square)), reads=[B_xp[i]], writes=[B_sq[j]])
                    first = (cntm == 0)
                    last = (cntm == FC - 1)
                    cntm += 1
                    S.op("pe", (lambda e, j=j, first=first, last=last, bk=bk_n: e.matmul(psb[bk][:], ones_bf[:], sqb[j][:], start=first, stop=last)),
                         reads=[B_sq[j], B_cbf], writes=[B_ps[bk_n]])
            S.op("act", (lambda e, bk=bk_n: e.activation(out=rstd[:], in_=psb[bk][:], func=AF.Sqrt, scale=1.0 / D, bias=NORM_EPS)),
                 reads=[B_ps[bk_n]], writes=[B_rstd])
            S.op("dve", (lambda e: e.reciprocal(out=rstd[:], in_=rstd[:])), reads=[B_rstd], writes=[B_rstd])
            for pi in range(NXP):
                i = rr["xp"] % 2
                rr["xp"] += 1
                S.op("sp", (lambda e, i=i, t=t, pi=pi: e.dma_start(out=xp[i][:], in_=xT_d[t, :, pi * XP:(pi + 1) * XP, :])),
                     reads=[B_xT[t]], writes=[B_xp[i]], pool=ld_pool)
                for a in range(XP):
                    fc = pi * XP + a
                    S.op("dve", (lambda e, i=i, a=a, fc=fc: e.scalar_tensor_tensor(out=hT[:, fc, :], in0=xp[i][:, a, :], scalar=norm_col(l, 0, fc), in1=rstd[:], op0=ALU.mult, op1=ALU.mult)),
                         reads=[B_xp[i], B_rstd, B_par], writes=[B_hT])
            def post1(h, ki, vi):
                ti = h % 2
                for cc_ in range(4):
                    bk = next_bank()
                    pv = psb[bk][:].bitcast(BF16)

                    def trk(e, cc_=cc_, ki=ki, vi=vi, pv=pv):
                        for half in range(2):
                            e.transpose(out=pv[:, half * 128:(half + 1) * 128], in_=kTh[ki][:, half, cc_ * 128:(cc_ + 1) * 128], identity=ident_bf[:])
                        for half in range(2):
                            ins = e.transpose(out=pv[:, 256 + half * 128:256 + (half + 1) * 128], in_=vTh[vi][:, half, cc_ * 128:(cc_ + 1) * 128], identity=ident_bf[:])
                        return ins
                    S.op("pe", trk, reads=[B_kTh[ki], B_vTh[vi], B_cbf], writes=[B_ps[bk]])
                    S.op("dve", (lambda e, ti=ti, cc_=cc_, pv=pv, h=h: e.tensor_scalar(out=tm[ti][:, cc_, 0, :], in0=pv[:, 0:256], scalar1=kdf[:, h:h + 1], scalar2=None, op0=ALU.mult)),
                         reads=[B_ps[bk], B_dec], writes=[B_tm[ti]])
                    S.op("dve", (lambda e, ti=ti, cc_=cc_, pv=pv, h=h: e.tensor_scalar(out=tm[ti][:, cc_, 1, :], in0=pv[:, 0:256], scalar1=kdb[:, h:h + 1], scalar2=None, op0=ALU.mult)),
                         reads=[B_ps[bk], B_dec], writes=[B_tm[ti]])
                    S.op("act", (lambda e, ti=ti, cc_=cc_, pv=pv: e.activation(out=tm[ti][:, cc_, 2, :], in_=pv[:, 256:512], func=AF.Copy)),
                         reads=[B_ps[bk]], writes=[B_tm[ti]])
                S.op("pool", (lambda e, ti=ti, g0=g0, h=h: e.dma_start(out=kkf_d[g0:g0 + 4, h].rearrange("c p d -> p c d"), in_=tm[ti][:, :, 0, :])),
                     reads=[B_tm[ti]], writes=[B_kkf[g0 + a][h] for a in range(4)], pool=st_pool)
                S.op("pool", (lambda e, ti=ti, g0=g0, h=h: e.dma_start(out=v_d[g0:g0 + 4, h].rearrange("c p d -> p c d"), in_=tm[ti][:, :, 2, :])),
                     reads=[B_tm[ti]], writes=[B_v[g0 + a][h] for a in range(4)], pool=st_pool)
            def post2(h):
                ti = h % 2
                for cc_ in reversed(range(4)):
                    g = g0 + cc_
                    si = rr["Sbf"] % 2
                    rr["Sbf"] += 1
                    if g == c.HALF_CH - 1:
                        S.op("dve", (lambda e, h=h: e.tensor_scalar(out=S32[:, h, :, :], in0=S32[:, h, :, :], scalar1=link_sb[:, 0:1], scalar2=None, op0=ALU.mult)),
                             reads=[B_S32[h], B_par], writes=[B_S32[h]])
                    S.op("act", (lambda e, si=si, h=h: e.activation(out=Sbf[si][:], in_=S32[:, h, :, :], func=AF.Copy)),
                         reads=[B_S32[h]], writes=[B_Sbf[si]])
                    S.op("pool", (lambda e, si=si, g=g, h=h: e.dma_start(out=sb_d[g, h], in_=Sbf[si][:])), reads=[B_Sbf[si]], writes=[B_sbd[g][h]], pool=st_pool)
                    bk = next_bank()

                    def upd(e, ti=ti, cc_=cc_, bk=bk):
                        for dh in range(2):
                            ins = e.matmul(psb[bk][:, dh * 256:(dh + 1) * 256], tm[ti][:, cc_, 1, dh * 128:(dh + 1) * 128], tm[ti][:, cc_, 2, :], start=True, stop=True)
                        return ins
                    S.op("pe", upd, reads=[B_tm[ti]], writes=[B_ps[bk]])
                    S.op("dve", (lambda e, h=h, bk=bk: e.scalar_tensor_tensor(out=S32[:, h, :, :], in0=S32[:, h, :, :], scalar=cdb[:, h:h + 1],
                                                                              in1=psb[bk][:].rearrange("p (a b) -> p a b", a=2), op0=ALU.mult, op1=ALU.add)),
                         reads=[B_S32[h], B_ps[bk], B_dec], writes=[B_S32[h]])

            pend = []
            for h in range(H):
                ab = rr["AB"] % 2
                rr["AB"] += 1
                qi = rr["q"] % 2
                rr["q"] += 1
                for half in range(2):
                    bk = next_bank()
                    proj(bk)
                    S.op("act", (lambda e, ab=ab, half=half, bk=bk: e.activation(out=AB[ab][:, half, :], in_=psb[bk][:], func=AF.Copy)),
                         reads=[B_ps[bk]], writes=[B_AB[ab]])
                rotary(ab, ci, qst[qi][:, 0, :], qst[qi][:, 1, :], B_qst[qi])
                for half in range(2):
                    S.op("pool", (lambda e, g0=g0, h=h, qi=qi, half=half: e.dma_start(out=qc_d[g0:g0 + 4, :, h, half, :].rearrange("c p i -> p c i"),
                                                                                      in_=qst[qi][:, half, :].rearrange("p (c i) -> p c i", c=4))),
                         reads=[B_qst[qi]], writes=[B_qc[g0 + a][h] for a in range(4)], pool=st_pool)
                ab = rr["AB"] % 2
                rr["AB"] += 1
                ki = rr["kT"] % 2
                rr["kT"] += 1
                for half in range(2):
                    bk = next_bank()
                    proj(bk)
                    S.op("act", (lambda e, ab=ab, half=half, bk=bk: e.activation(out=AB[ab][:, half, :], in_=psb[bk][:], func=AF.Copy, scale=float(c.HD ** -0.5))),
                         reads=[B_ps[bk]], writes=[B_AB[ab]])
                rotary(ab, ci, kTh[ki][:, 0, :], kTh[ki][:, 1, :], B_kTh[ki])
                for half in range(2):
                    S.op("pool", (lambda e, g0=g0, h=h, ki=ki, half=half: e.dma_start(out=kc_d[g0:g0 + 4, :, h, half, :].rearrange("c p i -> p c i"),
                                                                                      in_=kTh[ki][:, half, :].rearrange("p (c i) -> p c i", c=4))),
                         reads=[B_kTh[ki]], writes=[B_kc[g0 + a][h] for a in range(4)], pool=st_pool)
                if pend:
                    post1(*pend[-1])
                vi = rr["vT"] % 2
                rr["vT"] += 1
                for half in range(2):
                    bk = next_bank()
                    proj(bk)
                    S.op("act", (lambda e, vi=vi, half=half, bk=bk: e.activation(out=vTh[vi][:, half, :], in_=psb[bk][:], func=AF.Copy)),
                         reads=[B_ps[bk]], writes=[B_vTh[vi]])
                if pend:
                    post2(pend[-1][0])
                    pend.pop()
                pend.append((h, ki, vi))
            for fc in range(RFC):
                if fc == 0:
                    post1(*pend[-1])
                if fc == min(2, RFC - 1):
                    post2(pend[-1][0])
                    pend.pop()
                bk = next_bank()
                proj(bk)
                gi = rr["sg"] % 2
                rr["sg"] += 1
                S.op("act", (lambda e, gi=gi, bk=bk: e.activation(out=sgs[gi][:], in_=psb[bk][:], func=AF.Silu)), reads=[B_ps[bk]], writes=[B_sgs[gi]])
                S.op("pool", (lambda e, gi=gi, g0=g0, fc=fc: e.dma_start(out=sg_d[g0:g0 + 4, :, fc, :].rearrange("c p i -> p c i"), in_=sgs[gi][:].rearrange("p (c i) -> p c i", c=4))),
                     reads=[B_sgs[gi]], writes=[B_sg[g0 + a] for a in range(4)], pool=st_pool)
            for fc in range(CFC):
                bk = next_bank()
                proj(bk)
                bi = rr["cb"] % 2
                rr["cb"] += 1
                S.op("act", (lambda e, bi=bi, bk=bk: e.activation(out=cbst[bi][:], in_=psb[bk][:], func=AF.Copy)), reads=[B_ps[bk]], writes=[B_cbst[bi]])
                S.op("pool", (lambda e, bi=bi, t=t, fc=fc: e.dma_start(out=cb_d[t, :, fc, :], in_=cbst[bi][:])), reads=[B_cbst[bi]], writes=[B_cb[t]], pool=st_pool)
                bk = next_bank()
                proj(bk)
                zi = rr["z"] % 2
                rr["z"] += 1
                S.op("act", (lambda e, zi=zi, bk=bk: e.activation(out=zst[zi][:], in_=psb[bk][:], func=AF.Copy)), reads=[B_ps[bk]], writes=[B_zst[zi]])
                bk = next_bank()
                proj(bk)
                S.op("dve", (lambda e, zi=zi, bk=bk: e.tensor_tensor(out=zst[zi][:], in0=psb[bk][:], in1=zst[zi][:], op=ALU.mult)), reads=[B_ps[bk], B_zst[zi]], writes=[B_zst[zi]])
                S.op("dve", (lambda e, zi=zi, fc=fc, t=t: e.tensor_copy(out=zedge[:, fc, t, 0:1], in_=zst[zi][:, 0:1])), reads=[B_zst[zi]], writes=[B_zedge])
                S.op("dve", (lambda e, zi=zi, fc=fc, t=t: e.tensor_copy(out=zedge[:, fc, t, 1:2], in_=zst[zi][:, TS - 1:TS])), reads=[B_zst[zi]], writes=[B_zedge])
                S.op("pool", (lambda e, zi=zi, t=t, fc=fc: e.dma_start(out=z_d[t, :, fc, :], in_=zst[zi][:])), reads=[B_zst[zi]], writes=[B_z[t]], pool=st_pool)
        assert wi[0] == len(wpieces)
        S.barrier()
        st.close()

    def phaseC1(l):
        st = ExitStack()
        qcs = [sb(f"c_q{i}", [128, H, 2, 128], BF16, st) for i in range(2)]
        kcs = [sb(f"c_k{i}", [128, H, 2, 128], BF16, st) for i in range(2)]
        qts = [sb(f"c_qt{i}", [128, 2, H, 2, 128], BF16, st) for i in range(2)]
        sgc = [sb(f"c_sg{i}", [128, RFC, 128], BF16, st) for i in range(2)]
        vs = [sb(f"c_v{i}", [128, H, 256], BF16, st) for i in range(2)]
        kks = [sb(f"c_kk{i}", [128, H, 256], BF16, st) for i in range(2)]
        sbs = [sb(f"c_sb{i}", [128, H, 2, 256], BF16, st) for i in range(2)]
        B_in = [Buf(f"c_in{i}") for i in range(2)]
        B_qt = [Buf(f"c_qt{i}") for i in range(2)]
        zt = [sb(f"c_z{i}", [128, TS + 2], F32, st) for i in range(2)]
        B_zt = [Buf(f"c_z{i}") for i in range(2)]
        cbt = [sb(f"c_cb{i}", [128, TS], F32, st) for i in range(2)]
        B_cbt = [Buf(f"c_cb{i}") for i in range(2)]
        cvo = [sb(f"c_cvo{i}", [128, TS], BF16, st) for i in range(2)]
        B_cvo = [Buf(f"c_cvo{i}") for i in range(2)]
        yt = sb("c_y", [128, TS], F32, st)
        B_yt = Buf("c_y")
        mixc = [sb(f"c_mix{i}", [128, RFC, 128], BF16, st) for i in range(2)]
        B_mixc = [Buf(f"c_mix{i}") for i in range(2)]
        sTm = sb("c_sTm", [128, H, 128], BF16, st)
        B_sTm = Buf("c_sTm")
        S32 = sb("c_S32", [128, H, 2, 256], F32, st)
        Sbf = sb("c_Sbf", [128, H, 2, 256], BF16, st)
        B_S32 = [Buf(f"c_S32_{h}") for h in range(H)]
        B_Sbf = [Buf(f"c_Sbf_{h}") for h in range(H)]
        stats = sb("c_stats", [128, H, 6], F32, st)
        mv = sb("c_mv", [128, H, 2], F32, st)
        rs = sb("c_rs", [128, H], F32, st)
        nb = sb("c_nb", [128, H], F32, st)
        B_stat = [Buf(f"c_stat{h}") for h in range(H)]
        B_rs = Buf("c_rs")
        on = sb("c_on", [128, H, 256], BF16, st)
        B_on = Buf("c_on")
        for h in range(H):
            S.op("pool", (lambda e, h=h: e.memset(S32[:, h, :, :], 0.0)), writes=[B_S32[h]])
            S.op("pool", (lambda e, h=h: e.memset(Sbf[:, h, :, :], 0.0)), writes=[B_Sbf[h]])
        rrv = 0
        rrz = 0
        for t in range(NTILE):
            for cc_ in range(4):
                g = t * 4 + cc_
                vi = rrv % 2
                rrv += 1
                csl = slice(cc_ * 128, (cc_ + 1) * 128)
                S.op("sp", (lambda e, vi=vi, g=g: e.dma_start(out=qcs[vi][:], in_=qc_d[g])), reads=B_qc[g], writes=[B_in[vi]], pool=ld_pool)
                S.op("sp", (lambda e, vi=vi, g=g: e.dma_start(out=kcs[vi][:], in_=kc_d[g])), reads=B_kc[g], writes=[B_in[vi]], pool=ld_pool)
                S.op("sp", (lambda e, vi=vi, g=g: e.dma_start(out=sgc[vi][:], in_=sg_d[g])), reads=[B_sg[g]], writes=[B_in[vi]], pool=ld_pool)
                S.op("sp", (lambda e, vi=vi, g=g: e.dma_start(out=vs[vi][:], in_=v_d[g].rearrange("h p d -> p h d"))), reads=B_v[g], writes=[B_in[vi]], pool=ld_pool)
                S.op("sp", (lambda e, vi=vi, g=g: e.dma_start(out=kks[vi][:], in_=kkf_d[g].rearrange("h p d -> p h d"))), reads=B_kkf[g], writes=[B_in[vi]], pool=ld_pool)
                S.op("sp", (lambda e, vi=vi, g=g: e.dma_start(out=sbs[vi][:], in_=sb_d[g].rearrange("h p a d -> p h a d"))), reads=B_sbd[g], writes=[B_in[vi]], pool=ld_pool)
                for half in range(2):
                    S.op("dve", (lambda e, vi=vi, half=half: e.tensor_tensor(out=qts[vi][:, 0, :, half, :], in0=qcs[vi][:, :, half, :], in1=qdf[:], op=ALU.mult)),
                         reads=[B_in[vi], B_dec], writes=[B_qt[vi]])
                    S.op("pool", (lambda e, vi=vi, half=half: e.tensor_tensor(out=qts[vi][:, 1, :, half, :], in0=qcs[vi][:, :, half, :], in1=qdb[:], op=ALU.mult)),
                         reads=[B_in[vi], B_dec], writes=[B_qt[vi]])
                if g == c.HALF_CH:
                    for h in range(H):
                        S.op("dve", (lambda e, h=h: e.tensor_scalar(out=S32[:, h, :, :], in0=S32[:, h, :, :], scalar1=link_sb[:, 0:1], scalar2=None, op0=ALU.mult)),
                             reads=[B_S32[h], B_par], writes=[B_S32[h]])
                        S.op("act", (lambda e, h=h: e.activation(out=Sbf[:, h, :, :], in_=S32[:, h, :, :], func=AF.Copy)), reads=[B_S32[h]], writes=[B_Sbf[h]])
                HB = min(H, 4)
                for hb in range(0, H, HB):
                    bk = next_bank()

                    def sc(e, hb=hb, bk=bk, vi=vi):
                        for a in range(HB):
                            h = hb + a
                            for half in range(2):
                                ins = e.matmul(psb[bk][:, a * 128:(a + 1) * 128], kcs[vi][:, h, half, :], qcs[vi][:, h, half, :], start=(half == 0), stop=(half == 1))
                        return ins
                    S.op("pe", sc, reads=[B_in[vi]], writes=[B_ps[bk]])
                    S.op("dve", (lambda e, hb=hb, bk=bk: e.tensor_tensor(out=sTm[:, hb:hb + HB, :], in0=psb[bk][:, 0:HB * 128].rearrange("p (a b) -> p a b", a=HB), in1=Mt[:, hb:hb + HB, :], op=ALU.mult)),
                         reads=[B_ps[bk], B_dec], writes=[B_sTm])
                obank = {}
                for hp in range(0, H, 2):
                    bk = next_bank()

                    def om(e, hp=hp, bk=bk, vi=vi):
                        for a in range(2):
                            h = hp + a
                            o_ = psb[bk][:, a * 256:(a + 1) * 256]
                            e.matmul(o_, sTm[:, h, :], vs[vi][:, h, :], start=True, stop=False)
                            e.matmul(o_, qts[vi][:, 0, h, 0, :], Sbf[:, h, 0, :], start=False, stop=False)
                            e.matmul(o_, qts[vi][:, 0, h, 1, :], Sbf[:, h, 1, :], start=False, stop=False)
                            e.matmul(o_, qts[vi][:, 1, h, 0, :], sbs[vi][:, h, 0, :], start=False, stop=False)
                            ins = e.matmul(o_, qts[vi][:, 1, h, 1, :], sbs[vi][:, h, 1, :], start=False, stop=True)
                        return ins
                    S.op("pe", om, reads=[B_sTm, B_in[vi], B_qt[vi], B_Sbf[hp], B_Sbf[hp + 1]], writes=[B_ps[bk]])
                    for a in range(2):
                        h = hp + a
                        obank[h] = (bk, a)
                        S.op("dve", (lambda e, h=h, a=a, bk=bk: e.bn_stats(out=stats[:, h, :], in_=psb[bk][:, a * 256:(a + 1) * 256])), reads=[B_ps[bk], B_rs], writes=[B_stat[h]])
                        S.op("dve", (lambda e, h=h: e.bn_aggr(out=mv[:, h, :], in_=stats[:, h, :])), reads=[B_stat[h]], writes=[B_stat[h]])
                S.op("act", (lambda e: e.activation(out=rs[:], in_=mv[:, :, 1], func=AF.Sqrt, scale=1.0, bias=NORM_EPS)), reads=B_stat, writes=[B_rs])
                S.op("dve", (lambda e: e.reciprocal(out=rs[:], in_=rs[:])), reads=[B_rs], writes=[B_rs])
                S.op("dve", (lambda e: e.scalar_tensor_tensor(out=nb[:], in0=mv[:, :, 0], scalar=-1.0, in1=rs[:], op0=ALU.mult, op1=ALU.mult)), reads=B_stat + [B_rs], writes=[B_rs])
                for h in range(H):
                    bk, a = obank[h]
                    S.op("act", (lambda e, h=h, a=a, bk=bk: e.activation(out=on[:, h, :], in_=psb[bk][:, a * 256:(a + 1) * 256], func=AF.Identity, scale=rs[:, h:h + 1], bias=nb[:, h:h + 1])),
                         reads=[B_ps[bk], B_rs], writes=[B_on])
                mi = g % 2
                for f0 in range(0, RFC, 8):
                    nf = min(8, RFC - f0)
                    bk = next_bank()
                    pv = psb[bk][:].bitcast(BF16)

                    def trf(e, f0=f0, nf=nf, pv=pv):
                        for a in range(nf):
                            fc = f0 + a
                            ins = e.transpose(out=pv[:, a * 128:(a + 1) * 128], in_=on[:, fc // 2, (fc % 2) * 128:(fc % 2 + 1) * 128], identity=ident_bf[:])
                        return ins
                    S.op("pe", trf, reads=[B_on, B_cbf], writes=[B_ps[bk]])
                    S.op("dve", (lambda e, f0=f0, nf=nf, pv=pv, mi=mi, vi=vi: e.tensor_tensor(out=mixc[mi][:, f0:f0 + nf, :], in0=pv[:, 0:nf * 128].rearrange("p (a b) -> p a b", a=nf), in1=sgc[vi][:, f0:f0 + nf, :], op=ALU.mult)),
                         reads=[B_ps[bk], B_in[vi]], writes=[B_mixc[mi]])
                S.op("pool", (lambda e, t=t, mi=mi, csl=csl: e.dma_start(out=mix_d[t, :, 0:RFC, csl], in_=mixc[mi][:])), reads=[B_mixc[mi]], writes=[B_mix[t]], pool=st_pool)
                for h in range(H):
                    bk = next_bank()

                    def upd(e, h=h, bk=bk, vi=vi):
                        for dh in range(2):
                            ins = e.matmul(psb[bk][:, dh * 256:(dh + 1) * 256], kks[vi][:, h, dh * 128:(dh + 1) * 128], vs[vi][:, h, :], start=True, stop=True)
                        return ins
                    S.op("pe", upd, reads=[B_in[vi]], writes=[B_ps[bk]])
                    S.op("dve", (lambda e, h=h, bk=bk: e.scalar_tensor_tensor(out=S32[:, h, :, :], in0=S32[:, h, :, :], scalar=cdf[:, h:h + 1],
                                                                              in1=psb[bk][:].rearrange("p (a b) -> p a b", a=2), op0=ALU.mult, op1=ALU.add)),
                         reads=[B_S32[h], B_ps[bk], B_dec], writes=[B_S32[h]])
                    S.op("act", (lambda e, h=h: e.activation(out=Sbf[:, h, :, :], in_=S32[:, h, :, :], func=AF.Copy)), reads=[B_S32[h]], writes=[B_Sbf[h]])
            for fc in range(CFC):
                zi = rrz % 2
                rrz += 1
                S.op("sp", (lambda e, t=t, fc=fc, zi=zi: e.dma_start(out=zt[zi][:, 1:TS + 1], in_=z_d[t, :, fc, :])), reads=[B_z[t]], writes=[B_zt[zi]], pool=ld_pool)
                S.op("sp", (lambda e, t=t, fc=fc, zi=zi: e.dma_start(out=cbt[zi][:], in_=cb_d[t, :, fc, :])), reads=[B_cb[t]], writes=[B_cbt[zi]], pool=ld_pool)
                if t == 0:
                    S.op("dve", (lambda e, zi=zi: e.memset(zt[zi][:, 0:1], 0.0)), writes=[B_zt[zi]])
                elif t == c.HALF_TILE:
                    S.op("dve", (lambda e, zi=zi, fc=fc, t=t: e.tensor_scalar(out=zt[zi][:, 0:1], in0=zedge[:, fc, t - 1, 1:2], scalar1=link_sb[:, 0:1], scalar2=None, op0=ALU.mult)), reads=[B_zedge, B_par], writes=[B_zt[zi]])
                else:
                    S.op("dve", (lambda e, zi=zi, fc=fc, t=t: e.tensor_copy(out=zt[zi][:, 0:1], in_=zedge[:, fc, t - 1, 1:2])), reads=[B_zedge], writes=[B_zt[zi]])
                if t == NTILE - 1:
                    S.op("dve", (lambda e, zi=zi: e.memset(zt[zi][:, TS + 1:TS + 2], 0.0)), writes=[B_zt[zi]])
                elif t == c.HALF_TILE - 1:
                    S.op("dve", (lambda e, zi=zi, fc=fc, t=t: e.tensor_scalar(out=zt[zi][:, TS + 1:TS + 2], in0=zedge[:, fc, t + 1, 0:1], scalar1=link_sb[:, 0:1], scalar2=None, op0=ALU.mult)), reads=[B_zedge, B_par], writes=[B_zt[zi]])
                else:
                    S.op("dve", (lambda e, zi=zi, fc=fc, t=t: e.tensor_copy(out=zt[zi][:, TS + 1:TS + 2], in_=zedge[:, fc, t + 1, 0:1])), reads=[B_zedge], writes=[B_zt[zi]])

                def wcol(tap, fc=fc):
                    o = (l * 3 + tap) * CFC + fc
                    return convw_sb[:, o:o + 1]
                S.op("pool", (lambda e, zi=zi, wcol=wcol: e.tensor_scalar(out=yt[:], in0=zt[zi][:, 0:TS], scalar1=wcol(0), scalar2=None, op0=ALU.mult)), reads=[B_zt[zi], B_par], writes=[B_yt])
                S.op("dve", (lambda e, zi=zi, wcol=wcol: e.scalar_tensor_tensor(out=yt[:], in0=zt[zi][:, 1:TS + 1], scalar=wcol(1), in1=yt[:], op0=ALU.mult, op1=ALU.add)), reads=[B_zt[zi], B_par, B_yt], writes=[B_yt])
                S.op("dve", (lambda e, zi=zi, wcol=wcol: e.scalar_tensor_tensor(out=yt[:], in0=zt[zi][:, 2:TS + 2], scalar=wcol(2), in1=yt[:], op0=ALU.mult, op1=ALU.add)), reads=[B_zt[zi], B_par, B_yt], writes=[B_yt])
                S.op("pool", (lambda e, zi=zi: e.tensor_tensor(out=cvo[zi][:], in0=yt[:], in1=cbt[zi][:], op=ALU.mult)), reads=[B_yt, B_cbt[zi]], writes=[B_cvo[zi]])
                S.op("pool", (lambda e, t=t, fc=fc, zi=zi: e.dma_start(out=mix_d[t, :, RFC + fc, :], in_=cvo[zi][:])), reads=[B_cvo[zi]], writes=[B_mix[t]], pool=st_pool)
        S.barrier()
        st.close()

    def phaseC2(l):
        last = (l == L - 1)
        st = ExitStack()
        xT = sb("d_xT", [128, FC, TS], F32, st)
        B_x = [Buf(f"d_x{fc}") for fc in range(FC)]
        act = sb("d_act", [128, FC, TS], BF16, st)
        B_act = Buf("d_act")
        GK = c.FG // 128
        hid = [sb(f"d_hid{i}", [128, GK, TS], BF16, st) for i in range(2)]
        B_hidc = [[Buf(f"d_hid{i}_{a}") for a in range(GK)] for i in range(2)]
        sqb = [sb(f"d_sq{i}", [128, TS], BF16, st) for i in range(2)]
        B_sq = [Buf(f"d_sq{i}") for i in range(2)]
        rstd = sb("d_rstd", [128, TS], F32, st)
        B_rstd = Buf("d_rstd")
        rl = [sb(f"d_rl{i}", [128, TS], F32, st) for i in range(2)]
        B_rl = [Buf(f"d_rl{i}") for i in range(2)]
        sig = [sb(f"d_sig{i}", [128, TS], F32, st) for i in range(2)]
        B_sig = [Buf(f"d_sig{i}") for i in range(2)]
        pTs = sb("d_pT", [128, c.PKC, TS], BF16, st)
        B_pTs = Buf("d_pT")
        plew = sb("d_plew", [128, c.PKC, D], BF16, st)
        B_plew = Buf("d_plew")
        YW = min(D, 2048)
        if last:
            ysb = sb("d_y", [128, YW], F32, st)
            B_ysb = Buf("d_y")
        rr = {"sq": 0, "rl": 0, "sig": 0, "hid": 0}

        S.bg_flush_to(("ple", l))
        S.op("sp", (lambda e: e.dma_start(out=plew[:], in_=wt_ple[l])), reads=[B_wtp[("ple", l)]], writes=[B_plew], pool=ld_pool)

        base = len(wpieces)
        B_src = None
        for t in range(NTILE):
            for oc in range(FC):
                wpieces.append((wt_out[l, oc], [FC, 128], ("out", l, oc)))
            for kind, g in ffn_order():
                if kind == "ff1":
                    for a in range(GK):
                        wpieces.append((wt_ff1[l, g * GK + a], [FC, 128], ("ff1", l, g * GK + a)))
                else:
                    for q in range(D // 512):
                        wpieces.append((wt_ff2[l, g, q], [GK, 512], ("ff2", l, g, q)))
            for oc in range(FC):
                wpieces.append((wt_gate[l, oc], [FC, 128], ("gate", l, oc)))
        wi = [base]

        def nextw():
            i = wi[0]
            wi[0] += 1
            return wget(i, B_src)

        def norm_to(kind_l, kind, to_act):
            bk = next_bank()
            for fc in range(FC):
                j = rr["sq"] % 2
                rr["sq"] += 1
                S.op("act", (lambda e, fc=fc, j=j: e.activation(out=sqb[j][:], in_=xT[:, fc, :], func=AF.Square)), reads=[B_x[fc]], writes=[B_sq[j]])
                S.op("pe", (lambda e, j=j, fc=fc, bk=bk: e.matmul(psb[bk][:], ones_bf[:], sqb[j][:], start=(fc == 0), stop=(fc == FC - 1))),
                     reads=[B_sq[j], B_cbf], writes=[B_ps[bk]])
            S.op("act", (lambda e, bk=bk: e.activation(out=rstd[:], in_=psb[bk][:], func=AF.Sqrt, scale=1.0 / D, bias=NORM_EPS)),
                 reads=[B_ps[bk]], writes=[B_rstd])
            S.op("dve", (lambda e: e.reciprocal(out=rstd[:], in_=rstd[:])), reads=[B_rstd], writes=[B_rstd])
            for fc in range(FC):
                if not to_act:
                    S.op("dve", (lambda e, fc=fc: e.scalar_tensor_tensor(out=xT[:, fc, :], in0=xT[:, fc, :], scalar=norm_col(kind_l, kind, fc), in1=rstd[:], op0=ALU.mult, op1=ALU.mult)),
                         reads=[B_x[fc], B_rstd, B_par], writes=[B_x[fc]])
                else:
                    S.op("dve", (lambda e, fc=fc: e.scalar_tensor_tensor(out=act[:, fc, :], in0=xT[:, fc, :], scalar=norm_col(kind_l, kind, fc), in1=rstd[:], op0=ALU.mult, op1=ALU.mult)),
                         reads=[B_x[fc], B_rstd, B_par], writes=[B_act])

        def bigmm(wap, bk, n_kc, rhs_tile, col0=None):
            def f(e):
                for kc in range(n_kc):
                    lhs = wap[:, kc, :] if col0 is None else wap[:, kc, col0:col0 + 128]
                    ins = e.matmul(psb[bk][:], lhs, rhs_tile[:, kc, :], start=(kc == 0), stop=(kc == n_kc - 1))
                return ins
            return f

        for t in range(NTILE):
            for f0 in range(0, FC, 8):
                nf = min(8, FC - f0)
                S.op("sp", (lambda e, t=t, f0=f0, nf=nf: e.dma_start(out=xT[:, f0:f0 + nf, :], in_=xT_d[t, :, f0:f0 + nf, :])),
                     reads=[B_xT[t]], writes=B_x[f0:f0 + nf], pool=ld_pool)
            S.op("sp", (lambda e, t=t: e.dma_start(out=act[:], in_=mix_d[t])), reads=[B_mix[t]], writes=[B_act], pool=ld_pool)
            S.op("sp", (lambda e, t=t: e.dma_start(out=pTs[:], in_=pT_d[l, t])), reads=[B_pT[l][t]], writes=[B_pTs], pool=ld_pool)
            for oc in range(FC):
                wap, bw = nextw()
                bk = next_bank()
                S.op("pe", bigmm(wap, bk, FC, act), reads=[bw, B_act], writes=[B_ps[bk]])
                S.op("dve", (lambda e, oc=oc, bk=bk: e.tensor_tensor(out=xT[:, oc, :], in0=psb[bk][:], in1=xT[:, oc, :], op=ALU.add)), reads=[B_ps[bk], B_x[oc]], writes=[B_x[oc]])
            norm_to(l, 1, True)
            for kind, g in ffn_order():
                hi = g % 2
                if kind == "ff1":
                    for a in range(GK):
                        wap, bw = nextw()
                        bk = next_bank()
                        S.op("pe", bigmm(wap, bk, FC, act), reads=[bw, B_act], writes=[B_ps[bk]])
                        ri = rr["rl"] % 2
                        rr["rl"] += 1
                        S.op("act", (lambda e, ri=ri, bk=bk: e.activation(out=rl[ri][:], in_=psb[bk][:], func=AF.Relu)), reads=[B_ps[bk]], writes=[B_rl[ri]])
                        S.op("pool", (lambda e, ri=ri, hi=hi, a=a: e.tensor_tensor(out=hid[hi][:, a, :], in0=rl[ri][:], in1=rl[ri][:], op=ALU.mult)), reads=[B_rl[ri]], writes=[B_hidc[hi][a]])
                else:
                    for q in range(D // 512):
                        wap, bw = nextw()
                        for a in range(4):
                            oc = q * 4 + a
                            bk = next_bank()
                            S.op("pe", bigmm(wap, bk, GK, hid[hi], col0=a * 128), reads=[bw] + B_hidc[hi], writes=[B_ps[bk]])
                            S.op("dve", (lambda e, oc=oc, bk=bk: e.tensor_tensor(out=xT[:, oc, :], in0=psb[bk][:], in1=xT[:, oc, :], op=ALU.add)), reads=[B_ps[bk], B_x[oc]], writes=[B_x[oc]])
            norm_to(l, 2, True)
            for oc in range(FC):
                wap, bw = nextw()
                bk = next_bank()
                S.op("pe", bigmm(wap, bk, FC, act), reads=[bw, B_act], writes=[B_ps[bk]])
                si = rr["sig"] % 2
                rr["sig"] += 1
                S.op("act", (lambda e, si=si, bk=bk: e.activation(out=sig[si][:], in_=psb[bk][:], func=AF.Sigmoid)), reads=[B_ps[bk]], writes=[B_sig[si]])
                bk2 = next_bank()
                S.op("pe", bigmm(plew, bk2, c.PKC, pTs, col0=oc * 128), reads=[B_plew, B_pTs], writes=[B_ps[bk2]])
                S.op("dve", (lambda e, si=si, bk2=bk2: e.tensor_tensor(out=sig[si][:], in0=psb[bk2][:], in1=sig[si][:], op=ALU.mult)), reads=[B_ps[bk2], B_sig[si]], writes=[B_sig[si]])
                S.op("pool", (lambda e, si=si, oc=oc: e.tensor_tensor(out=xT[:, oc, :], in0=xT[:, oc, :], in1=sig[si][:], op=ALU.add)), reads=[B_sig[si], B_x[oc]], writes=[B_x[oc]])
            if not last:
                S.op("pool", (lambda e, t=t: e.dma_start(out=xT_d[t], in_=xT[:])), reads=B_x, writes=[B_xT[t]], pool=st_pool)
            else:
                norm_to(L, 0, False)
                for tg in range(4):
                    for y0 in range(0, D, YW):
                        for f0 in range(y0 // 128, (y0 + YW) // 128, 4):
                            nf = min(4, FC - f0)
                            bk = next_bank()

                            def tr(e, f0=f0, nf=nf, bk=bk, tg=tg):
                                for a in range(nf):
                                    ins = e.transpose(out=psb[bk][:, a * 128:(a + 1) * 128], in_=xT[:, f0 + a, tg * 128:(tg + 1) * 128], identity=cst("ident"))
                                return ins
                            S.op("pe", tr, reads=B_x[f0:f0 + nf] + [B_consts], writes=[B_ps[bk]])
                            c0 = f0 * 128 - y0
                            if (f0 // 4) % 2 == 0:
                                S.op("act", (lambda e, c0=c0, nf=nf, bk=bk: e.activation(out=ysb[:, c0:c0 + nf * 128], in_=psb[bk][:, 0:nf * 128], func=AF.Copy)), reads=[B_ps[bk]], writes=[B_ysb])
                            else:
                                S.op("dve", (lambda e, c0=c0, nf=nf, bk=bk: e.tensor_copy(out=ysb[:, c0:c0 + nf * 128], in_=psb[bk][:, 0:nf * 128])), reads=[B_ps[bk]], writes=[B_ysb])
                        r0 = t * TS + tg * 128
                        o = S.op("pool", (lambda e, r0=r0, y0=y0: e.dma_start(out=y_out[r0:r0 + 128, y0:y0 + YW], in_=ysb[:])), reads=[B_ysb], pool=st_pool)
                        out_store_ops.append(o)
        assert wi[0] == len(wpieces)
        S.barrier()
        st.close()

    stop_after = cfg.stop_after
    phase0()
    phaseT()
    for l in range(L):
        if stop_after == "T":
            break
        layer_setup(l)
        S.bg_every = cfg.pace_a
        phaseA(l)
        if stop_after == f"A{l}":
            break
        S.bg_every = cfg.pace_c1
        phaseC1(l)
        if stop_after == f"C1{l}":
            break
        S.bg_every = cfg.pace_c2
        phaseC2(l)
        if stop_after == f"C2{l}":
            break

    if not out_store_ops:
        out_store_ops.extend(s.last for s in S.all_slots if s.last is not None)
    S.emit(out_store_ops)
    es.close()
    return nc


def host_layout(cfg, units, p_units, links, seg_pos, W):
    c = cfg
    L, FC, H, CFC = c.DEPTH, c.FC, c.H, c.CFC
    consts = make_consts()
    norms = np.concatenate(
        [np.stack([W["norm_mix"][l], W["norm_mlp"][l], W["norm_ple"][l]], 0) for l in range(L)] + [W["norm_final"][None]], 0
    )
    norms = np.ascontiguousarray(norms.reshape(3 * L + 1, FC, 128).transpose(2, 0, 1).reshape(128, (3 * L + 1) * FC))
    convw = np.ascontiguousarray(W["conv_w"].reshape(L, 3, CFC, 128).transpose(3, 0, 1, 2).reshape(128, L * 3 * CFC))
    dec = np.stack([W["ret_decay_fwd"], W["ret_decay_bwd"]], 1).reshape(1, L * 2 * H)
    dec = np.ascontiguousarray(np.broadcast_to(dec, (128, L * 2 * H)))
    in_maps = []
    for u in range(len(units)):
        cosT, sinT = rope_tables(seg_pos[u])
        in_maps.append({
            "x": units[u], "p": p_units[u],
            "w_in": W["w_in"], "w_out": W["w_out"], "w_ff1": W["w_ff1"], "w_ff2": W["w_ff2"],
            "w_gate": W["w_ple_gate"], "w_ple": W["w_ple_proj"],
            "norms": norms, "convw": convw, "dec": dec, "consts": consts,
            "link": np.full((128, 1), links[u], np.float32), "cosT": cosT, "sinT": sinT,
        })
    return in_maps


_PROG_CACHE = {}


def kernel(x_prompt, x_sample, p_prompt, p_sample, norm_mix, w_in, ret_decay_fwd, ret_decay_bwd,
           conv_w, w_out, norm_mlp, w_ff1, w_ff2, norm_ple, w_ple_gate, w_ple_proj, norm_final):
    f = lambda a: np.ascontiguousarray(np.asarray(a, dtype=np.float32))
    x_prompt, x_sample, p_prompt, p_sample = f(x_prompt), f(x_sample), f(p_prompt), f(p_sample)
    W = dict(norm_mix=f(norm_mix), w_in=f(w_in), ret_decay_fwd=f(ret_decay_fwd), ret_decay_bwd=f(ret_decay_bwd),
             conv_w=f(conv_w), w_out=f(w_out), norm_mlp=f(norm_mlp), w_ff1=f(w_ff1), w_ff2=f(w_ff2),
             norm_ple=f(norm_ple), w_ple_gate=f(w_ple_gate), w_ple_proj=f(w_ple_proj), norm_final=f(norm_final))
    B, SEQ, D = x_prompt.shape
    DB, DSEQ, _ = x_sample.shape
    L = W["w_in"].shape[0]
    NT = SEQ
    assert DSEQ * 2 == NT and DB % 2 == 0
    cfg = Cfg(D=D, NT=NT, DEPTH=L, PL=p_prompt.shape[-1], n_cores=8)
    units, p_units, links, seg_pos = [], [], [], []
    for b in range(B):
        units.append(x_prompt[b])
        p_units.append(np.ascontiguousarray(p_prompt[:, b]))
        links.append(1.0)
        seg_pos.append(np.arange(NT, dtype=np.float32))
    for b in range(0, DB, 2):
        units.append(np.ascontiguousarray(x_sample[b:b + 2].reshape(NT, D)))
        p_units.append(np.ascontiguousarray(p_sample[:, b:b + 2].reshape(L, NT, -1)))
        links.append(0.0)
        seg_pos.append(np.concatenate([np.arange(DSEQ), np.arange(DSEQ)]).astype(np.float32))
    n_units = len(units)
    assert n_units <= cfg.n_cores
    while len(units) < cfg.n_cores:
        units.append(units[0]); p_units.append(p_units[0]); links.append(links[0]); seg_pos.append(seg_pos[0])
    make_consts()
    key = (D, NT, L)
    if key not in _PROG_CACHE:
        _PROG_CACHE[key] = build_program(cfg)
    nc = _PROG_CACHE[key]
    in_maps = host_layout(cfg, units, p_units, links, seg_pos, W)
    res = run_bass_kernel_spmd(nc, in_maps, core_ids=list(range(cfg.n_cores)))
    ys = [np.asarray(res.results[u]["y"], dtype=np.float32) for u in range(n_units)]
    y_prompt = np.stack(ys[:B], 0)
    y_sample = np.stack(ys[B:], 0).reshape(DB, DSEQ, D)
    return (y_prompt, y_sample)
```

```python
from contextlib import ExitStack

import numpy as np

import concourse.bass as bass
import concourse.mybir as mybir
from concourse.bass_utils import run_bass_kernel_spmd

F32 = mybir.dt.float32
BF16 = mybir.dt.bfloat16
AF = mybir.ActivationFunctionType
ALU = mybir.AluOpType

NORM_EPS = 1e-6
ROPE_BASE = 10000.0


class Cfg:
    def __init__(self, D=4096, NT=4096, DEPTH=2, PL=256, n_cores=8, stop_after=None):
        self.stop_after = stop_after
        self.pace_a, self.pace_c1, self.pace_c2 = 6, 4, 3
        self.D = D
        self.NT = NT
        self.DEPTH = DEPTH
        self.PL = PL
        self.n_cores = n_cores
        self.FC = D // 128
        self.RW = D // 2
        self.CW = D - self.RW
        self.HD = 256
        self.H = self.RW // self.HD
        self.RFC = self.RW // 128
        self.CFC = self.CW // 128
        self.DFF = 4 * D
        self.INW = 4 * self.RW + 3 * self.CW
        self.TS = 512
        self.NTILE = NT // self.TS
        self.NCH = NT // 128
        self.HALF_CH = self.NCH // 2
        self.HALF_TILE = self.NTILE // 2
        self.FG = 1024
        self.NFG = self.DFF // self.FG
        self.PKC = PL // 128


class Buf:
    __slots__ = ("name", "w", "r")

    def __init__(self, name):
        self.name = name
        self.w = None
        self.r = []


class Op:
    __slots__ = ("eng", "fn", "deps", "need", "sem", "val", "is_dma")


class SemSlot:
    __slots__ = ("sem", "count", "last")

    def __init__(self, sem):
        self.sem = sem
        self.count = 0
        self.last = None


class SemPool:
    def __init__(self, slots):
        self.slots = slots
        self.i = 0

    def next(self):
        s = self.slots[self.i % len(self.slots)]
        self.i += 1
        return s


ENGS = ("pe", "act", "dve", "pool", "sp")


class Sched:
    def __init__(self, nc, es):
        self.nc = nc
        self.es = es
        self.ops = []
        self.eng_sem = {e: es.enter_context(nc.semaphore("sem_" + e)) for e in ("pe", "act", "dve", "pool")}
        self.last_on_eng = {e: None for e in ENGS}
        self.all_slots = []
        self.bg_queue = []
        self.bg_next = 0
        self.bg_done = set()
        self.bg_every = 0
        self.pool_ticks = 0
        self.in_bg = False

    def sem_pool(self, name, n):
        slots = [SemSlot(self.es.enter_context(self.nc.semaphore(f"{name}{i}"))) for i in range(n)]
        self.all_slots.extend(slots)
        return SemPool(slots)

    def op(self, eng, fn, reads=(), writes=(), pool=None, extra_deps=()):
        o = Op()
        o.eng = eng
        o.fn = fn
        o.need = False
        o.sem = None
        o.val = None
        o.is_dma = pool is not None
        deps = set(extra_deps)
        for b in reads:
            if b.w is not None:
                deps.add(b.w)
        for b in writes:
            if b.w is not None:
                deps.add(b.w)
            deps.update(b.r)
        for b in reads:
            b.r.append(o)
        for b in writes:
            b.w = o
            b.r = []
        if pool is not None:
            slot = pool.next()
            if slot.last is not None:
                deps.add(slot.last)
            slot.last = o
            slot.count += 16
            o.sem = slot.sem
            o.val = slot.count
        deps.discard(o)
        o.deps = deps
        for d in deps:
            d.need = True
        self.ops.append(o)
        self.last_on_eng[eng] = o
        if eng == "pool" and not self.in_bg and self.bg_every > 0:
            self.pool_ticks += 1
            if self.pool_ticks % self.bg_every == 0:
                self.bg_release(1)
        return o

    def bg_release(self, n):
        self.in_bg = True
        while n > 0 and self.bg_next < len(self.bg_queue):
            key, fn = self.bg_queue[self.bg_next]
            self.bg_next += 1
            fn()
            self.bg_done.add(key)
            n -= 1
        self.in_bg = False

    def bg_flush_to(self, key):
        while key not in self.bg_done:
            assert self.bg_next < len(self.bg_queue), key
            self.bg_release(1)

    def barrier(self):
        lasts = [o for o in self.last_on_eng.values() if o is not None]
        lasts += [s.last for s in self.all_slots if s.last is not None]
        for e in ENGS:
            self.op(e, None, extra_deps=[o for o in lasts])

    def emit(self, final_wait_ops):
        nc = self.nc
        cnt = {e: 0 for e in self.eng_sem}
        for o in self.ops:
            if o.is_dma:
                continue
            if o.need and o.fn is not None:
                cnt[o.eng] += 1
                o.sem = self.eng_sem[o.eng]
                o.val = cnt[o.eng]
        def resolve(d, out, seen):
            if d in seen:
                return
            seen.add(d)
            if d.fn is None:
                for dd in d.deps:
                    resolve(dd, out, seen)
            else:
                out.append(d)

        per_eng = {e: [] for e in ENGS}
        for o in self.ops:
            per_eng[o.eng].append(o)

        def emit_engine(ename, eng):
            waited = {}
            for o in per_eng[ename]:
                flat = []
                seen = set()
                for d in o.deps:
                    resolve(d, flat, seen)
                for d in flat:
                    if ename == "pe" and d.eng == "pe" and not d.is_dma:
                        continue
                    key = id(d.sem)
                    if waited.get(key, 0) >= d.val:
                        continue
                    eng.wait_ge(d.sem, d.val)
                    waited[key] = d.val
                if o.fn is None:
                    continue
                inst = o.fn(eng)
                if o.is_dma:
                    inst.then_inc(o.sem, 16)
                elif o.need:
                    inst.then_inc(o.sem, 1)
            if ename == "sp":
                for d in final_wait_ops:
                    key = id(d.sem)
                    if waited.get(key, 0) >= d.val:
                        continue
                    eng.wait_ge(d.sem, d.val)
                    waited[key] = d.val

        with nc.Block() as block:
            @block.tensor
            def _(e):
                emit_engine("pe", e)

            @block.scalar
            def _(e):
                emit_engine("act", e)

            @block.vector
            def _(e):
                emit_engine("dve", e)

            @block.gpsimd
            def _(e):
                emit_engine("pool", e)

            @block.sync
            def _(e):
                emit_engine("sp", e)


CONST_COLS = {}


def make_consts():
    i = np.arange(128, dtype=np.float32)
    parts = []
    off = 0

    def add(name, arr):
        nonlocal off
        arr = np.asarray(arr, np.float32)
        CONST_COLS[name] = (off, arr.shape[1])
        off += arr.shape[1]
        parts.append(arr)

    diff = i[None, :] - i[:, None]
    add("ident", np.eye(128))
    add("ones", np.ones((128, 128)))
    add("row_f", np.broadcast_to(i[None, :] + 1.0, (128, 128)))
    add("row_b", np.broadcast_to(128.0 - i[None, :], (128, 128)))
    add("dpos", np.maximum(diff, 0.0))
    add("dneg", np.maximum(-diff, 0.0))
    add("mf", (diff >= 0).astype(np.float32))
    add("mb", (diff < 0).astype(np.float32))
    add("col_f", (127.0 - i)[:, None])
    add("col_b", i[:, None])
    return np.ascontiguousarray(np.concatenate(parts, axis=1))


def rope_tables(pos):
    half = 128
    inv_freq = (ROPE_BASE ** (-np.arange(half, dtype=np.float32) / half)).astype(np.float32)
    ang = (pos.astype(np.float32)[None, :] * inv_freq[:, None]).astype(np.float32)
    return np.ascontiguousarray(np.cos(ang).astype(np.float32)), np.ascontiguousarray(np.sin(ang).astype(np.float32))


def build_program(cfg, debug=False):
    c = cfg
    nc = bass.Bass("TRN2", target_bir_lowering=False)
    D, NT, FC, TS, NTILE, NCH, H = c.D, c.NT, c.FC, c.TS, c.NTILE, c.NCH, c.H
    RFC, CFC, L = c.RFC, c.CFC, c.DEPTH
    NCONST = sum(v[1] for v in CONST_COLS.values())

    def din(name, shape, dt=F32):
        return nc.dram_tensor(name, list(shape), dt, kind="ExternalInput").ap()

    def dscr(name, shape, dt):
        kind = "ExternalOutput" if debug else "Internal"
        return nc.dram_tensor(name, list(shape), dt, kind=kind).ap()

    x_in = din("x", [NT, D])
    p_in = din("p", [L, NT, c.PL])
    w_in_d = din("w_in", [L, D, c.INW])
    w_out_d = din("w_out", [L, D, D])
    w_ff1_d = din("w_ff1", [L, D, c.DFF])
    w_ff2_d = din("w_ff2", [L, c.DFF, D])
    w_gate_d = din("w_gate", [L, D, D])
    w_ple_d = din("w_ple", [L, c.PL, D])
    norms_d = din("norms", [128, (3 * L + 1) * FC])
    convw_d = din("convw", [128, L * 3 * CFC])
    dec_d = din("dec", [128, L * 2 * H])
    consts_d = din("consts", [128, NCONST])
    link_d = din("link", [128, 1])
    cos_d = din("cosT", [128, NT])
    sin_d = din("sinT", [128, NT])
    y_out = nc.dram_tensor("y", [NT, D], F32, kind="ExternalOutput").ap()

    wt_in = dscr("wt_in", [L, c.INW // 128, 128, FC, 128], BF16)
    wt_out = dscr("wt_out", [L, FC, 128, FC, 128], BF16)
    wt_ff1 = dscr("wt_ff1", [L, c.DFF // 128, 128, FC, 128], BF16)
    wt_gate = dscr("wt_gate", [L, FC, 128, FC, 128], BF16)
    wt_ff2 = dscr("wt_ff2", [L, c.NFG, D // 512, 128, c.FG // 128, 512], BF16)
    wt_ple = dscr("wt_ple", [L, 128, c.PKC, D], BF16)
    xT_d = dscr("xT_s", [NTILE, 128, FC, TS], F32)
    pT_d = dscr("pT_s", [L, NTILE, 128, c.PKC, TS], BF16)
    qc_d = dscr("qc_s", [NCH, 128, H, 2, 128], BF16)
    kc_d = dscr("kc_s", [NCH, 128, H, 2, 128], BF16)
    kkf_d = dscr("kkf_s", [NCH, H, 128, 256], BF16)
    v_d = dscr("v_s", [NCH, H, 128, 256], BF16)
    sb_d = dscr("sb_s", [NCH, H, 128, 2, 256], BF16)
    sg_d = dscr("sg_s", [NCH, 128, RFC, 128], BF16)
    z_d = dscr("z_s", [NTILE, 128, CFC, TS], F32)
    cb_d = dscr("cb_s", [NTILE, 128, CFC, TS], F32)
    mix_d = dscr("mix_s", [NTILE, 128, FC, TS], BF16)

    es = ExitStack()
    S = Sched(nc, es)

    uid = [0]

    def sb(name, shape, dt, stack=None):
        uid[0] += 1
        return (stack or es).enter_context(nc.sbuf_tensor(f"{name}_u{uid[0]}", list(shape), dt))

    def ps(name, shape, dt, stack=None):
        return (stack or es).enter_context(nc.psum_tensor(name, list(shape), dt))

    consts = sb("consts_sb", [128, NCONST], F32)
    B_consts = Buf("consts")
    ident_bf = sb("ident_bf", [128, 128], BF16)
    ones_bf = sb("ones_bf", [128, 128], BF16)
    B_cbf = Buf("cbf")
    norms_sb = sb("norms_sb", [128, (3 * L + 1) * FC], F32)
    convw_sb = sb("convw_sb", [128, L * 3 * CFC], F32)
    dec_sb = sb("dec_sb", [128, L * 2 * H], F32)
    link_sb = sb("link_sb", [128, 1], F32)
    B_par = Buf("params")
    NW = 4
    wslots = [sb(f"wslot{i}", [128, 4096], BF16) for i in range(NW)]
    B_w = [Buf(f"wslot{i}") for i in range(NW)]
    wsem = S.sem_pool("wsem", NW)

    def cst(name, cols=None):
        off, n = CONST_COLS[name]
        return consts[:, off:off + n]

    psb = [ps(f"psb{i}", [128, 512], F32) for i in range(8)]
    B_ps = [Buf(f"ps{i}") for i in range(8)]
    ps_rr = [0]

    def next_bank():
        i = ps_rr[0] % 8
        ps_rr[0] += 1
        return i

    ld_pool = S.sem_pool("ld", 6)
    st_pool = S.sem_pool("st", 8)
    misc_pool = S.sem_pool("misc", 2)

    out_store_ops = []

    S.op("sp", lambda e: e.dma_start(out=consts[:], in_=consts_d[:, :]), writes=[B_consts], pool=misc_pool)
    S.op("sp", lambda e: e.dma_start(out=norms_sb[:], in_=norms_d[:, :]), writes=[B_par], pool=misc_pool)
    S.op("sp", lambda e: e.dma_start(out=convw_sb[:], in_=convw_d[:, :]), writes=[B_par], pool=misc_pool)
    S.op("sp", lambda e: e.dma_start(out=dec_sb[:], in_=dec_d[:, :]), writes=[B_par], pool=misc_pool)
    S.op("sp", lambda e: e.dma_start(out=link_sb[:], in_=link_d[:, :]), writes=[B_par], pool=misc_pool)
    S.op("dve", lambda e: e.tensor_copy(out=ident_bf[:], in_=cst("ident")), reads=[B_consts], writes=[B_cbf])
    S.op("dve", lambda e: e.tensor_copy(out=ones_bf[:], in_=cst("ones")), reads=[B_consts], writes=[B_cbf])

    wpieces = []
    wissued = [0]

    def wpiece_ap(slot, shape):
        n = int(np.prod(shape))
        ap = wslots[slot][:, 0:n]
        if len(shape) == 2:
            return ap.rearrange("p (a b) -> p a b", a=shape[0])
        return ap

    def wget(i, B_src=None):
        while wissued[0] < min(len(wpieces), i + NW):
            j = wissued[0]
            src, shape, key = wpieces[j]
            S.bg_flush_to(key)
            slot = j % NW
            dst = wpiece_ap(slot, shape)
            S.op("sp", (lambda e, dst=dst, src=src: e.dma_start(out=dst, in_=src)),
                 reads=[B_wtp[key]], writes=[B_w[slot]], pool=SemPool([wsem.slots[slot]]))
            wissued[0] += 1
        slot = i % NW
        return wpiece_ap(slot, wpieces[i][1]), B_w[slot]

    B_wtp = {}
    cv_pool = S.sem_pool("cv", 12)

    def phase0():
        def add(key, dst, srcap):
            B_wtp[key] = Buf("wt" + str(key))

            def fn(key=key, dst=dst, srcap=srcap):
                S.op("pool", (lambda e: e.dma_start(out=dst, in_=srcap)), writes=[B_wtp[key]], pool=cv_pool)
            S.bg_queue.append((key, fn))

        GK = c.FG // 128
        for l in range(L):
            v_in = w_in_d[l].rearrange("(kc p) n -> p kc n", p=128)
            v_out = w_out_d[l].rearrange("(kc p) n -> p kc n", p=128)
            v_ff1 = w_ff1_d[l].rearrange("(kc p) n -> p kc n", p=128)
            v_ff2 = w_ff2_d[l].rearrange("(kc p) n -> p kc n", p=128)
            v_gate = w_gate_d[l].rearrange("(kc p) n -> p kc n", p=128)
            v_ple = w_ple_d[l].rearrange("(kc p) n -> p kc n", p=128)
            for oc in in_order:
                add(("in", l, oc), wt_in[l, oc], v_in[:, :, oc * 128:(oc + 1) * 128])
            add(("ple", l), wt_ple[l], v_ple)
            for oc in range(FC):
                add(("out", l, oc), wt_out[l, oc], v_out[:, :, oc * 128:(oc + 1) * 128])
            for kind, g in ffn_order():
                if kind == "ff1":
                    for a in range(GK):
                        oc = g * GK + a
                        add(("ff1", l, oc), wt_ff1[l, oc], v_ff1[:, :, oc * 128:(oc + 1) * 128])
                else:
                    for q in range(D // 512):
                        add(("ff2", l, g, q), wt_ff2[l, g, q], v_ff2[:, g * GK:(g + 1) * GK, q * 512:(q + 1) * 512])
            for oc in range(FC):
                add(("gate", l, oc), wt_gate[l, oc], v_gate[:, :, oc * 128:(oc + 1) * 128])
        S.bg_flush_to(("in", 0, in_order[-1]))

    def ffn_order():
        seq = []
        for g in range(c.NFG):
            seq.append(("ff1", g))
            if g >= 1:
                seq.append(("ff2", g - 1))
        seq.append(("ff2", c.NFG - 1))
        return seq

    in_order = []
    for h in range(H):
        for o0 in (0, RFC, 2 * RFC):
            for half in range(2):
                in_order.append(o0 + 2 * h + half)
    for fc in range(RFC):
        in_order.append(3 * RFC + fc)
    for fc in range(CFC):
        in_order.append(4 * RFC + fc)
        in_order.append(4 * RFC + CFC + fc)
        in_order.append(4 * RFC + 2 * CFC + fc)

    B_xT = [Buf(f"xT{t}") for t in range(NTILE)]
    B_pT = [[Buf(f"pT{l}_{t}") for t in range(NTILE)] for l in range(L)]

    def phaseT():
        st = ExitStack()
        xin = [sb(f"tx{i}", [128, D], F32, st) for i in range(2)]
        B_xin = [Buf(f"tx{i}") for i in range(2)]
        xTs = sb("txT", [128, FC, TS], F32, st)
        B_xTs = Buf("txT")
        pin = [sb(f"tp{i}", [128, c.PL], F32, st) for i in range(2)]
        B_pin = [Buf(f"tp{i}") for i in range(2)]
        pTs = [sb(f"tpT{i}", [128, c.PKC, TS], BF16, st) for i in range(2)]
        B_pTs = [Buf(f"tpT{i}") for i in range(2)]
        k = 0
        ev = 0
        for t in range(NTILE):
            for tg in range(4):
                i = k % 2
                k += 1
                r0 = t * TS + tg * 128
                S.op("sp", (lambda e, i=i, r0=r0: e.dma_start(out=xin[i][:], in_=x_in[r0:r0 + 128, :])),
                     writes=[B_xin[i]], pool=ld_pool)
                for f0 in range(0, FC, 4):
                    nf = min(4, FC - f0)
                    bk = next_bank()

                    def tr(e, i=i, f0=f0, nf=nf, bk=bk):
                        for a in range(nf):
                            ins = e.transpose(out=psb[bk][:, a * 128:(a + 1) * 128], in_=xin[i][:, (f0 + a) * 128:(f0 + a + 1) * 128], identity=cst("ident"))
                        return ins
                    S.op("pe", tr, reads=[B_xin[i], B_consts], writes=[B_ps[bk]])
                    outv = xTs[:, f0:f0 + nf, tg * 128:(tg + 1) * 128]
                    inv = psb[bk][:, 0:nf * 128].rearrange("p (a b) -> p a b", a=nf)
                    if ev % 2 == 0:
                        S.op("dve", (lambda e, outv=outv, inv=inv: e.tensor_copy(out=outv, in_=inv)), reads=[B_ps[bk]], writes=[B_xTs])
                    else:
                        S.op("act", (lambda e, outv=outv, inv=inv: e.activation(out=outv, in_=inv, func=AF.Copy)), reads=[B_ps[bk]], writes=[B_xTs])
                    ev += 1
            S.op("pool", (lambda e, t=t: e.dma_start(out=xT_d[t], in_=xTs[:])), reads=[B_xTs], writes=[B_xT[t]], pool=st_pool)
        for l in range(L):
            for t in range(NTILE):
                j = (l * NTILE + t) % 2
                for tg in range(4):
                    i = k % 2
                    k += 1
                    r0 = t * TS + tg * 128
                    S.op("sp", (lambda e, i=i, r0=r0, l=l: e.dma_start(out=pin[i][:], in_=p_in[l, r0:r0 + 128, :])),
                         writes=[B_pin[i]], pool=ld_pool)
                    bk = next_bank()

                    def tr(e, i=i, bk=bk):
                        for a in range(c.PKC):
                            ins = e.transpose(out=psb[bk][:, a * 128:(a + 1) * 128], in_=pin[i][:, a * 128:(a + 1) * 128], identity=cst("ident"))
                        return ins
                    S.op("pe", tr, reads=[B_pin[i], B_consts], writes=[B_ps[bk]])
                    outv = pTs[j][:, :, tg * 128:(tg + 1) * 128]
                    inv = psb[bk][:, 0:c.PKC * 128].rearrange("p (a b) -> p a b", a=c.PKC)
                    S.op("dve", (lambda e, outv=outv, inv=inv: e.tensor_copy(out=outv, in_=inv)), reads=[B_ps[bk]], writes=[B_pTs[j]])
                S.op("pool", (lambda e, l=l, t=t, j=j: e.dma_start(out=pT_d[l, t], in_=pTs[j][:])), reads=[B_pTs[j]], writes=[B_pT[l][t]], pool=st_pool)
        S.barrier()
        st.close()

    def norm_col(l, kind, fc):
        idx = (l * 3 + kind) if l < L else 3 * L
        o = idx * FC + fc
        return norms_sb[:, o:o + 1]

    qdf = sb("qdf", [128, H, 128], F32)
    qdb = sb("qdb", [128, H, 128], F32)
    Mt = sb("Mt", [128, H, 128], F32)
    kdf = sb("kdf", [128, H], F32)
    kdb = sb("kdb", [128, H], F32)
    cdf = sb("cdf", [128, H], F32)
    cdb = sb("cdb", [128, H], F32)
    lgf = sb("lgf", [128, H], F32)
    lgb = sb("lgb", [128, H], F32)
    e1 = sb("e1t", [128, 128], F32)
    e2 = sb("e2t", [128, 128], F32)
    B_dec = Buf("dec")
    zedge = sb("zedge", [128, CFC, NTILE, 2], F32)
    B_zedge = Buf("zedge")

    def layer_setup(l):
        o = l * 2 * H
        A = lambda fn, **kw: S.op("act", fn, reads=[B_par, B_consts, B_dec], writes=[B_dec])
        Dv = lambda fn, **kw: S.op("dve", fn, reads=[B_par, B_consts, B_dec], writes=[B_dec])
        A(lambda e: e.activation(out=lgf[:], in_=dec_sb[:, o:o + H], func=AF.Exp))
        A(lambda e: e.activation(out=lgb[:], in_=dec_sb[:, o + H:o + 2 * H], func=AF.Exp))
        Dv(lambda e: e.tensor_scalar(out=lgf[:], in0=lgf[:], scalar1=-1.0, scalar2=None, op0=ALU.mult))
        Dv(lambda e: e.tensor_scalar(out=lgb[:], in0=lgb[:], scalar1=-1.0, scalar2=None, op0=ALU.mult))
        A(lambda e: e.activation(out=cdf[:], in_=lgf[:], func=AF.Exp, scale=128.0))
        A(lambda e: e.activation(out=cdb[:], in_=lgb[:], func=AF.Exp, scale=128.0))
        for h in range(H):
            A(lambda e, h=h: e.activation(out=qdf[:, h, :], in_=cst("row_f"), func=AF.Exp, scale=lgf[:, h:h + 1]))
            A(lambda e, h=h: e.activation(out=qdb[:, h, :], in_=cst("row_b"), func=AF.Exp, scale=lgb[:, h:h + 1]))
            A(lambda e, h=h: e.activation(out=kdf[:, h:h + 1], in_=cst("col_f"), func=AF.Exp, scale=lgf[:, h:h + 1]))
            A(lambda e, h=h: e.activation(out=kdb[:, h:h + 1], in_=cst("col_b"), func=AF.Exp, scale=lgb[:, h:h + 1]))
            A(lambda e, h=h: e.activation(out=e1[:], in_=cst("dpos"), func=AF.Exp, scale=lgf[:, h:h + 1]))
            A(lambda e, h=h: e.activation(out=e2[:], in_=cst("dneg"), func=AF.Exp, scale=lgb[:, h:h + 1]))
            Dv(lambda e: e.tensor_tensor(out=e1[:], in0=e1[:], in1=cst("mf"), op=ALU.mult))
            Dv(lambda e: e.tensor_tensor(out=e2[:], in0=e2[:], in1=cst("mb"), op=ALU.mult))
            Dv(lambda e, h=h: e.tensor_tensor(out=Mt[:, h, :], in0=e1[:], in1=e2[:], op=ALU.add))

    B_qc = [[Buf(f"qc_{g}_{h}") for h in range(H)] for g in range(NCH)]
    B_kc = [[Buf(f"kc_{g}_{h}") for h in range(H)] for g in range(NCH)]
    B_kkf = [[Buf(f"kkf_{g}_{h}") for h in range(H)] for g in range(NCH)]
    B_v = [[Buf(f"v_{g}_{h}") for h in range(H)] for g in range(NCH)]
    B_sbd = [[Buf(f"sb_{g}_{h}") for h in range(H)] for g in range(NCH)]
    B_sg = [Buf(f"sg_{g}") for g in range(NCH)]
    B_z = [Buf(f"z_{t}") for t in range(NTILE)]
    B_cb = [Buf(f"cb_{t}") for t in range(NTILE)]
    B_mix = [Buf(f"mix_{t}") for t in range(NTILE)]

    def phaseA(l):
        st = ExitStack()
        XP = 8 if FC >= 8 else FC
        NXP = FC // XP
        xp = [sb(f"a_xp{i}", [128, XP, TS], F32, st) for i in range(2)]
        B_xp = [Buf(f"a_xp{i}") for i in range(2)]
        hT = sb("a_hT", [128, FC, TS], BF16, st)
        B_hT = Buf("a_hT")
        sqb = [sb(f"a_sq{i}", [128, TS], BF16, st) for i in range(2)]
        B_sq = [Buf(f"a_sq{i}") for i in range(2)]
        rstd = sb("a_rstd", [128, TS], F32, st)
        B_rstd = Buf("a_rstd")
        cs = [sb(f"a_cs{i}", [128, 2, TS], F32, st) for i in range(2)]
        B_cs = [Buf(f"a_cs{i}") for i in range(2)]
        AB = [sb(f"a_AB{i}", [128, 2, TS], F32, st) for i in range(2)]
        B_AB = [Buf(f"a_AB{i}") for i in range(2)]
        t12 = sb("a_t12", [128, 2, TS], F32, st)
        B_t12 = Buf("a_t12")
        t34 = sb("a_t34", [128, 2, TS], F32, st)
        B_t34 = Buf("a_t34")
        qst = [sb(f"a_q{i}", [128, 2, TS], BF16, st) for i in range(2)]
        B_qst = [Buf(f"a_q{i}") for i in range(2)]
        kTh = [sb(f"a_kT{i}", [128, 2, TS], BF16, st) for i in range(2)]
        B_kTh = [Buf(f"a_kT{i}") for i in range(2)]
        vTh = [sb(f"a_vT{i}", [128, 2, TS], BF16, st) for i in range(2)]
        B_vTh = [Buf(f"a_vT{i}") for i in range(2)]
        tm = [sb(f"a_tm{i}", [128, 4, 3, 256], BF16, st) for i in range(2)]
        B_tm = [Buf(f"a_tm{i}") for i in range(2)]
        S32 = sb("a_S32", [128, H, 2, 256], F32, st)
        B_S32 = [Buf(f"a_S32_{h}") for h in range(H)]
        Sbf = [sb(f"a_Sbf{i}", [128, 2, 256], BF16, st) for i in range(2)]
        B_Sbf = [Buf(f"a_Sbf{i}") for i in range(2)]
        sgs = [sb(f"a_sg{i}", [128, TS], BF16, st) for i in range(2)]
        B_sgs = [Buf(f"a_sg{i}") for i in range(2)]
        zst = [sb(f"a_z{i}", [128, TS], F32, st) for i in range(2)]
        B_zst = [Buf(f"a_z{i}") for i in range(2)]
        cbst = [sb(f"a_cb{i}", [128, TS], F32, st) for i in range(2)]
        B_cbst = [Buf(f"a_cb{i}") for i in range(2)]
        rr = {"xp": 0, "sq": 0, "cs": 0, "AB": 0, "q": 0, "kT": 0, "vT": 0, "tm": 0, "Sbf": 0, "sg": 0, "z": 0, "cb": 0}

        for h in range(H):
            S.op("pool", (lambda e, h=h: e.memset(S32[:, h, :, :], 0.0)), writes=[B_S32[h]])

        base = len(wpieces)
        B_src = None
        for t in range(NTILE):
            for oc in in_order:
                wpieces.append((wt_in[l, oc], [FC, 128], ("in", l, oc)))
        wi = [base]

        def proj(bk):
            i = wi[0]
            wi[0] += 1
            wap, bw = wget(i, B_src)

            def f(e, wap=wap, bk=bk):
                for kc in range(FC):
                    ins = e.matmul(psb[bk][:], wap[:, kc, :], hT[:, kc, :], start=(kc == 0), stop=(kc == FC - 1))
                return ins
            S.op("pe", f, reads=[bw, B_hT], writes=[B_ps[bk]])

        def rotary(ab, ci, out1, out2, Bout):
            A_ = AB[ab][:, 0, :]
            B_ = AB[ab][:, 1, :]
            cos_ = cs[ci][:, 0, :]
            sin_ = cs[ci][:, 1, :]
            S.op("dve", lambda e: e.tensor_tensor(out=t12[:, 0, :], in0=A_, in1=cos_, op=ALU.mult), reads=[B_AB[ab], B_cs[ci]], writes=[B_t12])
            S.op("dve", lambda e: e.tensor_tensor(out=t12[:, 1, :], in0=B_, in1=sin_, op=ALU.mult), reads=[B_AB[ab], B_cs[ci]], writes=[B_t12])
            S.op("dve", lambda e: e.tensor_tensor(out=out1, in0=t12[:, 0, :], in1=t12[:, 1, :], op=ALU.subtract), reads=[B_t12], writes=[Bout])
            S.op("pool", lambda e: e.tensor_tensor(out=t34[:, 0, :], in0=B_, in1=cos_, op=ALU.mult), reads=[B_AB[ab], B_cs[ci]], writes=[B_t34])
            S.op("pool", lambda e: e.tensor_tensor(out=t34[:, 1, :], in0=A_, in1=sin_, op=ALU.mult), reads=[B_AB[ab], B_cs[ci]], writes=[B_t34])
            S.op("pool", lambda e: e.tensor_tensor(out=out2, in0=t34[:, 0, :], in1=t34[:, 1, :], op=ALU.add), reads=[B_t34], writes=[Bout])

        for t in reversed(range(NTILE)):
            tok0 = t * TS
            g0 = t * 4
            ci = rr["cs"] % 2
            rr["cs"] += 1
            S.op("sp", (lambda e, ci=ci, tok0=tok0: e.dma_start(out=cs[ci][:, 0, :], in_=cos_d[:, tok0:tok0 + TS])), writes=[B_cs[ci]], pool=ld_pool)
            S.op("sp", (lambda e, ci=ci, tok0=tok0: e.dma_start(out=cs[ci][:, 1, :], in_=sin_d[:, tok0:tok0 + TS])), writes=[B_cs[ci]], pool=ld_pool)
            bk_n = next_bank()
            cntm = 0
            for pi in range(NXP):
                i = rr["xp"] % 2
                rr["xp"] += 1
                S.op("sp", (lambda e, i=i, t=t, pi=pi: e.dma_start(out=xp[i][:], in_=xT_d[t, :, pi * XP:(pi + 1) * XP, :])),
                     reads=[B_xT[t]], writes=[B_xp[i]], pool=ld_pool)
                for a in range(XP):
                    j = rr["sq"] % 2
                    rr["sq"] += 1
                    S.op("act", (lambda e, i=i, a=a, j=j: e.activation(out=sqb[j][:], in_=xp[i][:, a, :], func=AF.Square)), reads=[B_xp[i]], writes=[B_sq[j]])
                    first = (cntm == 0)
                    last = (cntm == FC - 1)
                    cntm += 1
                    S.op("pe", (lambda e, j=j, first=first, last=last, bk=bk_n: e.matmul(psb[bk][:], ones_bf[:], sqb[j][:], start=first, stop=last)),
                         reads=[B_sq[j], B_cbf], writes=[B_ps[bk_n]])
            S.op("act", (lambda e, bk=bk_n: e.activation(out=rstd[:], in_=psb[bk][:], func=AF.Sqrt, scale=1.0 / D, bias=NORM_EPS)),
                 reads=[B_ps[bk_n]], writes=[B_rstd])
            S.op("dve", (lambda e: e.reciprocal(out=rstd[:], in_=rstd[:])), reads=[B_rstd], writes=[B_rstd])
            for pi in range(NXP):
                i = rr["xp"] % 2
                rr["xp"] += 1
                S.op("sp", (lambda e, i=i, t=t, pi=pi: e.dma_start(out=xp[i][:], in_=xT_d[t, :, pi * XP:(pi + 1) * XP, :])),
                     reads=[B_xT[t]], writes=[B_xp[i]], pool=ld_pool)
                for a in range(XP):
                    fc = pi * XP + a
                    S.op("dve", (lambda e, i=i, a=a, fc=fc: e.scalar_tensor_tensor(out=hT[:, fc, :], in0=xp[i][:, a, :], scalar=norm_col(l, 0, fc), in1=rstd[:], op0=ALU.mult, op1=ALU.mult)),
                         reads=[B_xp[i], B_rstd, B_par], writes=[B_hT])
            def post1(h, ki, vi):
                ti = h % 2
                for cc_ in range(4):
                    bk = next_bank()
                    pv = psb[bk][:].bitcast(BF16)

                    def trk(e, cc_=cc_, ki=ki, vi=vi, pv=pv):
                        for half in range(2):
                            e.transpose(out=pv[:, half * 128:(half + 1) * 128], in_=kTh[ki][:, half, cc_ * 128:(cc_ + 1) * 128], identity=ident_bf[:])
                        for half in range(2):
                            ins = e.transpose(out=pv[:, 256 + half * 128:256 + (half + 1) * 128], in_=vTh[vi][:, half, cc_ * 128:(cc_ + 1) * 128], identity=ident_bf[:])
                        return ins
                    S.op("pe", trk, reads=[B_kTh[ki], B_vTh[vi], B_cbf], writes=[B_ps[bk]])
                    S.op("dve", (lambda e, ti=ti, cc_=cc_, pv=pv, h=h: e.tensor_scalar(out=tm[ti][:, cc_, 0, :], in0=pv[:, 0:256], scalar1=kdf[:, h:h + 1], scalar2=None, op0=ALU.mult)),
                         reads=[B_ps[bk], B_dec], writes=[B_tm[ti]])
                    S.op("dve", (lambda e, ti=ti, cc_=cc_, pv=pv, h=h: e.tensor_scalar(out=tm[ti][:, cc_, 1, :], in0=pv[:, 0:256], scalar1=kdb[:, h:h + 1], scalar2=None, op0=ALU.mult)),
                         reads=[B_ps[bk], B_dec], writes=[B_tm[ti]])
                    S.op("act", (lambda e, ti=ti, cc_=cc_, pv=pv: e.activation(out=tm[ti][:, cc_, 2, :], in_=pv[:, 256:512], func=AF.Copy)),
                         reads=[B_ps[bk]], writes=[B_tm[ti]])
                S.op("pool", (lambda e, ti=ti, g0=g0, h=h: e.dma_start(out=kkf_d[g0:g0 + 4, h].rearrange("c p d -> p c d"), in_=tm[ti][:, :, 0, :])),
                     reads=[B_tm[ti]], writes=[B_kkf[g0 + a][h] for a in range(4)], pool=st_pool)
                S.op("pool", (lambda e, ti=ti, g0=g0, h=h: e.dma_start(out=v_d[g0:g0 + 4, h].rearrange("c p d -> p c d"), in_=tm[ti][:, :, 2, :])),
                     reads=[B_tm[ti]], writes=[B_v[g0 + a][h] for a in range(4)], pool=st_pool)
            def post2(h):
                ti = h % 2
                for cc_ in reversed(range(4)):
                    g = g0 + cc_
                    si = rr["Sbf"] % 2
                    rr["Sbf"] += 1
                    if g == c.HALF_CH - 1:
                        S.op("dve", (lambda e, h=h: e.tensor_scalar(out=S32[:, h, :, :], in0=S32[:, h, :, :], scalar1=link_sb[:, 0:1], scalar2=None, op0=ALU.mult)),
                             reads=[B_S32[h], B_par], writes=[B_S32[h]])
                    S.op("act", (lambda e, si=si, h=h: e.activation(out=Sbf[si][:], in_=S32[:, h, :, :], func=AF.Copy)),
                         reads=[B_S32[h]], writes=[B_Sbf[si]])
                    S.op("pool", (lambda e, si=si, g=g, h=h: e.dma_start(out=sb_d[g, h], in_=Sbf[si][:])), reads=[B_Sbf[si]], writes=[B_sbd[g][h]], pool=st_pool)
                    bk = next_bank()

                    def upd(e, ti=ti, cc_=cc_, bk=bk):
                        for dh in range(2):
                            ins = e.matmul(psb[bk][:, dh * 256:(dh + 1) * 256], tm[ti][:, cc_, 1, dh * 128:(dh + 1) * 128], tm[ti][:, cc_, 2, :], start=True, stop=True)
                        return ins
                    S.op("pe", upd, reads=[B_tm[ti]], writes=[B_ps[bk]])
                    S.op("dve", (lambda e, h=h, bk=bk: e.scalar_tensor_tensor(out=S32[:, h, :, :], in0=S32[:, h, :, :], scalar=cdb[:, h:h + 1],
                                                                              in1=psb[bk][:].rearrange("p (a b) -> p a b", a=2), op0=ALU.mult, op1=ALU.add)),
                         reads=[B_S32[h], B_ps[bk], B_dec], writes=[B_S32[h]])

            pend = []
            for h in range(H):
                ab = rr["AB"] % 2
                rr["AB"] += 1
                qi = rr["q"] % 2
                rr["q"] += 1
                for half in range(2):
                    bk = next_bank()
                    proj(bk)
                    S.op("act", (lambda e, ab=ab, half=half, bk=bk: e.activation(out=AB[ab][:, half, :], in_=psb[bk][:], func=AF.Copy)),
                         reads=[B_ps[bk]], writes=[B_AB[ab]])
                rotary(ab, ci, qst[qi][:, 0, :], qst[qi][:, 1, :], B_qst[qi])
                for half in range(2):
                    S.op("pool", (lambda e, g0=g0, h=h, qi=qi, half=half: e.dma_start(out=qc_d[g0:g0 + 4, :, h, half, :].rearrange("c p i -> p c i"),
                                                                                      in_=qst[qi][:, half, :].rearrange("p (c i) -> p c i", c=4))),
                         reads=[B_qst[qi]], writes=[B_qc[g0 + a][h] for a in range(4)], pool=st_pool)
                ab = rr["AB"] % 2
                rr["AB"] += 1
                ki = rr["kT"] % 2
                rr["kT"] += 1
                for half in range(2):
                    bk = next_bank()
                    proj(bk)
                    S.op("act", (lambda e, ab=ab, half=half, bk=bk: e.activation(out=AB[ab][:, half, :], in_=psb[bk][:], func=AF.Copy, scale=float(c.HD ** -0.5))),
                         reads=[B_ps[bk]], writes=[B_AB[ab]])
                rotary(ab, ci, kTh[ki][:, 0, :], kTh[ki][:, 1, :], B_kTh[ki])
                for half in range(2):
                    S.op("pool", (lambda e, g0=g0, h=h, ki=ki, half=half: e.dma_start(out=kc_d[g0:g0 + 4, :, h, half, :].rearrange("c p i -> p c i"),
                                                                                      in_=kTh[ki][:, half, :].rearrange("p (c i) -> p c i", c=4))),
                         reads=[B_kTh[ki]], writes=[B_kc[g0 + a][h] for a in range(4)], pool=st_pool)
                if pend:
                    post1(*pend[-1])
                vi = rr["vT"] % 2
                rr["vT"] += 1
                for half in range(2):
                    bk = next_bank()
                    proj(bk)
                    S.op("act", (lambda e, vi=vi, half=half, bk=bk: e.activation(out=vTh[vi][:, half, :], in_=psb[bk][:], func=AF.Copy)),
                         reads=[B_ps[bk]], writes=[B_vTh[vi]])
                if pend:
                    post2(pend[-1][0])
                    pend.pop()
                pend.append((h, ki, vi))
            for fc in range(RFC):
                if fc == 0:
                    post1(*pend[-1])
                if fc == min(2, RFC - 1):
                    post2(pend[-1][0])
                    pend.pop()
                bk = next_bank()
                proj(bk)
                gi = rr["sg"] % 2
                rr["sg"] += 1
                S.op("act", (lambda e, gi=gi, bk=bk: e.activation(out=sgs[gi][:], in_=psb[bk][:], func=AF.Silu)), reads=[B_ps[bk]], writes=[B_sgs[gi]])
                S.op("pool", (lambda e, gi=gi, g0=g0, fc=fc: e.dma_start(out=sg_d[g0:g0 + 4, :, fc, :].rearrange("c p i -> p c i"), in_=sgs[gi][:].rearrange("p (c i) -> p c i", c=4))),
                     reads=[B_sgs[gi]], writes=[B_sg[g0 + a] for a in range(4)], pool=st_pool)
            for fc in range(CFC):
                bk = next_bank()
                proj(bk)
                bi = rr["cb"] % 2
                rr["cb"] += 1
                S.op("act", (lambda e, bi=bi, bk=bk: e.activation(out=cbst[bi][:], in_=psb[bk][:], func=AF.Copy)), reads=[B_ps[bk]], writes=[B_cbst[bi]])
                S.op("pool", (lambda e, bi=bi, t=t, fc=fc: e.dma_start(out=cb_d[t, :, fc, :], in_=cbst[bi][:])), reads=[B_cbst[bi]], writes=[B_cb[t]], pool=st_pool)
                bk = next_bank()
                proj(bk)
                zi = rr["z"] % 2
                rr["z"] += 1
                S.op("act", (lambda e, zi=zi, bk=bk: e.activation(out=zst[zi][:], in_=psb[bk][:], func=AF.Copy)), reads=[B_ps[bk]], writes=[B_zst[zi]])
                bk = next_bank()
                proj(bk)
                S.op("dve", (lambda e, zi=zi, bk=bk: e.tensor_tensor(out=zst[zi][:], in0=psb[bk][:], in1=zst[zi][:], op=ALU.mult)), reads=[B_ps[bk], B_zst[zi]], writes=[B_zst[zi]])
                S.op("dve", (lambda e, zi=zi, fc=fc, t=t: e.tensor_copy(out=zedge[:, fc, t, 0:1], in_=zst[zi][:, 0:1])), reads=[B_zst[zi]], writes=[B_zedge])
                S.op("dve", (lambda e, zi=zi, fc=fc, t=t: e.tensor_copy(out=zedge[:, fc, t, 1:2], in_=zst[zi][:, TS - 1:TS])), reads=[B_zst[zi]], writes=[B_zedge])
                S.op("pool", (lambda e, zi=zi, t=t, fc=fc: e.dma_start(out=z_d[t, :, fc, :], in_=zst[zi][:])), reads=[B_zst[zi]], writes=[B_z[t]], pool=st_pool)
        assert wi[0] == len(wpieces)
        S.barrier()
        st.close()

    def phaseC1(l):
        st = ExitStack()
        qcs = [sb(f"c_q{i}", [128, H, 2, 128], BF16, st) for i in range(2)]
        kcs = [sb(f"c_k{i}", [128, H, 2, 128], BF16, st) for i in range(2)]
        qts = [sb(f"c_qt{i}", [128, 2, H, 2, 128], BF16, st) for i in range(2)]
        sgc = [sb(f"c_sg{i}", [128, RFC, 128], BF16, st) for i in range(2)]
        vs = [sb(f"c_v{i}", [128, H, 256], BF16, st) for i in range(2)]
        kks = [sb(f"c_kk{i}", [128, H, 256], BF16, st) for i in range(2)]
        sbs = [sb(f"c_sb{i}", [128, H, 2, 256], BF16, st) for i in range(2)]
        B_in = [Buf(f"c_in{i}") for i in range(2)]
        B_qt = [Buf(f"c_qt{i}") for i in range(2)]
        zt = [sb(f"c_z{i}", [128, TS + 2], F32, st) for i in range(2)]
        B_zt = [Buf(f"c_z{i}") for i in range(2)]
        cbt = [sb(f"c_cb{i}", [128, TS], F32, st) for i in range(2)]
        B_cbt = [Buf(f"c_cb{i}") for i in range(2)]
        cvo = [sb(f"c_cvo{i}", [128, TS], BF16, st) for i in range(2)]
        B_cvo = [Buf(f"c_cvo{i}") for i in range(2)]
        yt = sb("c_y", [128, TS], F32, st)
        B_yt = Buf("c_y")
        mixc = [sb(f"c_mix{i}", [128, RFC, 128], BF16, st) for i in range(2)]
        B_mixc = [Buf(f"c_mix{i}") for i in range(2)]
        sTm = sb("c_sTm", [128, H, 128], BF16, st)
        B_sTm = Buf("c_sTm")
        S32 = sb("c_S32", [128, H, 2, 256], F32, st)
        Sbf = sb("c_Sbf", [128, H, 2, 256], BF16, st)
        B_S32 = [Buf(f"c_S32_{h}") for h in range(H)]
        B_Sbf = [Buf(f"c_Sbf_{h}") for h in range(H)]
        stats = sb("c_stats", [128, H, 6], F32, st)
        mv = sb("c_mv", [128, H, 2], F32, st)
        rs = sb("c_rs", [128, H], F32, st)
        nb = sb("c_nb", [128, H], F32, st)
        B_stat = [Buf(f"c_stat{h}") for h in range(H)]
        B_rs = Buf("c_rs")
        on = sb("c_on", [128, H, 256], BF16, st)
        B_on = Buf("c_on")
        for h in range(H):
            S.op("pool", (lambda e, h=h: e.memset(S32[:, h, :, :], 0.0)), writes=[B_S32[h]])
            S.op("pool", (lambda e, h=h: e.memset(Sbf[:, h, :, :], 0.0)), writes=[B_Sbf[h]])
        rrv = 0
        rrz = 0
        for t in range(NTILE):
            for cc_ in range(4):
                g = t * 4 + cc_
                vi = rrv % 2
                rrv += 1
                csl = slice(cc_ * 128, (cc_ + 1) * 128)
                S.op("sp", (lambda e, vi=vi, g=g: e.dma_start(out=qcs[vi][:], in_=qc_d[g])), reads=B_qc[g], writes=[B_in[vi]], pool=ld_pool)
                S.op("sp", (lambda e, vi=vi, g=g: e.dma_start(out=kcs[vi][:], in_=kc_d[g])), reads=B_kc[g], writes=[B_in[vi]], pool=ld_pool)
                S.op("sp", (lambda e, vi=vi, g=g: e.dma_start(out=sgc[vi][:], in_=sg_d[g])), reads=[B_sg[g]], writes=[B_in[vi]], pool=ld_pool)
                S.op("sp", (lambda e, vi=vi, g=g: e.dma_start(out=vs[vi][:], in_=v_d[g].rearrange("h p d -> p h d"))), reads=B_v[g], writes=[B_in[vi]], pool=ld_pool)
                S.op("sp", (lambda e, vi=vi, g=g: e.dma_start(out=kks[vi][:], in_=kkf_d[g].rearrange("h p d -> p h d"))), reads=B_kkf[g], writes=[B_in[vi]], pool=ld_pool)
                S.op("sp", (lambda e, vi=vi, g=g: e.dma_start(out=sbs[vi][:], in_=sb_d[g].rearrange("h p a d -> p h a d"))), reads=B_sbd[g], writes=[B_in[vi]], pool=ld_pool)
                for half in range(2):
                    S.op("dve", (lambda e, vi=vi, half=half: e.tensor_tensor(out=qts[vi][:, 0, :, half, :], in0=qcs[vi][:, :, half, :], in1=qdf[:], op=ALU.mult)),
                         reads=[B_in[vi], B_dec], writes=[B_qt[vi]])
                    S.op("pool", (lambda e, vi=vi, half=half: e.tensor_tensor(out=qts[vi][:, 1, :, half, :], in0=qcs[vi][:, :, half, :], in1=qdb[:], op=ALU.mult)),
                         reads=[B_in[vi], B_dec], writes=[B_qt[vi]])
                if g == c.HALF_CH:
                    for h in range(H):
                        S.op("dve", (lambda e, h=h: e.tensor_scalar(out=S32[:, h, :, :], in0=S32[:, h, :, :], scalar1=link_sb[:, 0:1], scalar2=None, op0=ALU.mult)),
                             reads=[B_S32[h], B_par], writes=[B_S32[h]])
                        S.op("act", (lambda e, h=h: e.activation(out=Sbf[:, h, :, :], in_=S32[:, h, :, :], func=AF.Copy)), reads=[B_S32[h]], writes=[B_Sbf[h]])
                HB = min(H, 4)
                for hb in range(0, H, HB):
                    bk = next_bank()

                    def sc(e, hb=hb, bk=bk, vi=vi):
                        for a in range(HB):
                            h = hb + a
                            for half in range(2):
                                ins = e.matmul(psb[bk][:, a * 128:(a + 1) * 128], kcs[vi][:, h, half, :], qcs[vi][:, h, half, :], start=(half == 0), stop=(half == 1))
                        return ins
                    S.op("pe", sc, reads=[B_in[vi]], writes=[B_ps[bk]])
                    S.op("dve", (lambda e, hb=hb, bk=bk: e.tensor_tensor(out=sTm[:, hb:hb + HB, :], in0=psb[bk][:, 0:HB * 128].rearrange("p (a b) -> p a b", a=HB), in1=Mt[:, hb:hb + HB, :], op=ALU.mult)),
                         reads=[B_ps[bk], B_dec], writes=[B_sTm])
                obank = {}
                for hp in range(0, H, 2):
                    bk = next_bank()

                    def om(e, hp=hp, bk=bk, vi=vi):
                        for a in range(2):
                            h = hp + a
                            o_ = psb[bk][:, a * 256:(a + 1) * 256]
                            e.matmul(o_, sTm[:, h, :], vs[vi][:, h, :], start=True, stop=False)
                            e.matmul(o_, qts[vi][:, 0, h, 0, :], Sbf[:, h, 0, :], start=False, stop=False)
                            e.matmul(o_, qts[vi][:, 0, h, 1, :], Sbf[:, h, 1, :], start=False, stop=False)
                            e.matmul(o_, qts[vi][:, 1, h, 0, :], sbs[vi][:, h, 0, :], start=False, stop=False)
                            ins = e.matmul(o_, qts[vi][:, 1, h, 1, :], sbs[vi][:, h, 1, :], start=False, stop=True)
                        return ins
                    S.op("pe", om, reads=[B_sTm, B_in[vi], B_qt[vi], B_Sbf[hp], B_Sbf[hp + 1]], writes=[B_ps[bk]])
                    for a in range(2):
                        h = hp + a
                        obank[h] = (bk, a)
                        S.op("dve", (lambda e, h=h, a=a, bk=bk: e.bn_stats(out=stats[:, h, :], in_=psb[bk][:, a * 256:(a + 1) * 256])), reads=[B_ps[bk], B_rs], writes=[B_stat[h]])
                        S.op("dve", (lambda e, h=h: e.bn_aggr(out=mv[:, h, :], in_=stats[:, h, :])), reads=[B_stat[h]], writes=[B_stat[h]])
                S.op("act", (lambda e: e.activation(out=rs[:], in_=mv[:, :, 1], func=AF.Sqrt, scale=1.0, bias=NORM_EPS)), reads=B_stat, writes=[B_rs])
                S.op("dve", (lambda e: e.reciprocal(out=rs[:], in_=rs[:])), reads=[B_rs], writes=[B_rs])
                S.op("dve", (lambda e: e.scalar_tensor_tensor(out=nb[:], in0=mv[:, :, 0], scalar=-1.0, in1=rs[:], op0=ALU.mult, op1=ALU.mult)), reads=B_stat + [B_rs], writes=[B_rs])
                for h in range(H):
                    bk, a = obank[h]
                    S.op("act", (lambda e, h=h, a=a, bk=bk: e.activation(out=on[:, h, :], in_=psb[bk][:, a * 256:(a + 1) * 256], func=AF.Identity, scale=rs[:, h:h + 1], bias=nb[:, h:h + 1])),
                         reads=[B_ps[bk], B_rs], writes=[B_on])
                for h in range(H):
                    bk = next_bank()

                    def upd(e, h=h, bk=bk, vi=vi):
                        for dh in range(2):
                            ins = e.matmul(psb[bk][:, dh * 256:(dh + 1) * 256], kks[vi][:, h, dh * 128:(dh + 1) * 128], vs[vi][:, h, :], start=True, stop=True)
                        return ins
                    S.op("pe", upd, reads=[B_in[vi]], writes=[B_ps[bk]])
                    S.op("dve", (lambda e, h=h, bk=bk: e.scalar_tensor_tensor(out=S32[:, h, :, :], in0=S32[:, h, :, :], scalar=cdf[:, h:h + 1],
                                                                              in1=psb[bk][:].rearrange("p (a b) -> p a b", a=2), op0=ALU.mult, op1=ALU.add)),
                         reads=[B_S32[h], B_ps[bk], B_dec], writes=[B_S32[h]])
                    S.op("pool", (lambda e, h=h: e.tensor_copy(out=Sbf[:, h, :, :], in_=S32[:, h, :, :])), reads=[B_S32[h]], writes=[B_Sbf[h]])
                mi = g % 2
                for f0 in range(0, RFC, 8):
                    nf = min(8, RFC - f0)
                    bk = next_bank()
                    pv = psb[bk][:].bitcast(BF16)

                    def trf(e, f0=f0, nf=nf, pv=pv):
                        for a in range(nf):
                            fc = f0 + a
                            ins = e.transpose(out=pv[:, a * 128:(a + 1) * 128], in_=on[:, fc // 2, (fc % 2) * 128:(fc % 2 + 1) * 128], identity=ident_bf[:])
                        return ins
                    S.op("pe", trf, reads=[B_on, B_cbf], writes=[B_ps[bk]])
                    S.op("dve", (lambda e, f0=f0, nf=nf, pv=pv, mi=mi, vi=vi: e.tensor_tensor(out=mixc[mi][:, f0:f0 + nf, :], in0=pv[:, 0:nf * 128].rearrange("p (a b) -> p a b", a=nf), in1=sgc[vi][:, f0:f0 + nf, :], op=ALU.mult)),
                         reads=[B_ps[bk], B_in[vi]], writes=[B_mixc[mi]])
                S.op("pool", (lambda e, t=t, mi=mi, csl=csl: e.dma_start(out=mix_d[t, :, 0:RFC, csl], in_=mixc[mi][:])), reads=[B_mixc[mi]], writes=[B_mix[t]], pool=st_pool)
                for fc in range(cc_ * CFC // 4, (cc_ + 1) * CFC // 4):
                    zi = rrz % 2
                    rrz += 1
                    S.op("sp", (lambda e, t=t, fc=fc, zi=zi: e.dma_start(out=zt[zi][:, 1:TS + 1], in_=z_d[t, :, fc, :])), reads=[B_z[t]], writes=[B_zt[zi]], pool=ld_pool)
                    S.op("sp", (lambda e, t=t, fc=fc, zi=zi: e.dma_start(out=cbt[zi][:], in_=cb_d[t, :, fc, :])), reads=[B_cb[t]], writes=[B_cbt[zi]], pool=ld_pool)
                    if t == 0:
                        S.op("dve", (lambda e, zi=zi: e.memset(zt[zi][:, 0:1], 0.0)), writes=[B_zt[zi]])
                    elif t == c.HALF_TILE:
                        S.op("dve", (lambda e, zi=zi, fc=fc, t=t: e.tensor_scalar(out=zt[zi][:, 0:1], in0=zedge[:, fc, t - 1, 1:2], scalar1=link_sb[:, 0:1], scalar2=None, op0=ALU.mult)), reads=[B_zedge, B_par], writes=[B_zt[zi]])
                    else:
                        S.op("dve", (lambda e, zi=zi, fc=fc, t=t: e.tensor_copy(out=zt[zi][:, 0:1], in_=zedge[:, fc, t - 1, 1:2])), reads=[B_zedge], writes=[B_zt[zi]])
                    if t == NTILE - 1:
                        S.op("dve", (lambda e, zi=zi: e.memset(zt[zi][:, TS + 1:TS + 2], 0.0)), writes=[B_zt[zi]])
                    elif t == c.HALF_TILE - 1:
                        S.op("dve", (lambda e, zi=zi, fc=fc, t=t: e.tensor_scalar(out=zt[zi][:, TS + 1:TS + 2], in0=zedge[:, fc, t + 1, 0:1], scalar1=link_sb[:, 0:1], scalar2=None, op0=ALU.mult)), reads=[B_zedge, B_par], writes=[B_zt[zi]])
                    else:
                        S.op("dve", (lambda e, zi=zi, fc=fc, t=t: e.tensor_copy(out=zt[zi][:, TS + 1:TS + 2], in_=zedge[:, fc, t + 1, 0:1])), reads=[B_zedge], writes=[B_zt[zi]])

                    def wcol(tap, fc=fc):
                        o = (l * 3 + tap) * CFC + fc
                        return convw_sb[:, o:o + 1]
                    S.op("pool", (lambda e, zi=zi, wcol=wcol: e.tensor_scalar(out=yt[:], in0=zt[zi][:, 0:TS], scalar1=wcol(0), scalar2=None, op0=ALU.mult)), reads=[B_zt[zi], B_par], writes=[B_yt])
                    S.op("dve", (lambda e, zi=zi, wcol=wcol: e.scalar_tensor_tensor(out=yt[:], in0=zt[zi][:, 1:TS + 1], scalar=wcol(1), in1=yt[:], op0=ALU.mult, op1=ALU.add)), reads=[B_zt[zi], B_par, B_yt], writes=[B_yt])
                    S.op("dve", (lambda e, zi=zi, wcol=wcol: e.scalar_tensor_tensor(out=yt[:], in0=zt[zi][:, 2:TS + 2], scalar=wcol(2), in1=yt[:], op0=ALU.mult, op1=ALU.add)), reads=[B_zt[zi], B_par, B_yt], writes=[B_yt])
                    S.op("pool", (lambda e, zi=zi: e.tensor_tensor(out=cvo[zi][:], in0=yt[:], in1=cbt[zi][:], op=ALU.mult)), reads=[B_yt, B_cbt[zi]], writes=[B_cvo[zi]])
                    S.op("pool", (lambda e, t=t, fc=fc, zi=zi: e.dma_start(out=mix_d[t, :, RFC + fc, :], in_=cvo[zi][:])), reads=[B_cvo[zi]], writes=[B_mix[t]], pool=st_pool)
        S.barrier()
        st.close()

    def phaseC2(l):
        last = (l == L - 1)
        st = ExitStack()
        xT = sb("d_xT", [128, FC, TS], F32, st)
        B_x = [Buf(f"d_x{fc}") for fc in range(FC)]
        act = sb("d_act", [128, FC, TS], BF16, st)
        NQ = 4
        QF = FC // NQ
        B_actq = [Buf(f"d_act{i}") for i in range(NQ)]
        GK = c.FG // 128
        hid = [sb(f"d_hid{i}", [128, GK, TS], BF16, st) for i in range(2)]
        B_hidc = [[Buf(f"d_hid{i}_{a}") for a in range(GK)] for i in range(2)]
        sqb = [sb(f"d_sq{i}", [128, TS], BF16, st) for i in range(2)]
        B_sq = [Buf(f"d_sq{i}") for i in range(2)]
        rstd = sb("d_rstd", [128, TS], F32, st)
        B_rstd = Buf("d_rstd")
        rl = [sb(f"d_rl{i}", [128, TS], F32, st) for i in range(2)]
        B_rl = [Buf(f"d_rl{i}") for i in range(2)]
        sig = [sb(f"d_sig{i}", [128, TS], F32, st) for i in range(2)]
        B_sig = [Buf(f"d_sig{i}") for i in range(2)]
        pTs = sb("d_pT", [128, c.PKC, TS], BF16, st)
        B_pTs = Buf("d_pT")
        plew = sb("d_plew", [128, c.PKC, D], BF16, st)
        B_plew = Buf("d_plew")
        YW = min(D, 2048)
        if last:
            ysb = sb("d_y", [128, YW], F32, st)
            B_ysb = Buf("d_y")
        rr = {"sq": 0, "rl": 0, "sig": 0, "hid": 0}

        S.bg_flush_to(("ple", l))
        S.op("sp", (lambda e: e.dma_start(out=plew[:], in_=wt_ple[l])), reads=[B_wtp[("ple", l)]], writes=[B_plew], pool=ld_pool)

        base = len(wpieces)
        B_src = None
        for t in range(NTILE):
            for oc in range(FC):
                wpieces.append((wt_out[l, oc], [FC, 128], ("out", l, oc)))
            for kind, g in ffn_order():
                if kind == "ff1":
                    for a in range(GK):
                        wpieces.append((wt_ff1[l, g * GK + a], [FC, 128], ("ff1", l, g * GK + a)))
                else:
                    for q in range(D // 512):
                        wpieces.append((wt_ff2[l, g, q], [GK, 512], ("ff2", l, g, q)))
            for oc in range(FC):
                wpieces.append((wt_gate[l, oc], [FC, 128], ("gate", l, oc)))
        wi = [base]

        def nextw():
            i = wi[0]
            wi[0] += 1
            return wget(i, B_src)

        def norm_to(kind_l, kind, to_act):
            bk = next_bank()
            for fc in range(FC):
                j = rr["sq"] % 2
                rr["sq"] += 1
                S.op("act", (lambda e, fc=fc, j=j: e.activation(out=sqb[j][:], in_=xT[:, fc, :], func=AF.Square)), reads=[B_x[fc]], writes=[B_sq[j]])
                S.op("pe", (lambda e, j=j, fc=fc, bk=bk: e.matmul(psb[bk][:], ones_bf[:], sqb[j][:], start=(fc == 0), stop=(fc == FC - 1))),
                     reads=[B_sq[j], B_cbf], writes=[B_ps[bk]])
            S.op("act", (lambda e, bk=bk: e.activation(out=rstd[:], in_=psb[bk][:], func=AF.Sqrt, scale=1.0 / D, bias=NORM_EPS)),
                 reads=[B_ps[bk]], writes=[B_rstd])
            S.op("dve", (lambda e: e.reciprocal(out=rstd[:], in_=rstd[:])), reads=[B_rstd], writes=[B_rstd])
            for fc in range(FC):
                if not to_act:
                    S.op("dve", (lambda e, fc=fc: e.scalar_tensor_tensor(out=xT[:, fc, :], in0=xT[:, fc, :], scalar=norm_col(kind_l, kind, fc), in1=rstd[:], op0=ALU.mult, op1=ALU.mult)),
                         reads=[B_x[fc], B_rstd, B_par], writes=[B_x[fc]])
                else:
                    S.op("dve", (lambda e, fc=fc: e.scalar_tensor_tensor(out=act[:, fc, :], in0=xT[:, fc, :], scalar=norm_col(kind_l, kind, fc), in1=rstd[:], op0=ALU.mult, op1=ALU.mult)),
                         reads=[B_x[fc], B_rstd, B_par], writes=[B_actq[fc // QF]])

        def bigmm(wap, bk, n_kc, rhs_tile, col0=None, k0=0, k1=None):
            k1 = n_kc if k1 is None else k1

            def f(e):
                for kc in range(k0, k1):
                    lhs = wap[:, kc, :] if col0 is None else wap[:, kc, col0:col0 + 128]
                    ins = e.matmul(psb[bk][:], lhs, rhs_tile[:, kc, :], start=(kc == 0), stop=(kc == n_kc - 1))
                return ins
            return f

        def actmm(wap, bw, bk):
            for qi_ in range(NQ):
                S.op("pe", bigmm(wap, bk, FC, act, k0=qi_ * QF, k1=(qi_ + 1) * QF), reads=[bw, B_actq[qi_]], writes=[B_ps[bk]])

        XS = min(8, FC)
        for t in range(NTILE):
            for f0 in range(0, FC, 8):
                nf = min(8, FC - f0)
                S.op("sp", (lambda e, t=t, f0=f0, nf=nf: e.dma_start(out=xT[:, f0:f0 + nf, :], in_=xT_d[t, :, f0:f0 + nf, :])),
                     reads=[B_xT[t]], writes=B_x[f0:f0 + nf], pool=ld_pool)
            S.op("sp", (lambda e, t=t: e.dma_start(out=act[:], in_=mix_d[t])), reads=[B_mix[t]], writes=B_actq, pool=ld_pool)
            S.op("sp", (lambda e, t=t: e.dma_start(out=pTs[:], in_=pT_d[l, t])), reads=[B_pT[l][t]], writes=[B_pTs], pool=ld_pool)
            for oc in range(FC):
                wap, bw = nextw()
                bk = next_bank()
                actmm(wap, bw, bk)
                S.op("dve", (lambda e, oc=oc, bk=bk: e.tensor_tensor(out=xT[:, oc, :], in0=psb[bk][:], in1=xT[:, oc, :], op=ALU.add)), reads=[B_ps[bk], B_x[oc]], writes=[B_x[oc]])
            norm_to(l, 1, True)
            for kind, g in ffn_order():
                hi = g % 2
                if kind == "ff1":
                    for a in range(GK):
                        wap, bw = nextw()
                        bk = next_bank()
                        actmm(wap, bw, bk)
                        ri = rr["rl"] % 2
                        rr["rl"] += 1
                        S.op("act", (lambda e, ri=ri, bk=bk: e.activation(out=rl[ri][:], in_=psb[bk][:], func=AF.Relu)), reads=[B_ps[bk]], writes=[B_rl[ri]])
                        S.op("pool", (lambda e, ri=ri, hi=hi, a=a: e.tensor_tensor(out=hid[hi][:, a, :], in0=rl[ri][:], in1=rl[ri][:], op=ALU.mult)), reads=[B_rl[ri]], writes=[B_hidc[hi][a]])
                else:
                    for q in range(D // 512):
                        wap, bw = nextw()
                        for a in range(4):
                            oc = q * 4 + a
                            bk = next_bank()
                            S.op("pe", bigmm(wap, bk, GK, hid[hi], col0=a * 128), reads=[bw] + B_hidc[hi], writes=[B_ps[bk]])
                            S.op("dve", (lambda e, oc=oc, bk=bk: e.tensor_tensor(out=xT[:, oc, :], in0=psb[bk][:], in1=xT[:, oc, :], op=ALU.add)), reads=[B_ps[bk], B_x[oc]], writes=[B_x[oc]])
            norm_to(l, 2, True)
            for oc in range(FC):
                wap, bw = nextw()
                bk = next_bank()
                actmm(wap, bw, bk)
                si = rr["sig"] % 2
                rr["sig"] += 1
                S.op("act", (lambda e, si=si, bk=bk: e.activation(out=sig[si][:], in_=psb[bk][:], func=AF.Sigmoid)), reads=[B_ps[bk]], writes=[B_sig[si]])
                bk2 = next_bank()
                S.op("pe", bigmm(plew, bk2, c.PKC, pTs, col0=oc * 128), reads=[B_plew, B_pTs], writes=[B_ps[bk2]])
                S.op("dve", (lambda e, si=si, bk2=bk2: e.tensor_tensor(out=sig[si][:], in0=psb[bk2][:], in1=sig[si][:], op=ALU.mult)), reads=[B_ps[bk2], B_sig[si]], writes=[B_sig[si]])
                S.op("pool", (lambda e, si=si, oc=oc: e.tensor_tensor(out=xT[:, oc, :], in0=xT[:, oc, :], in1=sig[si][:], op=ALU.add)), reads=[B_sig[si], B_x[oc]], writes=[B_x[oc]])
                if not last and (oc + 1) % XS == 0:
                    f0 = oc + 1 - XS
                    S.op("pool", (lambda e, t=t, f0=f0: e.dma_start(out=xT_d[t, :, f0:f0 + XS, :], in_=xT[:, f0:f0 + XS, :])), reads=B_x[f0:f0 + XS], writes=[B_xT[t]], pool=st_pool)
            if last:
                norm_to(L, 0, False)
                for tg in range(4):
                    for y0 in range(0, D, YW):
                        for f0 in range(y0 // 128, (y0 + YW) // 128, 4):
                            nf = min(4, FC - f0)
                            bk = next_bank()

                            def tr(e, f0=f0, nf=nf, bk=bk, tg=tg):
                                for a in range(nf):
                                    ins = e.transpose(out=psb[bk][:, a * 128:(a + 1) * 128], in_=xT[:, f0 + a, tg * 128:(tg + 1) * 128], identity=cst("ident"))
                                return ins
                            S.op("pe", tr, reads=B_x[f0:f0 + nf] + [B_consts], writes=[B_ps[bk]])
                            c0 = f0 * 128 - y0
                            if (f0 // 4) % 2 == 0:
                                S.op("act", (lambda e, c0=c0, nf=nf, bk=bk: e.activation(out=ysb[:, c0:c0 + nf * 128], in_=psb[bk][:, 0:nf * 128], func=AF.Copy)), reads=[B_ps[bk]], writes=[B_ysb])
                            else:
                                S.op("dve", (lambda e, c0=c0, nf=nf, bk=bk: e.tensor_copy(out=ysb[:, c0:c0 + nf * 128], in_=psb[bk][:, 0:nf * 128])), reads=[B_ps[bk]], writes=[B_ysb])
                        r0 = t * TS + tg * 128
                        o = S.op("pool", (lambda e, r0=r0, y0=y0: e.dma_start(out=y_out[r0:r0 + 128, y0:y0 + YW], in_=ysb[:])), reads=[B_ysb], pool=st_pool)
                        out_store_ops.append(o)
        assert wi[0] == len(wpieces)
        S.barrier()
        st.close()

    stop_after = cfg.stop_after
    phase0()
    phaseT()
    for l in range(L):
        if stop_after == "T":
            break
        layer_setup(l)
        S.bg_every = cfg.pace_a
        phaseA(l)
        if stop_after == f"A{l}":
            break
        S.bg_every = cfg.pace_c1
        phaseC1(l)
        if stop_after == f"C1{l}":
            break
        S.bg_every = cfg.pace_c2
        phaseC2(l)
        if stop_after == f"C2{l}":
            break

    if not out_store_ops:
        out_store_ops.extend(s.last for s in S.all_slots if s.last is not None)
    S.emit(out_store_ops)
    es.close()
    return nc


def host_layout(cfg, units, p_units, links, seg_pos, W):
    c = cfg
    L, FC, H, CFC = c.DEPTH, c.FC, c.H, c.CFC
    consts = make_consts()
    norms = np.concatenate(
        [np.stack([W["norm_mix"][l], W["norm_mlp"][l], W["norm_ple"][l]], 0) for l in range(L)] + [W["norm_final"][None]], 0
    )
    norms = np.ascontiguousarray(norms.reshape(3 * L + 1, FC, 128).transpose(2, 0, 1).reshape(128, (3 * L + 1) * FC))
    convw = np.ascontiguousarray(W["conv_w"].reshape(L, 3, CFC, 128).transpose(3, 0, 1, 2).reshape(128, L * 3 * CFC))
    dec = np.stack([W["ret_decay_fwd"], W["ret_decay_bwd"]], 1).reshape(1, L * 2 * H)
    dec = np.ascontiguousarray(np.broadcast_to(dec, (128, L * 2 * H)))
    in_maps = []
    for u in range(len(units)):
        cosT, sinT = rope_tables(seg_pos[u])
        in_maps.append({
            "x": units[u], "p": p_units[u],
            "w_in": W["w_in"], "w_out": W["w_out"], "w_ff1": W["w_ff1"], "w_ff2": W["w_ff2"],
            "w_gate": W["w_ple_gate"], "w_ple": W["w_ple_proj"],
            "norms": norms, "convw": convw, "dec": dec, "consts": consts,
            "link": np.full((128, 1), links[u], np.float32), "cosT": cosT, "sinT": sinT,
        })
    return in_maps


_PROG_CACHE = {}


def kernel(x_prompt, x_sample, p_prompt, p_sample, norm_mix, w_in, ret_decay_fwd, ret_decay_bwd,
           conv_w, w_out, norm_mlp, w_ff1, w_ff2, norm_ple, w_ple_gate, w_ple_proj, norm_final):
    f = lambda a: np.ascontiguousarray(np.asarray(a, dtype=np.float32))
    x_prompt, x_sample, p_prompt, p_sample = f(x_prompt), f(x_sample), f(p_prompt), f(p_sample)
    W = dict(norm_mix=f(norm_mix), w_in=f(w_in), ret_decay_fwd=f(ret_decay_fwd), ret_decay_bwd=f(ret_decay_bwd),
             conv_w=f(conv_w), w_out=f(w_out), norm_mlp=f(norm_mlp), w_ff1=f(w_ff1), w_ff2=f(w_ff2),
             norm_ple=f(norm_ple), w_ple_gate=f(w_ple_gate), w_ple_proj=f(w_ple_proj), norm_final=f(norm_final))
    B, SEQ, D = x_prompt.shape
    DB, DSEQ, _ = x_sample.shape
    L = W["w_in"].shape[0]
    NT = SEQ
    assert DSEQ * 2 == NT and DB % 2 == 0
    cfg = Cfg(D=D, NT=NT, DEPTH=L, PL=p_prompt.shape[-1], n_cores=8)
    units, p_units, links, seg_pos = [], [], [], []
    for b in range(B):
        units.append(x_prompt[b])
        p_units.append(np.ascontiguousarray(p_prompt[:, b]))
        links.append(1.0)
        seg_pos.append(np.arange(NT, dtype=np.float32))
    for b in range(0, DB, 2):
        units.append(np.ascontiguousarray(x_sample[b:b + 2].reshape(NT, D)))
        p_units.append(np.ascontiguousarray(p_sample[:, b:b + 2].reshape(L, NT, -1)))
        links.append(0.0)
        seg_pos.append(np.concatenate([np.arange(DSEQ), np.arange(DSEQ)]).astype(np.float32))
    n_units = len(units)
    assert n_units <= cfg.n_cores
    spare = [3, 7][: cfg.n_cores - n_units]
    slots = [s for s in range(cfg.n_cores) if s not in spare]
    make_consts()
    key = (D, NT, L)
    if key not in _PROG_CACHE:
        _PROG_CACHE[key] = build_program(cfg)
    nc = _PROG_CACHE[key]
    real_maps = host_layout(cfg, units, p_units, links, seg_pos, W)
    zero_map = {k: np.zeros_like(v) for k, v in real_maps[0].items()}
    in_maps = [None] * cfg.n_cores
    for u, s in enumerate(slots):
        in_maps[s] = real_maps[u]
    for s in spare:
        in_maps[s] = zero_map
    res = run_bass_kernel_spmd(nc, in_maps, core_ids=list(range(cfg.n_cores)))
    ys = [np.asarray(res.results[slots[u]]["y"], dtype=np.float32) for u in range(n_units)]
    y_prompt = np.stack(ys[:B], 0)
    y_sample = np.stack(ys[B:], 0).reshape(DB, DSEQ, D)
    return (y_prompt, y_sample)
```

```python
from contextlib import ExitStack

import numpy as np

import concourse.bass as bass
import concourse.mybir as mybir
from concourse.bass_utils import run_bass_kernel_spmd

F32 = mybir.dt.float32
BF16 = mybir.dt.bfloat16
AF = mybir.ActivationFunctionType
ALU = mybir.AluOpType

NORM_EPS = 1e-6
ROPE_BASE = 10000.0


class Cfg:
    def __init__(self, D=4096, NT=4096, DEPTH=2, PL=256, n_cores=8, stop_after=None):
        self.stop_after = stop_after
        self.pace_a, self.pace_c1, self.pace_c2 = 6, 0, 3
        self.D = D
        self.NT = NT
        self.DEPTH = DEPTH
        self.PL = PL
        self.n_cores = n_cores
        self.FC = D // 128
        self.RW = D // 2
        self.CW = D - self.RW
        self.HD = 256
        self.H = self.RW // self.HD
        self.RFC = self.RW // 128
        self.CFC = self.CW // 128
        self.DFF = 4 * D
        self.INW = 4 * self.RW + 3 * self.CW
        self.TS = 512
        self.NTILE = NT // self.TS
        self.NCH = NT // 128
        self.HALF_CH = self.NCH // 2
        self.HALF_TILE = self.NTILE // 2
        self.FG = 1024
        self.NFG = self.DFF // self.FG
        self.PKC = PL // 128


class Buf:
    __slots__ = ("name", "w", "r")

    def __init__(self, name):
        self.name = name
        self.w = None
        self.r = []


class Op:
    __slots__ = ("eng", "fn", "deps", "need", "sem", "val", "is_dma")


class SemSlot:
    __slots__ = ("sem", "count", "last")

    def __init__(self, sem):
        self.sem = sem
        self.count = 0
        self.last = None


class SemPool:
    def __init__(self, slots):
        self.slots = slots
        self.i = 0

    def next(self):
        s = self.slots[self.i % len(self.slots)]
        self.i += 1
        return s


ENGS = ("pe", "act", "dve", "pool", "sp")


class Sched:
    def __init__(self, nc, es):
        self.nc = nc
        self.es = es
        self.ops = []
        self.eng_sem = {e: es.enter_context(nc.semaphore("sem_" + e)) for e in ("pe", "act", "dve", "pool")}
        self.last_on_eng = {e: None for e in ENGS}
        self.all_slots = []
        self.bg_queue = []
        self.bg_next = 0
        self.bg_done = set()
        self.bg_every = 0
        self.pool_ticks = 0
        self.in_bg = False

    def sem_pool(self, name, n):
        slots = [SemSlot(self.es.enter_context(self.nc.semaphore(f"{name}{i}"))) for i in range(n)]
        self.all_slots.extend(slots)
        return SemPool(slots)

    def op(self, eng, fn, reads=(), writes=(), pool=None, extra_deps=()):
        o = Op()
        o.eng = eng
        o.fn = fn
        o.need = False
        o.sem = None
        o.val = None
        o.is_dma = pool is not None
        deps = set(extra_deps)
        for b in reads:
            if b.w is not None:
                deps.add(b.w)
        for b in writes:
            if b.w is not None:
                deps.add(b.w)
            deps.update(b.r)
        for b in reads:
            b.r.append(o)
        for b in writes:
            b.w = o
            b.r = []
        if pool is not None:
            slot = pool.next()
            if slot.last is not None:
                deps.add(slot.last)
            slot.last = o
            slot.count += 16
            o.sem = slot.sem
            o.val = slot.count
        deps.discard(o)
        o.deps = deps
        for d in deps:
            d.need = True
        self.ops.append(o)
        self.last_on_eng[eng] = o
        if eng == "pool" and not self.in_bg and self.bg_every > 0:
            self.pool_ticks += 1
            if self.pool_ticks % self.bg_every == 0:
                self.bg_release(1)
        return o

    def bg_release(self, n):
        self.in_bg = True
        while n > 0 and self.bg_next < len(self.bg_queue):
            key, fn = self.bg_queue[self.bg_next]
            self.bg_next += 1
            fn()
            self.bg_done.add(key)
            n -= 1
        self.in_bg = False

    def bg_flush_to(self, key):
        while key not in self.bg_done:
            assert self.bg_next < len(self.bg_queue), key
            self.bg_release(1)

    def barrier(self):
        lasts = [o for o in self.last_on_eng.values() if o is not None]
        lasts += [s.last for s in self.all_slots if s.last is not None]
        for e in ENGS:
            self.op(e, None, extra_deps=[o for o in lasts])

    def emit(self, final_wait_ops):
        nc = self.nc
        cnt = {e: 0 for e in self.eng_sem}
        for o in self.ops:
            if o.is_dma:
                continue
            if o.need and o.fn is not None:
                cnt[o.eng] += 1
                o.sem = self.eng_sem[o.eng]
                o.val = cnt[o.eng]
        def resolve(d, out, seen):
            if d in seen:
                return
            seen.add(d)
            if d.fn is None:
                for dd in d.deps:
                    resolve(dd, out, seen)
            else:
                out.append(d)

        per_eng = {e: [] for e in ENGS}
        for o in self.ops:
            per_eng[o.eng].append(o)

        def emit_engine(ename, eng):
            waited = {}
            for o in per_eng[ename]:
                flat = []
                seen = set()
                for d in o.deps:
                    resolve(d, flat, seen)
                for d in flat:
                    if ename == "pe" and d.eng == "pe" and not d.is_dma:
                        continue
                    key = id(d.sem)
                    if waited.get(key, 0) >= d.val:
                        continue
                    eng.wait_ge(d.sem, d.val)
                    waited[key] = d.val
                if o.fn is None:
                    continue
                inst = o.fn(eng)
                if o.is_dma:
                    inst.then_inc(o.sem, 16)
                elif o.need:
                    inst.then_inc(o.sem, 1)
            if ename == "sp":
                for d in final_wait_ops:
                    key = id(d.sem)
                    if waited.get(key, 0) >= d.val:
                        continue
                    eng.wait_ge(d.sem, d.val)
                    waited[key] = d.val

        with nc.Block() as block:
            @block.tensor
            def _(e):
                emit_engine("pe", e)

            @block.scalar
            def _(e):
                emit_engine("act", e)

            @block.vector
            def _(e):
                emit_engine("dve", e)

            @block.gpsimd
            def _(e):
                emit_engine("pool", e)

            @block.sync
            def _(e):
                emit_engine("sp", e)


CONST_COLS = {}


def make_consts():
    i = np.arange(128, dtype=np.float32)
    parts = []
    off = 0

    def add(name, arr):
        nonlocal off
        arr = np.asarray(arr, np.float32)
        CONST_COLS[name] = (off, arr.shape[1])
        off += arr.shape[1]
        parts.append(arr)

    diff = i[None, :] - i[:, None]
    add("ident", np.eye(128))
    add("ones", np.ones((128, 128)))
    add("row_f", np.broadcast_to(i[None, :] + 1.0, (128, 128)))
    add("row_b", np.broadcast_to(128.0 - i[None, :], (128, 128)))
    add("dpos", np.maximum(diff, 0.0))
    add("dneg", np.maximum(-diff, 0.0))
    add("mf", (diff >= 0).astype(np.float32))
    add("mb", (diff < 0).astype(np.float32))
    add("col_f", (127.0 - i)[:, None])
    add("col_b", i[:, None])
    return np.ascontiguousarray(np.concatenate(parts, axis=1))


def rope_tables(pos):
    half = 128
    inv_freq = (ROPE_BASE ** (-np.arange(half, dtype=np.float32) / half)).astype(np.float32)
    ang = (pos.astype(np.float32)[None, :] * inv_freq[:, None]).astype(np.float32)
    return np.ascontiguousarray(np.cos(ang).astype(np.float32)), np.ascontiguousarray(np.sin(ang).astype(np.float32))


def build_program(cfg, debug=False):
    c = cfg
    nc = bass.Bass("TRN2", target_bir_lowering=False)
    D, NT, FC, TS, NTILE, NCH, H = c.D, c.NT, c.FC, c.TS, c.NTILE, c.NCH, c.H
    RFC, CFC, L = c.RFC, c.CFC, c.DEPTH
    NCONST = sum(v[1] for v in CONST_COLS.values())

    def din(name, shape, dt=F32):
        return nc.dram_tensor(name, list(shape), dt, kind="ExternalInput").ap()

    def dscr(name, shape, dt):
        kind = "ExternalOutput" if debug else "Internal"
        return nc.dram_tensor(name, list(shape), dt, kind=kind).ap()

    x_in = din("x", [NT, D])
    p_in = din("p", [L, NT, c.PL])
    w_in_d = din("w_in", [L, D, c.INW])
    w_out_d = din("w_out", [L, D, D])
    w_ff1_d = din("w_ff1", [L, D, c.DFF])
    w_ff2_d = din("w_ff2", [L, c.DFF, D])
    w_gate_d = din("w_gate", [L, D, D])
    w_ple_d = din("w_ple", [L, c.PL, D])
    norms_d = din("norms", [128, (3 * L + 1) * FC])
    convw_d = din("convw", [128, L * 3 * CFC])
    dec_d = din("dec", [128, L * 2 * H])
    consts_d = din("consts", [128, NCONST])
    link_d = din("link", [128, 1])
    cos_d = din("cosT", [128, NT])
    sin_d = din("sinT", [128, NT])
    y_out = nc.dram_tensor("y", [NT, D], F32, kind="ExternalOutput").ap()

    wt_in = dscr("wt_in", [L, c.INW // 128, 128, FC, 128], BF16)
    wt_out = dscr("wt_out", [L, FC, 128, FC, 128], BF16)
    wt_ff1 = dscr("wt_ff1", [L, c.DFF // 128, 128, FC, 128], BF16)
    wt_gate = dscr("wt_gate", [L, FC, 128, FC, 128], BF16)
    wt_ff2 = dscr("wt_ff2", [L, c.NFG, D // 512, 128, c.FG // 128, 512], BF16)
    wt_ple = dscr("wt_ple", [L, 128, c.PKC, D], BF16)
    xT_d = dscr("xT_s", [NTILE, 128, FC, TS], F32)
    pT_d = dscr("pT_s", [L, NTILE, 128, c.PKC, TS], BF16)
    qc_d = dscr("qc_s", [NCH, 128, H, 2, 128], BF16)
    kc_d = dscr("kc_s", [NCH, 128, H, 2, 128], BF16)
    kkf_d = dscr("kkf_s", [NCH, H, 128, 256], BF16)
    v_d = dscr("v_s", [NCH, H, 128, 256], BF16)
    sb_d = dscr("sb_s", [NCH, H, 128, 2, 256], BF16)
    sg_d = dscr("sg_s", [NCH, 128, RFC, 128], BF16)
    z_d = dscr("z_s", [NTILE, 128, CFC, TS], F32)
    cb_d = dscr("cb_s", [NTILE, 128, CFC, TS], F32)
    mix_d = dscr("mix_s", [NTILE, 128, FC, TS], BF16)

    es = ExitStack()
    S = Sched(nc, es)

    uid = [0]

    def sb(name, shape, dt, stack=None):
        uid[0] += 1
        return (stack or es).enter_context(nc.sbuf_tensor(f"{name}_u{uid[0]}", list(shape), dt))

    def ps(name, shape, dt, stack=None):
        return (stack or es).enter_context(nc.psum_tensor(name, list(shape), dt))

    consts = sb("consts_sb", [128, NCONST], F32)
    B_consts = Buf("consts")
    ident_bf = sb("ident_bf", [128, 128], BF16)
    ones_bf = sb("ones_bf", [128, 128], BF16)
    B_cbf = Buf("cbf")
    norms_sb = sb("norms_sb", [128, (3 * L + 1) * FC], F32)
    convw_sb = sb("convw_sb", [128, L * 3 * CFC], F32)
    dec_sb = sb("dec_sb", [128, L * 2 * H], F32)
    link_sb = sb("link_sb", [128, 1], F32)
    B_par = Buf("params")
    NW = 4
    wslots = [sb(f"wslot{i}", [128, 4096], BF16) for i in range(NW)]
    B_w = [Buf(f"wslot{i}") for i in range(NW)]
    wsem = S.sem_pool("wsem", NW)

    def cst(name, cols=None):
        off, n = CONST_COLS[name]
        return consts[:, off:off + n]

    psb = [ps(f"psb{i}", [128, 512], F32) for i in range(8)]
    B_ps = [Buf(f"ps{i}") for i in range(8)]
    ps_rr = [0]

    def next_bank():
        i = ps_rr[0] % 8
        ps_rr[0] += 1
        return i

    ld_pool = S.sem_pool("ld", 6)
    st_pool = S.sem_pool("st", 8)
    misc_pool = S.sem_pool("misc", 2)

    out_store_ops = []

    S.op("sp", lambda e: e.dma_start(out=consts[:], in_=consts_d[:, :]), writes=[B_consts], pool=misc_pool)
    S.op("sp", lambda e: e.dma_start(out=norms_sb[:], in_=norms_d[:, :]), writes=[B_par], pool=misc_pool)
    S.op("sp", lambda e: e.dma_start(out=convw_sb[:], in_=convw_d[:, :]), writes=[B_par], pool=misc_pool)
    S.op("sp", lambda e: e.dma_start(out=dec_sb[:], in_=dec_d[:, :]), writes=[B_par], pool=misc_pool)
    S.op("sp", lambda e: e.dma_start(out=link_sb[:], in_=link_d[:, :]), writes=[B_par], pool=misc_pool)
    S.op("dve", lambda e: e.tensor_copy(out=ident_bf[:], in_=cst("ident")), reads=[B_consts], writes=[B_cbf])
    S.op("dve", lambda e: e.tensor_copy(out=ones_bf[:], in_=cst("ones")), reads=[B_consts], writes=[B_cbf])

    wpieces = []
    wissued = [0]

    def wpiece_ap(slot, shape):
        n = int(np.prod(shape))
        ap = wslots[slot][:, 0:n]
        if len(shape) == 2:
            return ap.rearrange("p (a b) -> p a b", a=shape[0])
        return ap

    def wget(i, B_src=None):
        while wissued[0] < min(len(wpieces), i + NW):
            j = wissued[0]
            src, shape, key = wpieces[j]
            S.bg_flush_to(key)
            slot = j % NW
            dst = wpiece_ap(slot, shape)
            S.op("sp", (lambda e, dst=dst, src=src: e.dma_start(out=dst, in_=src)),
                 reads=[B_wtp[key]], writes=[B_w[slot]], pool=SemPool([wsem.slots[slot]]))
            wissued[0] += 1
        slot = i % NW
        return wpiece_ap(slot, wpieces[i][1]), B_w[slot]

    B_wtp = {}
    cv_pool = S.sem_pool("cv", 12)

    def phase0():
        def add(key, dst, srcap):
            B_wtp[key] = Buf("wt" + str(key))

            def fn(key=key, dst=dst, srcap=srcap):
                S.op("pool", (lambda e: e.dma_start(out=dst, in_=srcap)), writes=[B_wtp[key]], pool=cv_pool)
            S.bg_queue.append((key, fn))

        GK = c.FG // 128
        for l in range(L):
            v_in = w_in_d[l].rearrange("(kc p) n -> p kc n", p=128)
            v_out = w_out_d[l].rearrange("(kc p) n -> p kc n", p=128)
            v_ff1 = w_ff1_d[l].rearrange("(kc p) n -> p kc n", p=128)
            v_ff2 = w_ff2_d[l].rearrange("(kc p) n -> p kc n", p=128)
            v_gate = w_gate_d[l].rearrange("(kc p) n -> p kc n", p=128)
            v_ple = w_ple_d[l].rearrange("(kc p) n -> p kc n", p=128)
            for oc in in_order:
                add(("in", l, oc), wt_in[l, oc], v_in[:, :, oc * 128:(oc + 1) * 128])
            add(("ple", l), wt_ple[l], v_ple)
            for oc in range(FC):
                add(("out", l, oc), wt_out[l, oc], v_out[:, :, oc * 128:(oc + 1) * 128])
            for kind, g in ffn_order():
                if kind == "ff1":
                    for a in range(GK):
                        oc = g * GK + a
                        add(("ff1", l, oc), wt_ff1[l, oc], v_ff1[:, :, oc * 128:(oc + 1) * 128])
                else:
                    for q in range(D // 512):
                        add(("ff2", l, g, q), wt_ff2[l, g, q], v_ff2[:, g * GK:(g + 1) * GK, q * 512:(q + 1) * 512])
            for oc in range(FC):
                add(("gate", l, oc), wt_gate[l, oc], v_gate[:, :, oc * 128:(oc + 1) * 128])
        S.bg_flush_to(("in", 0, in_order[-1]))

    def ffn_order():
        seq = []
        for g in range(c.NFG):
            seq.append(("ff1", g))
            if g >= 1:
                seq.append(("ff2", g - 1))
        seq.append(("ff2", c.NFG - 1))
        return seq

    in_order = []
    for h in range(H):
        for o0 in (0, RFC, 2 * RFC):
            for half in range(2):
                in_order.append(o0 + 2 * h + half)
    for fc in range(RFC):
        in_order.append(3 * RFC + fc)
    for fc in range(CFC):
        in_order.append(4 * RFC + fc)
        in_order.append(4 * RFC + CFC + fc)
        in_order.append(4 * RFC + 2 * CFC + fc)

    B_xT = [Buf(f"xT{t}") for t in range(NTILE)]
    B_pT = [[Buf(f"pT{l}_{t}") for t in range(NTILE)] for l in range(L)]

    def phaseT():
        st = ExitStack()
        xin = [sb(f"tx{i}", [128, D], F32, st) for i in range(2)]
        B_xin = [Buf(f"tx{i}") for i in range(2)]
        xTs = sb("txT", [128, FC, TS], F32, st)
        B_xTs = Buf("txT")
        pin = [sb(f"tp{i}", [128, c.PL], F32, st) for i in range(2)]
        B_pin = [Buf(f"tp{i}") for i in range(2)]
        pTs = [sb(f"tpT{i}", [128, c.PKC, TS], BF16, st) for i in range(2)]
        B_pTs = [Buf(f"tpT{i}") for i in range(2)]
        k = 0
        ev = 0
        for t in range(NTILE):
            for tg in range(4):
                i = k % 2
                k += 1
                r0 = t * TS + tg * 128
                S.op("sp", (lambda e, i=i, r0=r0: e.dma_start(out=xin[i][:], in_=x_in[r0:r0 + 128, :])),
                     writes=[B_xin[i]], pool=ld_pool)
                for f0 in range(0, FC, 4):
                    nf = min(4, FC - f0)
                    bk = next_bank()

                    def tr(e, i=i, f0=f0, nf=nf, bk=bk):
                        for a in range(nf):
                            ins = e.transpose(out=psb[bk][:, a * 128:(a + 1) * 128], in_=xin[i][:, (f0 + a) * 128:(f0 + a + 1) * 128], identity=cst("ident"))
                        return ins
                    S.op("pe", tr, reads=[B_xin[i], B_consts], writes=[B_ps[bk]])
                    outv = xTs[:, f0:f0 + nf, tg * 128:(tg + 1) * 128]
                    inv = psb[bk][:, 0:nf * 128].rearrange("p (a b) -> p a b", a=nf)
                    if ev % 2 == 0:
                        S.op("dve", (lambda e, outv=outv, inv=inv: e.tensor_copy(out=outv, in_=inv)), reads=[B_ps[bk]], writes=[B_xTs])
                    else:
                        S.op("act", (lambda e, outv=outv, inv=inv: e.activation(out=outv, in_=inv, func=AF.Copy)), reads=[B_ps[bk]], writes=[B_xTs])
                    ev += 1
            S.op("pool", (lambda e, t=t: e.dma_start(out=xT_d[t], in_=xTs[:])), reads=[B_xTs], writes=[B_xT[t]], pool=st_pool)
        for l in range(L):
            for t in range(NTILE):
                j = (l * NTILE + t) % 2
                for tg in range(4):
                    i = k % 2
                    k += 1
                    r0 = t * TS + tg * 128
                    S.op("sp", (lambda e, i=i, r0=r0, l=l: e.dma_start(out=pin[i][:], in_=p_in[l, r0:r0 + 128, :])),
                         writes=[B_pin[i]], pool=ld_pool)
                    bk = next_bank()

                    def tr(e, i=i, bk=bk):
                        for a in range(c.PKC):
                            ins = e.transpose(out=psb[bk][:, a * 128:(a + 1) * 128], in_=pin[i][:, a * 128:(a + 1) * 128], identity=cst("ident"))
                        return ins
                    S.op("pe", tr, reads=[B_pin[i], B_consts], writes=[B_ps[bk]])
                    outv = pTs[j][:, :, tg * 128:(tg + 1) * 128]
                    inv = psb[bk][:, 0:c.PKC * 128].rearrange("p (a b) -> p a b", a=c.PKC)
                    S.op("dve", (lambda e, outv=outv, inv=inv: e.tensor_copy(out=outv, in_=inv)), reads=[B_ps[bk]], writes=[B_pTs[j]])
                S.op("pool", (lambda e, l=l, t=t, j=j: e.dma_start(out=pT_d[l, t], in_=pTs[j][:])), reads=[B_pTs[j]], writes=[B_pT[l][t]], pool=st_pool)
        S.barrier()
        st.close()

    def norm_col(l, kind, fc):
        idx = (l * 3 + kind) if l < L else 3 * L
        o = idx * FC + fc
        return norms_sb[:, o:o + 1]

    qdf = sb("qdf", [128, H, 128], F32)
    qdb = sb("qdb", [128, H, 128], F32)
    Mt = sb("Mt", [128, H, 128], F32)
    kdf = sb("kdf", [128, H], F32)
    kdb = sb("kdb", [128, H], F32)
    cdf = sb("cdf", [128, H], F32)
    cdb = sb("cdb", [128, H], F32)
    lgf = sb("lgf", [128, H], F32)
    lgb = sb("lgb", [128, H], F32)
    e1 = sb("e1t", [128, 128], F32)
    e2 = sb("e2t", [128, 128], F32)
    B_dec = Buf("dec")
    zedge = sb("zedge", [128, CFC, NTILE, 2], F32)
    B_zedge = Buf("zedge")

    def layer_setup(l):
        o = l * 2 * H
        A = lambda fn, **kw: S.op("act", fn, reads=[B_par, B_consts, B_dec], writes=[B_dec])
        Dv = lambda fn, **kw: S.op("dve", fn, reads=[B_par, B_consts, B_dec], writes=[B_dec])
        A(lambda e: e.activation(out=lgf[:], in_=dec_sb[:, o:o + H], func=AF.Exp))
        A(lambda e: e.activation(out=lgb[:], in_=dec_sb[:, o + H:o + 2 * H], func=AF.Exp))
        Dv(lambda e: e.tensor_scalar(out=lgf[:], in0=lgf[:], scalar1=-1.0, scalar2=None, op0=ALU.mult))
        Dv(lambda e: e.tensor_scalar(out=lgb[:], in0=lgb[:], scalar1=-1.0, scalar2=None, op0=ALU.mult))
        A(lambda e: e.activation(out=cdf[:], in_=lgf[:], func=AF.Exp, scale=128.0))
        A(lambda e: e.activation(out=cdb[:], in_=lgb[:], func=AF.Exp, scale=128.0))
        for h in range(H):
            A(lambda e, h=h: e.activation(out=qdf[:, h, :], in_=cst("row_f"), func=AF.Exp, scale=lgf[:, h:h + 1]))
            A(lambda e, h=h: e.activation(out=qdb[:, h, :], in_=cst("row_b"), func=AF.Exp, scale=lgb[:, h:h + 1]))
            A(lambda e, h=h: e.activation(out=kdf[:, h:h + 1], in_=cst("col_f"), func=AF.Exp, scale=lgf[:, h:h + 1]))
            A(lambda e, h=h: e.activation(out=kdb[:, h:h + 1], in_=cst("col_b"), func=AF.Exp, scale=lgb[:, h:h + 1]))
            A(lambda e, h=h: e.activation(out=e1[:], in_=cst("dpos"), func=AF.Exp, scale=lgf[:, h:h + 1]))
            A(lambda e, h=h: e.activation(out=e2[:], in_=cst("dneg"), func=AF.Exp, scale=lgb[:, h:h + 1]))
            Dv(lambda e: e.tensor_tensor(out=e1[:], in0=e1[:], in1=cst("mf"), op=ALU.mult))
            Dv(lambda e: e.tensor_tensor(out=e2[:], in0=e2[:], in1=cst("mb"), op=ALU.mult))
            Dv(lambda e, h=h: e.tensor_tensor(out=Mt[:, h, :], in0=e1[:], in1=e2[:], op=ALU.add))

    B_qc = [[Buf(f"qc_{g}_{h}") for h in range(H)] for g in range(NCH)]
    B_kc = [[Buf(f"kc_{g}_{h}") for h in range(H)] for g in range(NCH)]
    B_kkf = [[Buf(f"kkf_{g}_{h}") for h in range(H)] for g in range(NCH)]
    B_v = [[Buf(f"v_{g}_{h}") for h in range(H)] for g in range(NCH)]
    B_sbd = [[Buf(f"sb_{g}_{h}") for h in range(H)] for g in range(NCH)]
    B_sg = [Buf(f"sg_{g}") for g in range(NCH)]
    B_z = [Buf(f"z_{t}") for t in range(NTILE)]
    B_cb = [Buf(f"cb_{t}") for t in range(NTILE)]
    B_mix = [Buf(f"mix_{t}") for t in range(NTILE)]

    def phaseA(l):
        st = ExitStack()
        XP = 8 if FC >= 8 else FC
        NXP = FC // XP
        xp = [sb(f"a_xp{i}", [128, XP, TS], F32, st) for i in range(2)]
        B_xp = [Buf(f"a_xp{i}") for i in range(2)]
        hT = sb("a_hT", [128, FC, TS], BF16, st)
        B_hT = Buf("a_hT")
        sqb = [sb(f"a_sq{i}", [128, TS], BF16, st) for i in range(2)]
        B_sq = [Buf(f"a_sq{i}") for i in range(2)]
        rstd = sb("a_rstd", [128, TS], F32, st)
        B_rstd = Buf("a_rstd")
        cs = [sb(f"a_cs{i}", [128, 2, TS], F32, st) for i in range(2)]
        B_cs = [Buf(f"a_cs{i}") for i in range(2)]
        AB = [sb(f"a_AB{i}", [128, 2, TS], F32, st) for i in range(2)]
        B_AB = [Buf(f"a_AB{i}") for i in range(2)]
        t12 = sb("a_t12", [128, 2, TS], F32, st)
        B_t12 = Buf("a_t12")
        t34 = sb("a_t34", [128, 2, TS], F32, st)
        B_t34 = Buf("a_t34")
        qst = [sb(f"a_q{i}", [128, 2, TS], BF16, st) for i in range(2)]
        B_qst = [Buf(f"a_q{i}") for i in range(2)]
        kTh = [sb(f"a_kT{i}", [128, 2, TS], BF16, st) for i in range(2)]
        B_kTh = [Buf(f"a_kT{i}") for i in range(2)]
        vTh = [sb(f"a_vT{i}", [128, 2, TS], BF16, st) for i in range(2)]
        B_vTh = [Buf(f"a_vT{i}") for i in range(2)]
        tm = [sb(f"a_tm{i}", [128, 4, 3, 256], BF16, st) for i in range(2)]
        B_tm = [Buf(f"a_tm{i}") for i in range(2)]
        S32 = sb("a_S32", [128, H, 2, 256], F32, st)
        B_S32 = [Buf(f"a_S32_{h}") for h in range(H)]
        Sbf = [sb(f"a_Sbf{i}", [128, 2, 256], BF16, st) for i in range(2)]
        B_Sbf = [Buf(f"a_Sbf{i}") for i in range(2)]
        sgs = [sb(f"a_sg{i}", [128, TS], BF16, st) for i in range(2)]
        B_sgs = [Buf(f"a_sg{i}") for i in range(2)]
        zst = [sb(f"a_z{i}", [128, TS], F32, st) for i in range(2)]
        B_zst = [Buf(f"a_z{i}") for i in range(2)]
        cbst = [sb(f"a_cb{i}", [128, TS], F32, st) for i in range(2)]
        B_cbst = [Buf(f"a_cb{i}") for i in range(2)]
        rr = {"xp": 0, "sq": 0, "cs": 0, "AB": 0, "q": 0, "kT": 0, "vT": 0, "tm": 0, "Sbf": 0, "sg": 0, "z": 0, "cb": 0}

        for h in range(H):
            S.op("pool", (lambda e, h=h: e.memset(S32[:, h, :, :], 0.0)), writes=[B_S32[h]])

        base = len(wpieces)
        B_src = None
        for t in range(NTILE):
            for oc in in_order:
                wpieces.append((wt_in[l, oc], [FC, 128], ("in", l, oc)))
        wi = [base]

        def proj(bk):
            i = wi[0]
            wi[0] += 1
            wap, bw = wget(i, B_src)

            def f(e, wap=wap, bk=bk):
                for kc in range(FC):
                    ins = e.matmul(psb[bk][:], wap[:, kc, :], hT[:, kc, :], start=(kc == 0), stop=(kc == FC - 1))
                return ins
            S.op("pe", f, reads=[bw, B_hT], writes=[B_ps[bk]])

        def rotary(ab, ci, out1, out2, Bout):
            A_ = AB[ab][:, 0, :]
            B_ = AB[ab][:, 1, :]
            cos_ = cs[ci][:, 0, :]
            sin_ = cs[ci][:, 1, :]
            S.op("dve", lambda e: e.tensor_tensor(out=t12[:, 0, :], in0=A_, in1=cos_, op=ALU.mult), reads=[B_AB[ab], B_cs[ci]], writes=[B_t12])
            S.op("dve", lambda e: e.tensor_tensor(out=t12[:, 1, :], in0=B_, in1=sin_, op=ALU.mult), reads=[B_AB[ab], B_cs[ci]], writes=[B_t12])
            S.op("dve", lambda e: e.tensor_tensor(out=out1, in0=t12[:, 0, :], in1=t12[:, 1, :], op=ALU.subtract), reads=[B_t12], writes=[Bout])
            S.op("pool", lambda e: e.tensor_tensor(out=t34[:, 0, :], in0=B_, in1=cos_, op=ALU.mult), reads=[B_AB[ab], B_cs[ci]], writes=[B_t34])
            S.op("pool", lambda e: e.tensor_tensor(out=t34[:, 1, :], in0=A_, in1=sin_, op=ALU.mult), reads=[B_AB[ab], B_cs[ci]], writes=[B_t34])
            S.op("pool", lambda e: e.tensor_tensor(out=out2, in0=t34[:, 0, :], in1=t34[:, 1, :], op=ALU.add), reads=[B_t34], writes=[Bout])

        for t in reversed(range(NTILE)):
            tok0 = t * TS
            g0 = t * 4
            ci = rr["cs"] % 2
            rr["cs"] += 1
            S.op("sp", (lambda e, ci=ci, tok0=tok0: e.dma_start(out=cs[ci][:, 0, :], in_=cos_d[:, tok0:tok0 + TS])), writes=[B_cs[ci]], pool=ld_pool)
            S.op("sp", (lambda e, ci=ci, tok0=tok0: e.dma_start(out=cs[ci][:, 1, :], in_=sin_d[:, tok0:tok0 + TS])), writes=[B_cs[ci]], pool=ld_pool)
            bk_n = next_bank()
            cntm = 0
            for pi in range(NXP):
                i = rr["xp"] % 2
                rr["xp"] += 1
                S.op("sp", (lambda e, i=i, t=t, pi=pi: e.dma_start(out=xp[i][:], in_=xT_d[t, :, pi * XP:(pi + 1) * XP, :])),
                     reads=[B_xT[t]], writes=[B_xp[i]], pool=ld_pool)
                for a in range(XP):
                    j = rr["sq"] % 2
                    rr["sq"] += 1
                    S.op("act", (lambda e, i=i, a=a, j=j: e.activation(out=sqb[j][:], in_=xp[i][:, a, :], func=AF.Square)), reads=[B_xp[i]], writes=[B_sq[j]])
                    first = (cntm == 0)
                    last = (cntm == FC - 1)
                    cntm += 1
                    S.op("pe", (lambda e, j=j, first=first, last=last, bk=bk_n: e.matmul(psb[bk][:], ones_bf[:], sqb[j][:], start=first, stop=last)),
                         reads=[B_sq[j], B_cbf], writes=[B_ps[bk_n]])
            S.op("act", (lambda e, bk=bk_n: e.activation(out=rstd[:], in_=psb[bk][:], func=AF.Sqrt, scale=1.0 / D, bias=NORM_EPS)),
                 reads=[B_ps[bk_n]], writes=[B_rstd])
            S.op("dve", (lambda e: e.reciprocal(out=rstd[:], in_=rstd[:])), reads=[B_rstd], writes=[B_rstd])
            for pi in range(NXP):
                i = rr["xp"] % 2
                rr["xp"] += 1
                S.op("sp", (lambda e, i=i, t=t, pi=pi: e.dma_start(out=xp[i][:], in_=xT_d[t, :, pi * XP:(pi + 1) * XP, :])),
                     reads=[B_xT[t]], writes=[B_xp[i]], pool=ld_pool)
                for a in range(XP):
                    fc = pi * XP + a
                    S.op("dve", (lambda e, i=i, a=a, fc=fc: e.scalar_tensor_tensor(out=hT[:, fc, :], in0=xp[i][:, a, :], scalar=norm_col(l, 0, fc), in1=rstd[:], op0=ALU.mult, op1=ALU.mult)),
                         reads=[B_xp[i], B_rstd, B_par], writes=[B_hT])
            def post1(h, ki, vi):
                ti = h % 2
                for cc_ in range(4):
                    bk = next_bank()
                    pv = psb[bk][:].bitcast(BF16)

                    def trk(e, cc_=cc_, ki=ki, vi=vi, pv=pv):
                        for half in range(2):
                            e.transpose(out=pv[:, half * 128:(half + 1) * 128], in_=kTh[ki][:, half, cc_ * 128:(cc_ + 1) * 128], identity=ident_bf[:])
                        for half in range(2):
                            ins = e.transpose(out=pv[:, 256 + half * 128:256 + (half + 1) * 128], in_=vTh[vi][:, half, cc_ * 128:(cc_ + 1) * 128], identity=ident_bf[:])
                        return ins
                    S.op("pe", trk, reads=[B_kTh[ki], B_vTh[vi], B_cbf], writes=[B_ps[bk]])
                    S.op("dve", (lambda e, ti=ti, cc_=cc_, pv=pv, h=h: e.tensor_scalar(out=tm[ti][:, cc_, 0, :], in0=pv[:, 0:256], scalar1=kdf[:, h:h + 1], scalar2=None, op0=ALU.mult)),
                         reads=[B_ps[bk], B_dec], writes=[B_tm[ti]])
                    S.op("dve", (lambda e, ti=ti, cc_=cc_, pv=pv, h=h: e.tensor_scalar(out=tm[ti][:, cc_, 1, :], in0=pv[:, 0:256], scalar1=kdb[:, h:h + 1], scalar2=None, op0=ALU.mult)),
                         reads=[B_ps[bk], B_dec], writes=[B_tm[ti]])
                    S.op("act", (lambda e, ti=ti, cc_=cc_, pv=pv: e.activation(out=tm[ti][:, cc_, 2, :], in_=pv[:, 256:512], func=AF.Copy)),
                         reads=[B_ps[bk]], writes=[B_tm[ti]])
                S.op("pool", (lambda e, ti=ti, g0=g0, h=h: e.dma_start(out=kkf_d[g0:g0 + 4, h].rearrange("c p d -> p c d"), in_=tm[ti][:, :, 0, :])),
                     reads=[B_tm[ti]], writes=[B_kkf[g0 + a][h] for a in range(4)], pool=st_pool)
                S.op("pool", (lambda e, ti=ti, g0=g0, h=h: e.dma_start(out=v_d[g0:g0 + 4, h].rearrange("c p d -> p c d"), in_=tm[ti][:, :, 2, :])),
                     reads=[B_tm[ti]], writes=[B_v[g0 + a][h] for a in range(4)], pool=st_pool)
            def post2(h):
                ti = h % 2
                for cc_ in reversed(range(4)):
                    g = g0 + cc_
                    si = rr["Sbf"] % 2
                    rr["Sbf"] += 1
                    if g == c.HALF_CH - 1:
                        S.op("dve", (lambda e, h=h: e.tensor_scalar(out=S32[:, h, :, :], in0=S32[:, h, :, :], scalar1=link_sb[:, 0:1], scalar2=None, op0=ALU.mult)),
                             reads=[B_S32[h], B_par], writes=[B_S32[h]])
                    S.op("act", (lambda e, si=si, h=h: e.activation(out=Sbf[si][:], in_=S32[:, h, :, :], func=AF.Copy)),
                         reads=[B_S32[h]], writes=[B_Sbf[si]])
                    S.op("pool", (lambda e, si=si, g=g, h=h: e.dma_start(out=sb_d[g, h], in_=Sbf[si][:])), reads=[B_Sbf[si]], writes=[B_sbd[g][h]], pool=st_pool)
                    bk = next_bank()

                    def upd(e, ti=ti, cc_=cc_, bk=bk):
                        for dh in range(2):
                            ins = e.matmul(psb[bk][:, dh * 256:(dh + 1) * 256], tm[ti][:, cc_, 1, dh * 128:(dh + 1) * 128], tm[ti][:, cc_, 2, :], start=True, stop=True)
                        return ins
                    S.op("pe", upd, reads=[B_tm[ti]], writes=[B_ps[bk]])
                    S.op("dve", (lambda e, h=h, bk=bk: e.scalar_tensor_tensor(out=S32[:, h, :, :], in0=S32[:, h, :, :], scalar=cdb[:, h:h + 1],
                                                                              in1=psb[bk][:].rearrange("p (a b) -> p a b", a=2), op0=ALU.mult, op1=ALU.add)),
                         reads=[B_S32[h], B_ps[bk], B_dec], writes=[B_S32[h]])

            pend = []
            for h in range(H):
                ab = rr["AB"] % 2
                rr["AB"] += 1
                qi = rr["q"] % 2
                rr["q"] += 1
                for half in range(2):
                    bk = next_bank()
                    proj(bk)
                    S.op("act", (lambda e, ab=ab, half=half, bk=bk: e.activation(out=AB[ab][:, half, :], in_=psb[bk][:], func=AF.Copy)),
                         reads=[B_ps[bk]], writes=[B_AB[ab]])
                rotary(ab, ci, qst[qi][:, 0, :], qst[qi][:, 1, :], B_qst[qi])
                for half in range(2):
                    S.op("pool", (lambda e, g0=g0, h=h, qi=qi, half=half: e.dma_start(out=qc_d[g0:g0 + 4, :, h, half, :].rearrange("c p i -> p c i"),
                                                                                      in_=qst[qi][:, half, :].rearrange("p (c i) -> p c i", c=4))),
                         reads=[B_qst[qi]], writes=[B_qc[g0 + a][h] for a in range(4)], pool=st_pool)
                ab = rr["AB"] % 2
                rr["AB"] += 1
                ki = rr["kT"] % 2
                rr["kT"] += 1
                for half in range(2):
                    bk = next_bank()
                    proj(bk)
                    S.op("act", (lambda e, ab=ab, half=half, bk=bk: e.activation(out=AB[ab][:, half, :], in_=psb[bk][:], func=AF.Copy, scale=float(c.HD ** -0.5))),
                         reads=[B_ps[bk]], writes=[B_AB[ab]])
                rotary(ab, ci, kTh[ki][:, 0, :], kTh[ki][:, 1, :], B_kTh[ki])
                for half in range(2):
                    S.op("pool", (lambda e, g0=g0, h=h, ki=ki, half=half: e.dma_start(out=kc_d[g0:g0 + 4, :, h, half, :].rearrange("c p i -> p c i"),
                                                                                      in_=kTh[ki][:, half, :].rearrange("p (c i) -> p c i", c=4))),
                         reads=[B_kTh[ki]], writes=[B_kc[g0 + a][h] for a in range(4)], pool=st_pool)
                if pend:
                    post1(*pend[-1])
                vi = rr["vT"] % 2
                rr["vT"] += 1
                for half in range(2):
                    bk = next_bank()
                    proj(bk)
                    S.op("act", (lambda e, vi=vi, half=half, bk=bk: e.activation(out=vTh[vi][:, half, :], in_=psb[bk][:], func=AF.Copy)),
                         reads=[B_ps[bk]], writes=[B_vTh[vi]])
                if pend:
                    post2(pend[-1][0])
                    pend.pop()
                pend.append((h, ki, vi))
            for fc in range(RFC):
                if fc == 0:
                    post1(*pend[-1])
                if fc == min(2, RFC - 1):
                    post2(pend[-1][0])
                    pend.pop()
                bk = next_bank()
                proj(bk)
                gi = rr["sg"] % 2
                rr["sg"] += 1
                S.op("act", (lambda e, gi=gi, bk=bk: e.activation(out=sgs[gi][:], in_=psb[bk][:], func=AF.Silu)), reads=[B_ps[bk]], writes=[B_sgs[gi]])
                S.op("pool", (lambda e, gi=gi, g0=g0, fc=fc: e.dma_start(out=sg_d[g0:g0 + 4, :, fc, :].rearrange("c p i -> p c i"), in_=sgs[gi][:].rearrange("p (c i) -> p c i", c=4))),
                     reads=[B_sgs[gi]], writes=[B_sg[g0 + a] for a in range(4)], pool=st_pool)
            for fc in range(CFC):
                bk = next_bank()
                proj(bk)
                bi = rr["cb"] % 2
                rr["cb"] += 1
                S.op("act", (lambda e, bi=bi, bk=bk: e.activation(out=cbst[bi][:], in_=psb[bk][:], func=AF.Copy)), reads=[B_ps[bk]], writes=[B_cbst[bi]])
                S.op("pool", (lambda e, bi=bi, t=t, fc=fc: e.dma_start(out=cb_d[t, :, fc, :], in_=cbst[bi][:])), reads=[B_cbst[bi]], writes=[B_cb[t]], pool=st_pool)
                bk = next_bank()
                proj(bk)
                zi = rr["z"] % 2
                rr["z"] += 1
                S.op("act", (lambda e, zi=zi, bk=bk: e.activation(out=zst[zi][:], in_=psb[bk][:], func=AF.Copy)), reads=[B_ps[bk]], writes=[B_zst[zi]])
                bk = next_bank()
                proj(bk)
                S.op("dve", (lambda e, zi=zi, bk=bk: e.tensor_tensor(out=zst[zi][:], in0=psb[bk][:], in1=zst[zi][:], op=ALU.mult)), reads=[B_ps[bk], B_zst[zi]], writes=[B_zst[zi]])
                S.op("dve", (lambda e, zi=zi, fc=fc, t=t: e.tensor_copy(out=zedge[:, fc, t, 0:1], in_=zst[zi][:, 0:1])), reads=[B_zst[zi]], writes=[B_zedge])
                S.op("dve", (lambda e, zi=zi, fc=fc, t=t: e.tensor_copy(out=zedge[:, fc, t, 1:2], in_=zst[zi][:, TS - 1:TS])), reads=[B_zst[zi]], writes=[B_zedge])
                S.op("pool", (lambda e, zi=zi, t=t, fc=fc: e.dma_start(out=z_d[t, :, fc, :], in_=zst[zi][:])), reads=[B_zst[zi]], writes=[B_z[t]], pool=st_pool)
        assert wi[0] == len(wpieces)
        S.barrier()
        st.close()

    def phaseC1(l):
        st = ExitStack()
        qcs = [sb(f"c_q{i}", [128, H, 2, 128], BF16, st) for i in range(2)]
        kcs = [sb(f"c_k{i}", [128, H, 2, 128], BF16, st) for i in range(2)]
        qts = [sb(f"c_qt{i}", [128, 2, H, 2, 128], BF16, st) for i in range(2)]
        sgc = [sb(f"c_sg{i}", [128, RFC, 128], BF16, st) for i in range(2)]
        vs = [sb(f"c_v{i}", [128, H, 256], BF16, st) for i in range(2)]
        kks = [sb(f"c_kk{i}", [128, H, 256], BF16, st) for i in range(2)]
        sbs = [sb(f"c_sb{i}", [128, H, 2, 256], BF16, st) for i in range(2)]
        B_in = [Buf(f"c_in{i}") for i in range(2)]
        B_qt = [Buf(f"c_qt{i}") for i in range(2)]
        zt = [sb(f"c_z{i}", [128, TS + 2], F32, st) for i in range(2)]
        B_zt = [Buf(f"c_z{i}") for i in range(2)]
        cbt = [sb(f"c_cb{i}", [128, TS], F32, st) for i in range(2)]
        B_cbt = [Buf(f"c_cb{i}") for i in range(2)]
        cvo = [sb(f"c_cvo{i}", [128, TS], BF16, st) for i in range(2)]
        B_cvo = [Buf(f"c_cvo{i}") for i in range(2)]
        yt = sb("c_y", [128, TS], F32, st)
        B_yt = Buf("c_y")
        mixc = [sb(f"c_mix{i}", [128, RFC, 128], BF16, st) for i in range(2)]
        B_mixc = [Buf(f"c_mix{i}") for i in range(2)]
        sTm = sb("c_sTm", [128, H, 128], BF16, st)
        B_sTm = Buf("c_sTm")
        S32 = sb("c_S32", [128, H, 2, 256], F32, st)
        Sbf = sb("c_Sbf", [128, H, 2, 256], BF16, st)
        B_S32 = [Buf(f"c_S32_{h}") for h in range(H)]
        B_Sbf = [Buf(f"c_Sbf_{h}") for h in range(H)]
        stats = sb("c_stats", [128, H, 6], F32, st)
        mv = sb("c_mv", [128, H, 2], F32, st)
        rs = sb("c_rs", [128, H], F32, st)
        nb = sb("c_nb", [128, H], F32, st)
        B_stat = [Buf(f"c_stat{h}") for h in range(H)]
        B_rs = Buf("c_rs")
        on = sb("c_on", [128, H, 256], BF16, st)
        B_on = Buf("c_on")
        for h in range(H):
            S.op("pool", (lambda e, h=h: e.memset(S32[:, h, :, :], 0.0)), writes=[B_S32[h]])
            S.op("pool", (lambda e, h=h: e.memset(Sbf[:, h, :, :], 0.0)), writes=[B_Sbf[h]])
        rrv = 0
        rrz = 0
        for t in range(NTILE):
            for cc_ in range(4):
                g = t * 4 + cc_
                vi = rrv % 2
                rrv += 1
                csl = slice(cc_ * 128, (cc_ + 1) * 128)
                S.op("sp", (lambda e, vi=vi, g=g: e.dma_start(out=qcs[vi][:], in_=qc_d[g])), reads=B_qc[g], writes=[B_in[vi]], pool=ld_pool)
                S.op("sp", (lambda e, vi=vi, g=g: e.dma_start(out=kcs[vi][:], in_=kc_d[g])), reads=B_kc[g], writes=[B_in[vi]], pool=ld_pool)
                S.op("sp", (lambda e, vi=vi, g=g: e.dma_start(out=sgc[vi][:], in_=sg_d[g])), reads=[B_sg[g]], writes=[B_in[vi]], pool=ld_pool)
                S.op("sp", (lambda e, vi=vi, g=g: e.dma_start(out=vs[vi][:], in_=v_d[g].rearrange("h p d -> p h d"))), reads=B_v[g], writes=[B_in[vi]], pool=ld_pool)
                S.op("sp", (lambda e, vi=vi, g=g: e.dma_start(out=kks[vi][:], in_=kkf_d[g].rearrange("h p d -> p h d"))), reads=B_kkf[g], writes=[B_in[vi]], pool=ld_pool)
                S.op("sp", (lambda e, vi=vi, g=g: e.dma_start(out=sbs[vi][:], in_=sb_d[g].rearrange("h p a d -> p h a d"))), reads=B_sbd[g], writes=[B_in[vi]], pool=ld_pool)
                for half in range(2):
                    S.op("dve", (lambda e, vi=vi, half=half: e.tensor_tensor(out=qts[vi][:, 0, :, half, :], in0=qcs[vi][:, :, half, :], in1=qdf[:], op=ALU.mult)),
                         reads=[B_in[vi], B_dec], writes=[B_qt[vi]])
                    S.op("pool", (lambda e, vi=vi, half=half: e.tensor_tensor(out=qts[vi][:, 1, :, half, :], in0=qcs[vi][:, :, half, :], in1=qdb[:], op=ALU.mult)),
                         reads=[B_in[vi], B_dec], writes=[B_qt[vi]])
                if g == c.HALF_CH:
                    for h in range(H):
                        S.op("dve", (lambda e, h=h: e.tensor_scalar(out=S32[:, h, :, :], in0=S32[:, h, :, :], scalar1=link_sb[:, 0:1], scalar2=None, op0=ALU.mult)),
                             reads=[B_S32[h], B_par], writes=[B_S32[h]])
                        S.op("act", (lambda e, h=h: e.activation(out=Sbf[:, h, :, :], in_=S32[:, h, :, :], func=AF.Copy)), reads=[B_S32[h]], writes=[B_Sbf[h]])
                HB = min(H, 4)
                for hb in range(0, H, HB):
                    bk = next_bank()

                    def sc(e, hb=hb, bk=bk, vi=vi):
                        for a in range(HB):
                            h = hb + a
                            for half in range(2):
                                ins = e.matmul(psb[bk][:, a * 128:(a + 1) * 128], kcs[vi][:, h, half, :], qcs[vi][:, h, half, :], start=(half == 0), stop=(half == 1))
                        return ins
                    S.op("pe", sc, reads=[B_in[vi]], writes=[B_ps[bk]])
                    S.op("dve", (lambda e, hb=hb, bk=bk: e.tensor_tensor(out=sTm[:, hb:hb + HB, :], in0=psb[bk][:, 0:HB * 128].rearrange("p (a b) -> p a b", a=HB), in1=Mt[:, hb:hb + HB, :], op=ALU.mult)),
                         reads=[B_ps[bk], B_dec], writes=[B_sTm])
                obank = {}
                for hp in range(0, H, 2):
                    bk = next_bank()

                    def om(e, hp=hp, bk=bk, vi=vi):
                        for a in range(2):
                            h = hp + a
                            o_ = psb[bk][:, a * 256:(a + 1) * 256]
                            e.matmul(o_, sTm[:, h, :], vs[vi][:, h, :], start=True, stop=False)
                            e.matmul(o_, qts[vi][:, 0, h, 0, :], Sbf[:, h, 0, :], start=False, stop=False)
                            e.matmul(o_, qts[vi][:, 0, h, 1, :], Sbf[:, h, 1, :], start=False, stop=False)
                            e.matmul(o_, qts[vi][:, 1, h, 0, :], sbs[vi][:, h, 0, :], start=False, stop=False)
                            ins = e.matmul(o_, qts[vi][:, 1, h, 1, :], sbs[vi][:, h, 1, :], start=False, stop=True)
                        return ins
                    S.op("pe", om, reads=[B_sTm, B_in[vi], B_qt[vi], B_Sbf[hp], B_Sbf[hp + 1]], writes=[B_ps[bk]])
                    for a in range(2):
                        h = hp + a
                        obank[h] = (bk, a)
                        S.op("dve", (lambda e, h=h, a=a, bk=bk: e.bn_stats(out=stats[:, h, :], in_=psb[bk][:, a * 256:(a + 1) * 256])), reads=[B_ps[bk], B_rs], writes=[B_stat[h]])
                        S.op("dve", (lambda e, h=h: e.bn_aggr(out=mv[:, h, :], in_=stats[:, h, :])), reads=[B_stat[h]], writes=[B_stat[h]])
                S.op("act", (lambda e: e.activation(out=rs[:], in_=mv[:, :, 1], func=AF.Sqrt, scale=1.0, bias=NORM_EPS)), reads=B_stat, writes=[B_rs])
                S.op("dve", (lambda e: e.reciprocal(out=rs[:], in_=rs[:])), reads=[B_rs], writes=[B_rs])
                S.op("dve", (lambda e: e.scalar_tensor_tensor(out=nb[:], in0=mv[:, :, 0], scalar=-1.0, in1=rs[:], op0=ALU.mult, op1=ALU.mult)), reads=B_stat + [B_rs], writes=[B_rs])
                for h in range(H):
                    bk, a = obank[h]
                    S.op("act", (lambda e, h=h, a=a, bk=bk: e.activation(out=on[:, h, :], in_=psb[bk][:, a * 256:(a + 1) * 256], func=AF.Identity, scale=rs[:, h:h + 1], bias=nb[:, h:h + 1])),
                         reads=[B_ps[bk], B_rs], writes=[B_on])
                for h in range(H):
                    bk = next_bank()

                    def upd(e, h=h, bk=bk, vi=vi):
                        for dh in range(2):
                            ins = e.matmul(psb[bk][:, dh * 256:(dh + 1) * 256], kks[vi][:, h, dh * 128:(dh + 1) * 128], vs[vi][:, h, :], start=True, stop=True)
                        return ins
                    S.op("pe", upd, reads=[B_in[vi]], writes=[B_ps[bk]])
                    S.op("dve", (lambda e, h=h, bk=bk: e.scalar_tensor_tensor(out=S32[:, h, :, :], in0=S32[:, h, :, :], scalar=cdf[:, h:h + 1],
                                                                              in1=psb[bk][:].rearrange("p (a b) -> p a b", a=2), op0=ALU.mult, op1=ALU.add)),
                         reads=[B_S32[h], B_ps[bk], B_dec], writes=[B_S32[h]])
                    S.op("pool", (lambda e, h=h: e.tensor_copy(out=Sbf[:, h, :, :], in_=S32[:, h, :, :])), reads=[B_S32[h]], writes=[B_Sbf[h]])
                mi = g % 2
                for f0 in range(0, RFC, 8):
                    nf = min(8, RFC - f0)
                    bk = next_bank()
                    pv = psb[bk][:].bitcast(BF16)

                    def trf(e, f0=f0, nf=nf, pv=pv):
                        for a in range(nf):
                            fc = f0 + a
                            ins = e.transpose(out=pv[:, a * 128:(a + 1) * 128], in_=on[:, fc // 2, (fc % 2) * 128:(fc % 2 + 1) * 128], identity=ident_bf[:])
                        return ins
                    S.op("pe", trf, reads=[B_on, B_cbf], writes=[B_ps[bk]])
                    S.op("dve", (lambda e, f0=f0, nf=nf, pv=pv, mi=mi, vi=vi: e.tensor_tensor(out=mixc[mi][:, f0:f0 + nf, :], in0=pv[:, 0:nf * 128].rearrange("p (a b) -> p a b", a=nf), in1=sgc[vi][:, f0:f0 + nf, :], op=ALU.mult)),
                         reads=[B_ps[bk], B_in[vi]], writes=[B_mixc[mi]])
                S.op("pool", (lambda e, t=t, mi=mi, csl=csl: e.dma_start(out=mix_d[t, :, 0:RFC, csl], in_=mixc[mi][:])), reads=[B_mixc[mi]], writes=[B_mix[t]], pool=st_pool)
                for fc in range(cc_ * CFC // 4, (cc_ + 1) * CFC // 4):
                    zi = rrz % 2
                    rrz += 1
                    S.op("sp", (lambda e, t=t, fc=fc, zi=zi: e.dma_start(out=zt[zi][:, 1:TS + 1], in_=z_d[t, :, fc, :])), reads=[B_z[t]], writes=[B_zt[zi]], pool=ld_pool)
                    S.op("sp", (lambda e, t=t, fc=fc, zi=zi: e.dma_start(out=cbt[zi][:], in_=cb_d[t, :, fc, :])), reads=[B_cb[t]], writes=[B_cbt[zi]], pool=ld_pool)
                    if t == 0:
                        S.op("dve", (lambda e, zi=zi: e.memset(zt[zi][:, 0:1], 0.0)), writes=[B_zt[zi]])
                    elif t == c.HALF_TILE:
                        S.op("dve", (lambda e, zi=zi, fc=fc, t=t: e.tensor_scalar(out=zt[zi][:, 0:1], in0=zedge[:, fc, t - 1, 1:2], scalar1=link_sb[:, 0:1], scalar2=None, op0=ALU.mult)), reads=[B_zedge, B_par], writes=[B_zt[zi]])
                    else:
                        S.op("dve", (lambda e, zi=zi, fc=fc, t=t: e.tensor_copy(out=zt[zi][:, 0:1], in_=zedge[:, fc, t - 1, 1:2])), reads=[B_zedge], writes=[B_zt[zi]])
                    if t == NTILE - 1:
                        S.op("dve", (lambda e, zi=zi: e.memset(zt[zi][:, TS + 1:TS + 2], 0.0)), writes=[B_zt[zi]])
                    elif t == c.HALF_TILE - 1:
                        S.op("dve", (lambda e, zi=zi, fc=fc, t=t: e.tensor_scalar(out=zt[zi][:, TS + 1:TS + 2], in0=zedge[:, fc, t + 1, 0:1], scalar1=link_sb[:, 0:1], scalar2=None, op0=ALU.mult)), reads=[B_zedge, B_par], writes=[B_zt[zi]])
                    else:
                        S.op("dve", (lambda e, zi=zi, fc=fc, t=t: e.tensor_copy(out=zt[zi][:, TS + 1:TS + 2], in_=zedge[:, fc, t + 1, 0:1])), reads=[B_zedge], writes=[B_zt[zi]])

                    def wcol(tap, fc=fc):
                        o = (l * 3 + tap) * CFC + fc
                        return convw_sb[:, o:o + 1]
                    S.op("pool", (lambda e, zi=zi, wcol=wcol: e.tensor_scalar(out=yt[:], in0=zt[zi][:, 0:TS], scalar1=wcol(0), scalar2=None, op0=ALU.mult)), reads=[B_zt[zi], B_par], writes=[B_yt])
                    S.op("dve", (lambda e, zi=zi, wcol=wcol: e.scalar_tensor_tensor(out=yt[:], in0=zt[zi][:, 1:TS + 1], scalar=wcol(1), in1=yt[:], op0=ALU.mult, op1=ALU.add)), reads=[B_zt[zi], B_par, B_yt], writes=[B_yt])
                    S.op("dve", (lambda e, zi=zi, wcol=wcol: e.scalar_tensor_tensor(out=yt[:], in0=zt[zi][:, 2:TS + 2], scalar=wcol(2), in1=yt[:], op0=ALU.mult, op1=ALU.add)), reads=[B_zt[zi], B_par, B_yt], writes=[B_yt])
                    S.op("pool", (lambda e, zi=zi: e.tensor_tensor(out=cvo[zi][:], in0=yt[:], in1=cbt[zi][:], op=ALU.mult)), reads=[B_yt, B_cbt[zi]], writes=[B_cvo[zi]])
                    S.op("pool", (lambda e, t=t, fc=fc, zi=zi: e.dma_start(out=mix_d[t, :, RFC + fc, :], in_=cvo[zi][:])), reads=[B_cvo[zi]], writes=[B_mix[t]], pool=st_pool)
        S.barrier()
        st.close()

    def phaseC2(l):
        last = (l == L - 1)
        st = ExitStack()
        xT = sb("d_xT", [128, FC, TS], F32, st)
        B_x = [Buf(f"d_x{fc}") for fc in range(FC)]
        act = sb("d_act", [128, FC, TS], BF16, st)
        NQ = 4
        QF = FC // NQ
        B_actq = [Buf(f"d_act{i}") for i in range(NQ)]
        GK = c.FG // 128
        hid = [sb(f"d_hid{i}", [128, GK, TS], BF16, st) for i in range(2)]
        B_hidc = [[Buf(f"d_hid{i}_{a}") for a in range(GK)] for i in range(2)]
        sqb = [sb(f"d_sq{i}", [128, TS], BF16, st) for i in range(2)]
        B_sq = [Buf(f"d_sq{i}") for i in range(2)]
        rstd = sb("d_rstd", [128, TS], F32, st)
        B_rstd = Buf("d_rstd")
        rl = [sb(f"d_rl{i}", [128, TS], F32, st) for i in range(2)]
        B_rl = [Buf(f"d_rl{i}") for i in range(2)]
        sig = [sb(f"d_sig{i}", [128, TS], F32, st) for i in range(2)]
        B_sig = [Buf(f"d_sig{i}") for i in range(2)]
        pTs = sb("d_pT", [128, c.PKC, TS], BF16, st)
        B_pTs = Buf("d_pT")
        plew = sb("d_plew", [128, c.PKC, D], BF16, st)
        B_plew = Buf("d_plew")
        YW = min(D, 2048)
        if last:
            ysb = sb("d_y", [128, YW], F32, st)
            B_ysb = Buf("d_y")
        rr = {"sq": 0, "rl": 0, "sig": 0, "hid": 0}

        S.bg_flush_to(("ple", l))
        S.op("sp", (lambda e: e.dma_start(out=plew[:], in_=wt_ple[l])), reads=[B_wtp[("ple", l)]], writes=[B_plew], pool=ld_pool)

        base = len(wpieces)
        B_src = None
        for t in range(NTILE):
            for oc in range(FC):
                wpieces.append((wt_out[l, oc], [FC, 128], ("out", l, oc)))
            for kind, g in ffn_order():
                if kind == "ff1":
                    for a in range(GK):
                        wpieces.append((wt_ff1[l, g * GK + a], [FC, 128], ("ff1", l, g * GK + a)))
                else:
                    for q in range(D // 512):
                        wpieces.append((wt_ff2[l, g, q], [GK, 512], ("ff2", l, g, q)))
            for oc in range(FC):
                wpieces.append((wt_gate[l, oc], [FC, 128], ("gate", l, oc)))
        wi = [base]

        def nextw():
            i = wi[0]
            wi[0] += 1
            return wget(i, B_src)

        def norm_to(kind_l, kind, to_act):
            bk = next_bank()
            for fc in range(FC):
                j = rr["sq"] % 2
                rr["sq"] += 1
                S.op("act", (lambda e, fc=fc, j=j: e.activation(out=sqb[j][:], in_=xT[:, fc, :], func=AF.Square)), reads=[B_x[fc]], writes=[B_sq[j]])
                S.op("pe", (lambda e, j=j, fc=fc, bk=bk: e.matmul(psb[bk][:], ones_bf[:], sqb[j][:], start=(fc == 0), stop=(fc == FC - 1))),
                     reads=[B_sq[j], B_cbf], writes=[B_ps[bk]])
            S.op("act", (lambda e, bk=bk: e.activation(out=rstd[:], in_=psb[bk][:], func=AF.Sqrt, scale=1.0 / D, bias=NORM_EPS)),
                 reads=[B_ps[bk]], writes=[B_rstd])
            S.op("dve", (lambda e: e.reciprocal(out=rstd[:], in_=rstd[:])), reads=[B_rstd], writes=[B_rstd])
            for fc in range(FC):
                if not to_act:
                    S.op("dve", (lambda e, fc=fc: e.scalar_tensor_tensor(out=xT[:, fc, :], in0=xT[:, fc, :], scalar=norm_col(kind_l, kind, fc), in1=rstd[:], op0=ALU.mult, op1=ALU.mult)),
                         reads=[B_x[fc], B_rstd, B_par], writes=[B_x[fc]])
                else:
                    S.op("dve", (lambda e, fc=fc: e.scalar_tensor_tensor(out=act[:, fc, :], in0=xT[:, fc, :], scalar=norm_col(kind_l, kind, fc), in1=rstd[:], op0=ALU.mult, op1=ALU.mult)),
                         reads=[B_x[fc], B_rstd, B_par], writes=[B_actq[fc // QF]])

        def bigmm(wap, bk, n_kc, rhs_tile, col0=None, k0=0, k1=None):
            k1 = n_kc if k1 is None else k1

            def f(e):
                for kc in range(k0, k1):
                    lhs = wap[:, kc, :] if col0 is None else wap[:, kc, col0:col0 + 128]
                    ins = e.matmul(psb[bk][:], lhs, rhs_tile[:, kc, :], start=(kc == 0), stop=(kc == n_kc - 1))
                return ins
            return f

        def actmm(wap, bw, bk):
            for qi_ in range(NQ):
                S.op("pe", bigmm(wap, bk, FC, act, k0=qi_ * QF, k1=(qi_ + 1) * QF), reads=[bw, B_actq[qi_]], writes=[B_ps[bk]])

        XS = min(8, FC)
        for t in range(NTILE):
            for f0 in range(0, FC, 8):
                nf = min(8, FC - f0)
                S.op("sp", (lambda e, t=t, f0=f0, nf=nf: e.dma_start(out=xT[:, f0:f0 + nf, :], in_=xT_d[t, :, f0:f0 + nf, :])),
                     reads=[B_xT[t]], writes=B_x[f0:f0 + nf], pool=ld_pool)
            S.op("sp", (lambda e, t=t: e.dma_start(out=act[:], in_=mix_d[t])), reads=[B_mix[t]], writes=B_actq, pool=ld_pool)
            S.op("sp", (lambda e, t=t: e.dma_start(out=pTs[:], in_=pT_d[l, t])), reads=[B_pT[l][t]], writes=[B_pTs], pool=ld_pool)
            for oc in range(FC):
                wap, bw = nextw()
                bk = next_bank()
                actmm(wap, bw, bk)
                S.op("dve", (lambda e, oc=oc, bk=bk: e.tensor_tensor(out=xT[:, oc, :], in0=psb[bk][:], in1=xT[:, oc, :], op=ALU.add)), reads=[B_ps[bk], B_x[oc]], writes=[B_x[oc]])
            norm_to(l, 1, True)
            for kind, g in ffn_order():
                hi = g % 2
                if kind == "ff1":
                    for a in range(GK):
                        wap, bw = nextw()
                        bk = next_bank()
                        actmm(wap, bw, bk)
                        ri = rr["rl"] % 2
                        rr["rl"] += 1
                        S.op("act", (lambda e, ri=ri, bk=bk: e.activation(out=rl[ri][:], in_=psb[bk][:], func=AF.Relu)), reads=[B_ps[bk]], writes=[B_rl[ri]])
                        S.op("pool", (lambda e, ri=ri, hi=hi, a=a: e.tensor_tensor(out=hid[hi][:, a, :], in0=rl[ri][:], in1=rl[ri][:], op=ALU.mult)), reads=[B_rl[ri]], writes=[B_hidc[hi][a]])
                else:
                    for q in range(D // 512):
                        wap, bw = nextw()
                        for a in range(4):
                            oc = q * 4 + a
                            bk = next_bank()
                            S.op("pe", bigmm(wap, bk, GK, hid[hi], col0=a * 128), reads=[bw] + B_hidc[hi], writes=[B_ps[bk]])
                            S.op("dve", (lambda e, oc=oc, bk=bk: e.tensor_tensor(out=xT[:, oc, :], in0=psb[bk][:], in1=xT[:, oc, :], op=ALU.add)), reads=[B_ps[bk], B_x[oc]], writes=[B_x[oc]])
            norm_to(l, 2, True)
            for oc in range(FC):
                wap, bw = nextw()
                bk = next_bank()
                actmm(wap, bw, bk)
                si = rr["sig"] % 2
                rr["sig"] += 1
                S.op("act", (lambda e, si=si, bk=bk: e.activation(out=sig[si][:], in_=psb[bk][:], func=AF.Sigmoid)), reads=[B_ps[bk]], writes=[B_sig[si]])
                bk2 = next_bank()
                S.op("pe", bigmm(plew, bk2, c.PKC, pTs, col0=oc * 128), reads=[B_plew, B_pTs], writes=[B_ps[bk2]])
                S.op("dve", (lambda e, si=si, bk2=bk2: e.tensor_tensor(out=sig[si][:], in0=psb[bk2][:], in1=sig[si][:], op=ALU.mult)), reads=[B_ps[bk2], B_sig[si]], writes=[B_sig[si]])
                S.op("pool", (lambda e, si=si, oc=oc: e.tensor_tensor(out=xT[:, oc, :], in0=xT[:, oc, :], in1=sig[si][:], op=ALU.add)), reads=[B_sig[si], B_x[oc]], writes=[B_x[oc]])
                if not last and (oc + 1) % XS == 0:
                    f0 = oc + 1 - XS
                    S.op("pool", (lambda e, t=t, f0=f0: e.dma_start(out=xT_d[t, :, f0:f0 + XS, :], in_=xT[:, f0:f0 + XS, :])), reads=B_x[f0:f0 + XS], writes=[B_xT[t]], pool=st_pool)
            if last:
                norm_to(L, 0, False)
                for tg in range(4):
                    for y0 in range(0, D, YW):
                        for f0 in range(y0 // 128, (y0 + YW) // 128, 4):
                            nf = min(4, FC - f0)
                            bk = next_bank()

                            def tr(e, f0=f0, nf=nf, bk=bk, tg=tg):
                                for a in range(nf):
                                    ins = e.transpose(out=psb[bk][:, a * 128:(a + 1) * 128], in_=xT[:, f0 + a, tg * 128:(tg + 1) * 128], identity=cst("ident"))
                                return ins
                            S.op("pe", tr, reads=B_x[f0:f0 + nf] + [B_consts], writes=[B_ps[bk]])
                            c0 = f0 * 128 - y0
                            if (f0 // 4) % 2 == 0:
                                S.op("act", (lambda e, c0=c0, nf=nf, bk=bk: e.activation(out=ysb[:, c0:c0 + nf * 128], in_=psb[bk][:, 0:nf * 128], func=AF.Copy)), reads=[B_ps[bk]], writes=[B_ysb])
                            else:
                                S.op("dve", (lambda e, c0=c0, nf=nf, bk=bk: e.tensor_copy(out=ysb[:, c0:c0 + nf * 128], in_=psb[bk][:, 0:nf * 128])), reads=[B_ps[bk]], writes=[B_ysb])
                        r0 = t * TS + tg * 128
                        o = S.op("pool", (lambda e, r0=r0, y0=y0: e.dma_start(out=y_out[r0:r0 + 128, y0:y0 + YW], in_=ysb[:])), reads=[B_ysb], pool=st_pool)
                        out_store_ops.append(o)
        assert wi[0] == len(wpieces)
        S.barrier()
        st.close()

    stop_after = cfg.stop_after
    phase0()
    phaseT()
    for l in range(L):
        if stop_after == "T":
            break
        layer_setup(l)
        S.bg_every = cfg.pace_a
        phaseA(l)
        if stop_after == f"A{l}":
            break
        S.bg_every = cfg.pace_c1
        phaseC1(l)
        if stop_after == f"C1{l}":
            break
        S.bg_every = cfg.pace_c2
        phaseC2(l)
        if stop_after == f"C2{l}":
            break

    if not out_store_ops:
        out_store_ops.extend(s.last for s in S.all_slots if s.last is not None)
    S.emit(out_store_ops)
    es.close()
    return nc


def host_layout(cfg, units, p_units, links, seg_pos, W):
    c = cfg
    L, FC, H, CFC = c.DEPTH, c.FC, c.H, c.CFC
    consts = make_consts()
    norms = np.concatenate(
        [np.stack([W["norm_mix"][l], W["norm_mlp"][l], W["norm_ple"][l]], 0) for l in range(L)] + [W["norm_final"][None]], 0
    )
    norms = np.ascontiguousarray(norms.reshape(3 * L + 1, FC, 128).transpose(2, 0, 1).reshape(128, (3 * L + 1) * FC))
    convw = np.ascontiguousarray(W["conv_w"].reshape(L, 3, CFC, 128).transpose(3, 0, 1, 2).reshape(128, L * 3 * CFC))
    dec = np.stack([W["ret_decay_fwd"], W["ret_decay_bwd"]], 1).reshape(1, L * 2 * H)
    dec = np.ascontiguousarray(np.broadcast_to(dec, (128, L * 2 * H)))
    in_maps = []
    for u in range(len(units)):
        cosT, sinT = rope_tables(seg_pos[u])
        in_maps.append({
            "x": units[u], "p": p_units[u],
            "w_in": W["w_in"], "w_out": W["w_out"], "w_ff1": W["w_ff1"], "w_ff2": W["w_ff2"],
            "w_gate": W["w_ple_gate"], "w_ple": W["w_ple_proj"],
            "norms": norms, "convw": convw, "dec": dec, "consts": consts,
            "link": np.full((128, 1), links[u], np.float32), "cosT": cosT, "sinT": sinT,
        })
    return in_maps


_PROG_CACHE = {}


def kernel(x_prompt, x_sample, p_prompt, p_sample, norm_mix, w_in, ret_decay_fwd, ret_decay_bwd,
           conv_w, w_out, norm_mlp, w_ff1, w_ff2, norm_ple, w_ple_gate, w_ple_proj, norm_final):
    f = lambda a: np.ascontiguousarray(np.asarray(a, dtype=np.float32))
    x_prompt, x_sample, p_prompt, p_sample = f(x_prompt), f(x_sample), f(p_prompt), f(p_sample)
    W = dict(norm_mix=f(norm_mix), w_in=f(w_in), ret_decay_fwd=f(ret_decay_fwd), ret_decay_bwd=f(ret_decay_bwd),
             conv_w=f(conv_w), w_out=f(w_out), norm_mlp=f(norm_mlp), w_ff1=f(w_ff1), w_ff2=f(w_ff2),
             norm_ple=f(norm_ple), w_ple_gate=f(w_ple_gate), w_ple_proj=f(w_ple_proj), norm_final=f(norm_final))
    B, SEQ, D = x_prompt.shape
    DB, DSEQ, _ = x_sample.shape
    L = W["w_in"].shape[0]
    NT = SEQ
    assert DSEQ * 2 == NT and DB % 2 == 0
    cfg = Cfg(D=D, NT=NT, DEPTH=L, PL=p_prompt.shape[-1], n_cores=8)
    units, p_units, links, seg_pos = [], [], [], []
    for b in range(B):
        units.append(x_prompt[b])
        p_units.append(np.ascontiguousarray(p_prompt[:, b]))
        links.append(1.0)
        seg_pos.append(np.arange(NT, dtype=np.float32))
    for b in range(0, DB, 2):
        units.append(np.ascontiguousarray(x_sample[b:b + 2].reshape(NT, D)))
        p_units.append(np.ascontiguousarray(p_sample[:, b:b + 2].reshape(L, NT, -1)))
        links.append(0.0)
        seg_pos.append(np.concatenate([np.arange(DSEQ), np.arange(DSEQ)]).astype(np.float32))
    n_units = len(units)
    assert n_units <= cfg.n_cores
    spare = [3, 7][: cfg.n_cores - n_units]
    slots = [s for s in range(cfg.n_cores) if s not in spare]
    make_consts()
    key = (D, NT, L)
    if key not in _PROG_CACHE:
        _PROG_CACHE[key] = build_program(cfg)
    nc = _PROG_CACHE[key]
    real_maps = host_layout(cfg, units, p_units, links, seg_pos, W)
    zero_map = {k: np.zeros_like(v) for k, v in real_maps[0].items()}
    in_maps = [None] * cfg.n_cores
    for u, s in enumerate(slots):
        in_maps[s] = real_maps[u]
    for s in spare:
        in_maps[s] = zero_map
    res = run_bass_kernel_spmd(nc, in_maps, core_ids=list(range(cfg.n_cores)))
    ys = [np.asarray(res.results[slots[u]]["y"], dtype=np.float32) for u in range(n_units)]
    y_prompt = np.stack(ys[:B], 0)
    y_sample = np.stack(ys[B:], 0).reshape(DB, DSEQ, D)
    return (y_prompt, y_sample)
```
